# Optimizing a Trainium2 kernel written in Bass

```python
import jax, jax.numpy as jnp
from jax import lax
import numpy as np

D_MODEL = 1024
BATCH = 16
SEQ = 2048
DEPTH = 2
DEC_BATCH = 4
DEC_SEQ = 4096
PAST_LEN = 128

GRID_W = 64
N_MEM = 256
HEAD_DIM = 64
N_HEADS = 8
N_KV_HEADS = 2
ATTN_W = N_HEADS * HEAD_DIM
KV_W = N_KV_HEADS * HEAD_DIM
POOL_W = 256
POOL_WINDOWS = (2, 4, 8, 16)
POOL_GROUP = POOL_W // len(POOL_WINDOWS)
N_XHEADS = 4
XATTN_W = N_XHEADS * HEAD_DIM
MIX_W = POOL_W + ATTN_W + XATTN_W
IN_W = 2 * POOL_W + 2 * ATTN_W + 2 * KV_W + 2 * XATTN_W
Q_BLOCK = 128
ROPE_THETA = 10000.0
ROPE_PAIRS = HEAD_DIM // 4
EPS = 1e-6

kernel_name = "hybrid_pool_gqa_memory_encoder"


def rms_norm(x, g):
    xf = x.astype(jnp.float32)
    y = xf * lax.rsqrt(jnp.mean(xf * xf, axis=-1, keepdims=True) + EPS)
    return (y * g.astype(jnp.float32)).astype(x.dtype)


def axial_rope(T):
    rows = T // GRID_W
    row = jnp.repeat(jnp.arange(rows), GRID_W).astype(jnp.float32)
    col = jnp.tile(jnp.arange(GRID_W), rows).astype(jnp.float32)
    freqs = ROPE_THETA ** (-jnp.arange(ROPE_PAIRS, dtype=jnp.float32) / ROPE_PAIRS)
    ang = jnp.stack([row[:, None] * freqs, col[:, None] * freqs], axis=1)
    return jnp.cos(ang), jnp.sin(ang)


def apply_rope(x, cos, sin):
    B, T, H, D = x.shape
    xr = x.astype(jnp.float32).reshape(B, T, H, 2, 2, ROPE_PAIRS)
    a, b = xr[..., 0, :], xr[..., 1, :]
    c, s = cos[None, :, None], sin[None, :, None]
    out = jnp.stack([a * c - b * s, b * c + a * s], axis=-2)
    return out.reshape(B, T, H, D).astype(x.dtype)


def multiscale_pool(u, pool_w, pool_scale):
    B, T, _ = u.shape
    uf = u.astype(jnp.float32)
    cs = jnp.concatenate([jnp.zeros((B, 1, POOL_W), jnp.float32), jnp.cumsum(uf, axis=1)], axis=1)
    t = jnp.arange(T)
    outs = []
    for g, w in enumerate(POOL_WINDOWS):
        lo = jnp.clip(t - w // 2, 0, T)
        hi = jnp.clip(t + w - w // 2, 0, T)
        sl = slice(g * POOL_GROUP, (g + 1) * POOL_GROUP)
        csg = cs[:, :, sl]
        cnt = (hi - lo).astype(jnp.float32)[None, :, None]
        mean = (jnp.take(csg, hi, axis=1) - jnp.take(csg, lo, axis=1)) / cnt
        d = (mean - uf[:, :, sl]).astype(u.dtype)
        outs.append(jnp.einsum('btc,cd->btd', d, pool_w[g]))
    return jnp.concatenate(outs, axis=-1) * pool_scale


def block_self_attention(q, k, v):
    B, T, H, D = q.shape
    G = H // N_KV_HEADS
    nb = T // Q_BLOCK
    qb = q.reshape(B, nb, Q_BLOCK, N_KV_HEADS, G, D).transpose(1, 0, 2, 3, 4, 5)
    scale = D ** -0.5

    def one_block(qblk):
        s = jnp.einsum('bqkgd,bskd->bkgqs', qblk, k, preferred_element_type=jnp.float32) * scale
        p = jax.nn.softmax(s, axis=-1)
        return jnp.einsum('bkgqs,bskd->bqkgd', p.astype(v.dtype), v)

    o = lax.map(one_block, qb)
    return o.transpose(1, 0, 2, 3, 4, 5).reshape(B, T, H * D)


def memory_attention(q, k, v):
    B, T, XH, D = q.shape
    s = jnp.einsum('bthd,bmhd->bhtm', q, k, preferred_element_type=jnp.float32) * (D ** -0.5)
    p = jax.nn.softmax(s, axis=-1)
    return jnp.einsum('bhtm,bmhd->bthd', p.astype(v.dtype), v).reshape(B, T, XH * D)


SPLITS = [int(i) for i in np.cumsum([POOL_W, POOL_W, ATTN_W, KV_W, KV_W, ATTN_W, XATTN_W])]


def hybrid_layer(x, mem, cos, sin, norm_pre, norm_post, w_in, pool_w, pool_scale,
                 q_norm, k_norm, mem_norm, w_mem_kv, w_out):
    B, T, _ = x.shape
    h = rms_norm(x, norm_pre)
    z = jnp.einsum('btd,de->bte', h, w_in)
    u_pool, g_pool, q, k, v, g_attn, q_x, g_x = jnp.split(z, SPLITS, axis=-1)

    pool_out = multiscale_pool(u_pool, pool_w, pool_scale) * jax.nn.silu(g_pool)

    q = apply_rope(rms_norm(q.reshape(B, T, N_HEADS, HEAD_DIM), q_norm), cos, sin)
    k = apply_rope(rms_norm(k.reshape(B, T, N_KV_HEADS, HEAD_DIM), k_norm), cos, sin)
    v = v.reshape(B, T, N_KV_HEADS, HEAD_DIM)
    attn_out = block_self_attention(q, k, v) * jax.nn.silu(g_attn)

    mh = rms_norm(mem, mem_norm)
    kv_m = jnp.einsum('bmd,de->bme', mh, w_mem_kv)
    k_m, v_m = jnp.split(kv_m, [XATTN_W], axis=-1)
    M = mem.shape[1]
    x_out = memory_attention(q_x.reshape(B, T, N_XHEADS, HEAD_DIM),
                             k_m.reshape(B, M, N_XHEADS, HEAD_DIM),
                             v_m.reshape(B, M, N_XHEADS, HEAD_DIM)) * jax.nn.silu(g_x)

    mix = jnp.concatenate([pool_out, attn_out, x_out], axis=-1)
    y = jnp.einsum('bte,ed->btd', mix, w_out)
    return x + rms_norm(y, norm_post)


def run_trunk(x, mem, norm_pre, norm_post, w_in, pool_w, pool_scale,
              q_norm, k_norm, mem_norm, w_mem_kv, w_out):
    cos, sin = axial_rope(x.shape[1])
    for l in range(DEPTH):
        x = hybrid_layer(x, mem, cos, sin, norm_pre[l], norm_post[l], w_in[l], pool_w[l], pool_scale[l],
                         q_norm[l], k_norm[l], mem_norm[l], w_mem_kv[l], w_out[l])
    return x


def setup_inputs(seed: int = 0) -> dict:
    key = jax.random.key(seed)
    ks = jax.random.split(key, 16)
    f32 = jnp.float32

    def gain(k, shape):
        return 1.0 + 0.02 * jax.random.normal(k, shape, f32)

    return {
        "x_prompt": jax.random.normal(ks[0], (BATCH, SEQ, D_MODEL), f32),
        "x_sample": jax.random.normal(ks[1], (DEC_BATCH, DEC_SEQ, D_MODEL), f32),
        "mem_prompt": jax.random.normal(ks[2], (BATCH, N_MEM, D_MODEL), f32),
        "mem_sample": jax.random.normal(ks[3], (DEC_BATCH, N_MEM, D_MODEL), f32),
        "norm_pre": gain(ks[4], (DEPTH, D_MODEL)),
        "norm_post": gain(ks[5], (DEPTH, D_MODEL)),
        "w_in": jax.random.normal(ks[6], (DEPTH, D_MODEL, IN_W), f32) * D_MODEL ** -0.5,
        "pool_w": jax.random.normal(ks[7], (DEPTH, len(POOL_WINDOWS), POOL_GROUP, POOL_GROUP), f32) * POOL_GROUP ** -0.5,
        "pool_scale": 1.0 + 0.1 * jax.random.normal(ks[8], (DEPTH, POOL_W), f32),
        "q_norm": gain(ks[9], (DEPTH, HEAD_DIM)),
        "k_norm": gain(ks[10], (DEPTH, HEAD_DIM)),
        "mem_norm": gain(ks[11], (DEPTH, D_MODEL)),
        "w_mem_kv": jax.random.normal(ks[12], (DEPTH, D_MODEL, 2 * XATTN_W), f32) * D_MODEL ** -0.5,
        "w_out": jax.random.normal(ks[13], (DEPTH, MIX_W, D_MODEL), f32) * MIX_W ** -0.5,
    }


def reference(x_prompt, x_sample, mem_prompt, mem_sample, norm_pre, norm_post, w_in, pool_w, pool_scale,
              q_norm, k_norm, mem_norm, w_mem_kv, w_out):
    y_prompt = run_trunk(x_prompt, mem_prompt, norm_pre, norm_post, w_in, pool_w, pool_scale,
                         q_norm, k_norm, mem_norm, w_mem_kv, w_out)
    y_sample = run_trunk(x_sample, mem_sample, norm_pre, norm_post, w_in, pool_w, pool_scale,
                         q_norm, k_norm, mem_norm, w_mem_kv, w_out)
    return (y_prompt, y_sample)
```

```python
import numpy as np
import ml_dtypes
import concourse.bass as bass
import concourse.mybir as mybir
from concourse.bass_utils import run_bass_kernel_spmd

F32 = mybir.dt.float32
BF16 = mybir.dt.bfloat16
AF = mybir.ActivationFunctionType
ALU = mybir.AluOpType

NCORES = 8
D = 1024
T = 2048
NCH = 3
NMEM = 256
INW = 2304
EPS = 1e-6
HALO = 16
UBW = T + 2 * HALO
VB = 129
XW = 2 * T // 2 + 0

C_UPOOL, C_GPOOL, C_Q, C_K, C_V, C_GATTN, C_QX, C_GX = 0, 256, 512, 1024, 1152, 1280, 1792, 2048

XK0 = 0
XV0 = T
XVW = 16 * 2 * VB
XH0 = XV0 + XVW
XW = XH0 + 2 * 2 * HALO


class Buf:
    __slots__ = ("name", "w", "r", "sem", "semcnt", "excl")

    def __init__(self, name, excl=False):
        self.name = name
        self.excl = excl
        self.w = {}
        self.r = {}
        self.sem = None
        self.semcnt = 0


class Tracker:
    def __init__(self, nc):
        self.nc = nc
        self.eng = {}
        for n, e in (("pe", nc.tensor), ("act", nc.scalar), ("dve", nc.vector),
                     ("pool", nc.gpsimd), ("sp", nc.sync)):
            sem = nc.alloc_semaphore(name=f"sem_{n}")
            self.eng[n] = {"e": e, "sem": sem, "key": f"sem_{n}", "cnt": 0, "seen": {}}
        self.nwaits = 0
        self.nops = 0

    def _collect(self, en, reads, writes, is_dma, pwrites=(), pkey=None):
        E = self.eng[en]
        need = {}

        def add(key, ev, raw):
            h, val, owner = ev
            if owner is not None:
                val = owner.semcnt
            if not is_dma and key == E["key"] and en == "pe":
                return
            cur = need.get(key)
            if cur is None or cur[1] < val:
                need[key] = (h, val)

        for b in reads:
            for key, ev in b.w.items():
                add(key, ev, True)
        for b in writes:
            for key, ev in b.w.items():
                add(key, ev, False)
            for key, ev in b.r.items():
                add(key, ev, False)
        for b in pwrites:
            for key, ev in b.r.items():
                add(key, ev, False)
            for key, ev in b.w.items():
                if key != pkey:
                    add(key, ev, False)
        for key, (h, val) in need.items():
            if E["seen"].get(key, 0) < val:
                E["e"].wait_ge(h, val)
                E["seen"][key] = val
                self.nwaits += 1

    def _record(self, key, ev, reads, writes, pwrites=()):
        for b in reads:
            b.r[key] = ev
        for b in writes:
            b.w = {key: ev}
            b.r = {}
        for b in pwrites:
            b.w[key] = ev
            b.r = {}

    def op(self, en, fn, r=(), w=()):
        E = self.eng[en]
        if any(b.excl for b in r):
            w = list(w) + [b for b in r if b.excl]
            r = [b for b in r if not b.excl]
        self._collect(en, r, w, False)
        inst = fn(E["e"])
        E["cnt"] += 1
        inst.then_inc(E["sem"], 1)
        self._record(E["key"], (E["sem"], E["cnt"], None), r, w)
        self.nops += 1

    def dma(self, q, owner, out, in_, r=(), w=(), pw=(), **kw):
        E = self.eng[q]
        if owner.sem is None:
            owner.sem = self.nc.alloc_semaphore(name=f"dsem_{owner.name}")
        self._collect(q, r, w, True, pw, f"dsem_{owner.name}")
        inst = E["e"].dma_start(out=out, in_=in_, **kw)
        owner.semcnt += 16
        inst.then_inc(owner.sem, 16)
        self._record(f"dsem_{owner.name}", (owner.sem, owner.semcnt, owner), r, w, pw)
        self.nops += 1

    def wait_all(self, en, bufs):
        self._collect(en, [], bufs, True)


def build_program(nlayers=2, ncores=NCORES):
    nc = bass.Bass("TRN2", target_bir_lowering=False)
    tr = Tracker(nc)

    def dram_in(name, shape, dt=F32):
        return nc.dram_tensor(name, list(shape), dt, kind="ExternalInput").ap()

    xs = dram_in("xs", [NCH * T, D])
    mems = dram_in("mems", [NCH * NMEM, D])
    w_in = dram_in("w_in", [2, D, INW])
    w_out = dram_in("w_out", [2, D, D])
    w_mem = dram_in("w_mem", [2, D, 512])
    pool_w = dram_in("pool_w", [2, 256, 64])
    norm_pre = dram_in("norm_pre", [2, D])
    norm_post = dram_in("norm_post", [2, D])
    mem_norm = dram_in("mem_norm", [2, D])
    pool_scale = dram_in("pool_scale", [2, 256])
    q_norm = dram_in("q_norm", [2, 64])
    k_norm = dram_in("k_norm", [2, 64])
    rope = dram_in("rope", [2, 2, 128, T])
    ptab = dram_in("ptab", [2, 128, 2 * 2 * HALO])
    hmask = dram_in("hmask", [128, 2])
    cmat = dram_in("cmat", [3, 128, 128])
    selm = dram_in("selm", [12, 6 * 128])
    yout = nc.dram_tensor("y", [NCH * T, D], F32, kind="ExternalOutput").ap()
    x1d = nc.dram_tensor("x1_scratch", [NCH * T, D], F32)
    xin_d = [nc.dram_tensor(f"xin{l}", [128, XW], BF16) for l in range(2)]
    xout_d = [nc.dram_tensor(f"xout{l}", [256, XW], BF16) for l in range(2)]
    b_x1 = [Buf(f"x1_{c}") for c in range(NCH)]
    b_xin = [Buf(f"xin{l}") for l in range(2)]
    b_xout = [Buf(f"xout{l}") for l in range(2)]

    def sb(name, shape, dt):
        return nc.alloc_sbuf_tensor(name, list(shape), dt)

    w_in_sb = sb("w_in_sb", [128, 8, INW], BF16)
    w_out_sb = sb("w_out_sb", [128, 8, D], BF16)
    w_krep = sb("w_krep", [128, 8, 2, 128], BF16)
    poolw_sb = sb("poolw_sb", [128, 2, 128], BF16)
    hT = sb("hT", [128, 8, T], BF16)
    def view(region, byte_off, shape, dt):
        esz = 4 if dt == F32 else 2
        n = 1
        for d_ in shape[1:]:
            n *= d_
        a = region[:, byte_off // 2: byte_off // 2 + n * esz // 2]
        if dt == F32:
            a = a.bitcast(F32)
        if len(shape) == 3:
            a = a.rearrange("p (a b) -> p a b", a=shape[1])
        return a

    regX = sb("regX", [128, 18432 // 2], BF16)
    regY = sb("regY", [128, 16512 // 2], BF16)
    regZ = sb("regZ", [128, 4096 // 2], BF16)
    ub = view(regX, 0, [128, 2, UBW], F32)
    sg_blk = view(regX, 0, [128, 8, 512], BF16)
    mixT_blk = view(regX, 8192, [128, 8, 512], BF16)
    qxT_blk = view(regX, 16384, [128, 2, 512], BF16)
    w_mem_sb = view(regY, 0, [128, 8, 512], BF16)
    cs = view(regY, 0, [128, UBW], F32)
    win = view(regY, 8320, [128, T], F32)
    qTm = view(regY, 0, [128, 8, 512], BF16)
    kT2 = sb("kT2", [128, 2, 2 * T], BF16)
    vaug_f = sb("vaug", [128, 64 * VB + 64], BF16)
    vaug = vaug_f[:, 0:64 * VB].rearrange("p (b c) -> p b c", c=VB)
    xst = [sb(f"xst{i}", [128, D], F32) for i in range(2)]
    yst = [view(regZ, 0, [128, D], F32)] * 2
    NPT = 3
    PTP = [view(regY, 8192, [128, 1024], BF16), view(regY, 10240, [128, 1024], BF16),
           view(regY, 14336, [128, 1024], BF16)]
    ropeC = sb("ropeC", [128, T], BF16)
    ropeS = sb("ropeS", [128, T], BF16)
    dT = sb("dT", [128, 2, T], BF16)
    kmT = sb("kmT", [128, NCH, 2, NMEM], BF16)
    vmaug_f = sb("vmaug", [128, NCH * 8 * VB + 64], BF16)
    vmaug = vmaug_f[:, 0:NCH * 8 * VB].rearrange("p (b c) -> p b c", c=VB)
    hb = [view(regZ, i * 2048, [128, D], BF16) for i in range(2)]
    ident_bf = sb("ident_bf", [128, 128], BF16)
    bones_bf = sb("bones_bf", [128, 128], BF16)
    swapP = sb("swapP", [128, 128], F32)
    Pq = sb("Pq", [128, 128], BF16)
    Pk = sb("Pk", [128, 128], BF16)
    ones_bf = sb("ones_bf", [128, 64], BF16)
    gq = sb("gq", [128, 1], F32)
    gk = sb("gk", [128, 1], F32)
    gpre = sb("gpre", [128, 8], F32)
    gmem = sb("gmem", [128, 8], F32)
    gpost = sb("gpost", [128, D], F32)
    pscale = sb("pscale", [128, 2], F32)
    invw = sb("invw", [128, 2], F32)
    hmask_sb = sb("hmask_sb", [128, 2], F32)
    ptab_sb = sb("ptab_sb", [128, 2, 2 * 2 * HALO], F32)
    halo_st = sb("halo_st", [128, 2, 2 * HALO], BF16)
    halo_in = sb("halo_in", [128, 2, 2 * 2 * HALO], BF16)
    etmp = sb("etmp", [128, 2, 2 * HALO], F32)
    ss = sb("ss", [128, 64], F32)
    lnv = sb("lnv", [128, 64], F32)
    rstd = sb("rstd", [128, 64], F32)
    NQS = 1
    sqb = [sb(f"sqb{i}", [128, 512], BF16) for i in range(NQS)]
    zsb = [sb(f"zsb{i}", [128, 512], BF16) for i in range(NQS)]
    t1b = [sb(f"t1b{i}", [128, 512], F32) for i in range(NQS)]
    sqb.append(view(regY, 12288, [128, 512], BF16))
    zsb.append(view(regY, 13312, [128, 512], BF16))
    t1b.append(view(regY, 14336, [128, 512], F32))
    srow_all = view(regY, 12288, [128, 512], F32)
    rhl = view(regY, 14336, [128, 2, 512], BF16)
    sel_bf = sb("sel_bf", [12, 6, 128], BF16)
    Rb0 = sb("Rb0", [128, 512], F32)
    junk = Rb0[:, :].bitcast(BF16)
    Rb = [Rb0, t1b[0]]

    ps_all = nc.alloc_psum_tensor("ps_all", [128, 8 * 512], F32)

    def bank(b, n=1):
        return ps_all[:, b * 512:(b + n) * 512]

    B = {}

    def bf(name):
        if name not in B:
            B[name] = Buf(name)
        return B[name]

    b_bank = [bf(f"bank{i}") for i in range(8)]
    for b_ in b_bank:
        b_.excl = True
    b_xst = [bf(f"xst{i}") for i in range(2)]
    b_yst = [bf("yst0")] * 2
    b_hb = [bf(f"hb{i}") for i in range(2)]
    b_PT = [bf("PT0"), bf("PT1"), bf("rhl")]
    b_hT = [bf(f"hT{i}") for i in range(4)]
    b_kT = [bf(f"kT{i}") for i in range(8)]
    b_v = [bf(f"v{i}") for i in range(8)]
    b_ub = [bf(f"ub{i}") for i in range(4)]
    b_ubh = bf("ubh")
    b_dT = bf("dT")
    b_cs = bf("cs")
    b_qT = [bf(f"qT{i}") for i in range(4)]
    b_qz = bf("qzero")
    b_qx = [bf(f"qx{i}") for i in range(2)]
    b_sg = [bf(f"sg{i}") for i in range(8)]
    b_mix = [bf(f"mix{i}") for i in range(8)]
    b_w = {n: bf(n) for n in ("w_in", "w_out", "w_mem", "w_krep", "poolw", "consts", "gains",
                              "rope", "ptab", "kmv", "stat", "halo_st", "halo_in", "etmp", "junk")}
    b_stat = [bf(f"stat{i}") for i in range(64)]
    b_sq = [bf(f"sq{i}") for i in range(NQS)]
    b_zs = [bf(f"zs{i}") for i in range(NQS)]
    b_t1 = [bf(f"t1{i}") for i in range(NQS)]
    b_sq.append(bf("sq_1"))
    b_zs.append(bf("zs_1"))
    b_t1.append(bf("t1_1"))
    b_gt = []
    b_srow = [bf("srow_all"), bf("rhl")]
    b_R = [bf("R0"), b_t1[0]]
    b_tb = []

    OP = tr.op
    DMA = tr.dma
    XS1 = [xst[0], xst[1]] + [view(regY, i * 4096, [128, D], F32) for i in range(3)]
    b_XS1 = [b_xst[0], b_xst[1]] + [bf(f"xsY{i}") for i in range(3)] + [b_sq[1], b_zs[1], b_t1[1]]
    XSW = [xst[0], xst[1]] + [view(regX, i * 4096, [128, D], F32) for i in range(4)]
    b_XSW = [b_xst[0], b_xst[1]] + [bf(f"xsX{i}") for i in range(4)]

    def fence(A, Bs):
        for a in A:
            for src in (a.w, a.r):
                for key, ev in src.items():
                    for b in Bs:
                        cur = b.r.get(key)
                        if cur is None or cur[1] < ev[1]:
                            b.r[key] = ev

    def p2_tmp():
        return b_qT + b_PT + b_gt + b_srow + b_tb
    rr = {"proj": 0, "nrm": 0, "qs": 0, "gt": 0, "pt": 0, "S": 0, "O": 0, "xst": 0, "yst": 0, "hb": 0,
          "sr": 0, "xs1": 0, "xsw": 0}

    def nxt(k, n):
        if k == "proj":
            n = rot["n"]
        v = rr[k] % n
        rr[k] = (v + 1) % n
        return v
    rot = {"n": 4}

    cst = xst[0][:, 0:384].rearrange("p (k m) -> p k m", k=3)
    with nc.allow_non_contiguous_dma(reason="small constant / gain loads"):
        DMA("sp", b_xst[0], cst, cmat.rearrange("k p m -> p k m"), w=[b_xst[0]])
        DMA("sp", b_w["ptab"], hmask_sb[:, :], hmask, pw=[b_w["gains"]])
    OP("dve", lambda e: e.tensor_copy(out=ident_bf[:, :], in_=cst[:, 0, :]), r=[b_xst[0]], w=[bf("ident")])
    OP("dve", lambda e: e.tensor_copy(out=swapP[:, :], in_=cst[:, 1, :]), r=[b_xst[0]], w=[bf("swapP")])
    OP("dve", lambda e: e.tensor_copy(out=bones_bf[:, :], in_=cst[:, 2, :]), r=[b_xst[0]], w=[bf("bones")])
    OP("dve", lambda e: e.memset(ones_bf[:, :], 1.0), w=[bf("ones")])
    DMA("pool", bf("sel"), sel_bf[:, :, :].rearrange("p a b -> p (a b)"), selm, w=[bf("sel")])
    OP("dve", lambda e: e.memset(invw[0:64, 0:1], 0.5), w=[bf("invw")])
    OP("dve", lambda e: e.memset(invw[64:128, 0:1], 0.25), w=[bf("invw")])
    OP("dve", lambda e: e.memset(invw[0:64, 1:2], 0.125), w=[bf("invw")])
    OP("dve", lambda e: e.memset(invw[64:128, 1:2], 0.0625), w=[bf("invw")])
    OP("dve", lambda e: e.memset(vaug_f[:, :], 0.0), w=b_v)
    OP("dve", lambda e: e.memset(vaug[:, :, 0:1], 1.0), w=b_v)
    OP("dve", lambda e: e.memset(vaug[:, :, 128:129], 1.0), w=b_v)
    OP("dve", lambda e: e.memset(vmaug_f[:, :], 0.0), w=[b_w["kmv"]])
    OP("dve", lambda e: e.memset(vmaug[:, :, 0:1], 1.0), w=[b_w["kmv"]])
    OP("dve", lambda e: e.memset(vmaug[:, :, 128:129], 1.0), w=[b_w["kmv"]])

    def rms_stats(src_ap, idx, rd, nfree, extra_w=()):
        OP("act", lambda e: e.activation(out=junk[:, 0:nfree], in_=src_ap, func=AF.Square,
                                         accum_out=ss[:, idx:idx + 1]),
           r=rd, w=[b_R[0], b_stat[idx]])
        OP("act", lambda e: e.activation(out=lnv[:, idx:idx + 1], in_=ss[:, idx:idx + 1], func=AF.Ln,
                                         scale=1.0 / nfree, bias=eps_t[:, 0:1]),
           r=[b_stat[idx], bf("eps")], w=[b_stat[idx]])
        OP("act", lambda e: e.activation(out=rstd[:, idx:idx + 1], in_=lnv[:, idx:idx + 1], func=AF.Exp,
                                         scale=-0.5),
           r=[b_stat[idx]], w=[b_stat[idx]])

    eps_t = sb("eps_t", [128, 1], F32)
    OP("dve", lambda e: e.memset(eps_t[:, :], EPS), w=[bf("eps")])

    def transposes_to(dst_ap_fn, src_tile, src_buf, dst_bufs, nchunk=8):
        pb = nxt("proj", 4)
        psb = bank(pb).bitcast(BF16)

        def f(e):
            i = None
            for c in range(nchunk):
                i = e.transpose(out=psb[:, c * 128:(c + 1) * 128], in_=src_tile[:, c * 128:(c + 1) * 128],
                                identity=ident_bf[:, :])
            return i
        OP("pe", f, r=[src_buf, bf("ident")], w=[b_bank[pb]])
        OP("dve", lambda e: e.tensor_copy(out=dst_ap_fn(),
                                          in_=psb[:, 0:nchunk * 128].rearrange("p (c t) -> p c t", c=nchunk)),
           r=[b_bank[pb]], w=dst_bufs)

    def proj_group(lhs_fn, rhs_fn, n, rd, nk=8, pb=None):
        if pb is None:
            pb = nxt("proj", 4)

        def f(e):
            i = None
            for c in range(nk):
                i = e.matmul(bank(pb)[:, 0:n], lhsT=lhs_fn(c), rhs=rhs_fn(c), start=(c == 0), stop=(c == nk - 1))
            return i
        OP("pe", f, r=rd, w=[b_bank[pb]])
        return pb

    def qk_norm_rope(pb, Pmat, Pbuf, gvec, tok0, dst_ap, dst_bufs, nbs=None, staged=False, qs=0):
        s = qs
        if nbs is None:
            nb = 4 + 2 * nxt("nrm", 2)
            nbz = nb + 1
        else:
            nb, nbz = nbs
        z = bank(pb)

        def st_act1():
            OP("act", lambda e: e.activation(out=sqb[s][:, :], in_=z, func=AF.Square), r=[b_bank[pb]], w=[b_sq[s]])
            OP("act", lambda e: e.activation(out=zsb[s][:, :], in_=z, func=AF.Copy), r=[b_bank[pb]], w=[b_zs[s]])

        def st_dve0():
            OP("dve", lambda e: e.scalar_tensor_tensor(out=t1b[s][:, :], in0=z, scalar=gvec[:, 0:1],
                                                       in1=ropeC[:, tok0:tok0 + 512], op0=ALU.mult, op1=ALU.mult),
               r=[b_bank[pb], b_w["rope"], b_w["gains"]], w=[b_t1[s]])

        def st_pe():
            OP("pe", lambda e: e.matmul(bank(nb), lhsT=bones_bf[:, :], rhs=sqb[s][:, :], start=True, stop=True),
               r=[b_sq[s], bf("bones")], w=[b_bank[nb]])
            OP("pe", lambda e: e.matmul(bank(nbz), lhsT=Pmat[:, :], rhs=zsb[s][:, :], start=True, stop=True),
               r=[b_zs[s], Pbuf], w=[b_bank[nbz]])

        def st_act2():
            OP("act", lambda e: e.activation(out=bank(nb), in_=bank(nb), func=AF.Ln, bias=eps_t[:, 0:1]),
               r=[b_bank[nb], bf("eps")], w=[b_bank[nb]])
            OP("act", lambda e: e.activation(out=bank(nb), in_=bank(nb), func=AF.Exp, scale=-0.5),
               r=[b_bank[nb]], w=[b_bank[nb]])

        def st_dve1():
            OP("dve", lambda e: e.tensor_tensor(out=bank(nbz), in0=bank(nbz), in1=ropeS[:, tok0:tok0 + 512],
                                                op=ALU.mult),
               r=[b_bank[nbz], b_w["rope"]], w=[b_bank[nbz]])
            OP("dve", lambda e: e.tensor_tensor(out=t1b[s][:, :], in0=t1b[s][:, :], in1=bank(nbz), op=ALU.add),
               r=[b_t1[s], b_bank[nbz]], w=[b_t1[s]])

        def st_dve2():
            if isinstance(dst_ap, tuple):
                for hf, dap in enumerate(dst_ap):
                    ln_ = slice(hf * 64, (hf + 1) * 64)
                    OP("dve", lambda e, ln_=ln_, dap=dap: e.tensor_tensor(out=dap, in0=t1b[s][ln_, :],
                                                                         in1=bank(nb)[ln_, :], op=ALU.mult),
                       r=[b_t1[s], b_bank[nb]], w=dst_bufs)
            else:
                OP("dve", lambda e: e.tensor_tensor(out=dst_ap, in0=t1b[s][:, :], in1=bank(nb), op=ALU.mult),
                   r=[b_t1[s], b_bank[nb]], w=dst_bufs)
        stages = [st_act1, st_dve0, st_pe, st_act2, st_dve1, st_dve2]
        if staged:
            return stages
        for f in stages:
            f()

    def load_layer(l):
        fence([b_cs, bf("win")] + p2_tmp() + b_XS1[2:], [b_w["w_mem"]])
        fence([b_yst[0]], b_hb)
        with nc.allow_non_contiguous_dma(reason="gain vectors"):
            DMA("sp", b_w["gains"], gpre[:, :], norm_pre[l].rearrange("(c p) -> p c", p=128), pw=[b_w["gains"]])
            DMA("sp", b_w["gains"], gmem[:, :], mem_norm[l].rearrange("(c p) -> p c", p=128), pw=[b_w["gains"]])
            DMA("sp", b_w["gains"], pscale[:, :], pool_scale[l].rearrange("(c p) -> p c", p=128),
                pw=[b_w["gains"]])
            for hh in range(2):
                DMA("sp", b_w["gains"], gq[hh * 64:(hh + 1) * 64, :], q_norm[l].rearrange("(p o) -> p o", o=1),
                    pw=[b_w["gains"]])
                DMA("sp", b_w["gains"], gk[hh * 64:(hh + 1) * 64, :], k_norm[l].rearrange("(p o) -> p o", o=1),
                    pw=[b_w["gains"]])
            DMA("sp", b_w["gains"], gpost[:, :], norm_post[l:l + 1, :].to_broadcast([128, D]), pw=[b_w["gains"]])
        OP("dve", lambda e: e.tensor_scalar(out=Pq[:, :], in0=swapP[:, :], scalar1=gq[:, 0:1], scalar2=None,
                                            op0=ALU.mult), r=[bf("swapP"), b_w["gains"]], w=[bf("Pq")])
        OP("dve", lambda e: e.tensor_scalar(out=Pk[:, :], in0=swapP[:, :], scalar1=gk[:, 0:1], scalar2=None,
                                            op0=ALU.mult), r=[bf("swapP"), b_w["gains"]], w=[bf("Pk")])
        fence(b_ub + [b_ubh] + b_sg + b_mix + b_qx, b_XSW[2:])
        jobs = []
        for c in range(8):
            for (o, n) in [(0, 1024), (1024, 1024), (2048, 256)]:
                jobs.append(("in", c, o, n))
        for c in range(8):
            jobs.append(("mem", c, 0, 512))
        slot_of = {}

        def issue(k):
            kind, c, o, n = jobs[k]
            s_ = nxt("xsw", 6)
            slot_of[k] = s_
            src = w_in[l, c * 128:(c + 1) * 128, o:o + n] if kind == "in" else w_mem[l, c * 128:(c + 1) * 128, :]
            DMA("sp", b_XSW[s_], XSW[s_][:, 0:n], src, w=[b_XSW[s_]])
        for k in range(min(5, len(jobs))):
            issue(k)
        for k, (kind, c, o, n) in enumerate(jobs):
            if k + 5 < len(jobs):
                issue(k + 5)
            s_ = slot_of[k]
            if kind == "in":
                OP("dve", lambda e, s_=s_, c=c, o=o, n=n: e.tensor_scalar(
                    out=w_in_sb[:, c, o:o + n], in0=XSW[s_][:, 0:n], scalar1=gpre[:, c:c + 1], scalar2=None,
                    op0=ALU.mult), r=[b_XSW[s_], b_w["gains"]], w=[b_w["w_in"]])
                if o == 1024:
                    for kvh in range(2):
                        for rep in range(2):
                            OP("dve", lambda e, c=c, kvh=kvh, rep=rep: e.tensor_copy(
                                out=w_krep[:, c, kvh, rep * 64:(rep + 1) * 64],
                                in_=w_in_sb[:, c, C_K + kvh * 64:C_K + (kvh + 1) * 64]),
                               r=[b_w["w_in"]], w=[b_w["w_krep"]])
            else:
                OP("dve", lambda e, s_=s_, c=c: e.tensor_scalar(
                    out=w_mem_sb[:, c, :], in0=XSW[s_][:, 0:512], scalar1=gmem[:, c:c + 1], scalar2=None,
                    op0=ALU.mult), r=[b_XSW[s_], b_w["gains"]], w=[b_w["w_mem"]])
        for c in range(8):
            DMA("pool", b_w["w_out"], w_out_sb[:, c, :], w_out[l, c * 128:(c + 1) * 128, :], pw=[b_w["w_out"]])
        OP("dve", lambda e: e.memset(poolw_sb[:, :, :], 0.0), w=[b_w["poolw"]])
        for g in range(4):
            ti, hf = g // 2, g % 2
            DMA("pool", b_w["poolw"], poolw_sb[hf * 64:(hf + 1) * 64, ti, hf * 64:(hf + 1) * 64],
                pool_w[l, g * 64:(g + 1) * 64, :], pw=[b_w["poolw"]])

    def mem_kv(c):
        for mt in range(2):
            s = nxt("xst", 2)
            DMA("sp", b_xst[s], xst[s][:, :], mems[c * NMEM + mt * 128: c * NMEM + (mt + 1) * 128, :], w=[b_xst[s]])
            rms_stats(xst[s][:, :], 60 + mt, [b_xst[s]], D)
            h = nxt("hb", 2)
            OP("dve", lambda e, s=s, h=h, mt=mt: e.tensor_scalar(
                out=hb[h][:, :], in0=xst[s][:, :], scalar1=rstd[:, 60 + mt:61 + mt], scalar2=None, op0=ALU.mult),
               r=[b_xst[s], b_stat[60 + mt]], w=[b_hb[h]])
            transposes_to(lambda mt=mt: hT[:, :, mt * 128:(mt + 1) * 128], hb[h], b_hb[h], [b_hT[0]])
        for g in range(2):
            pb = proj_group(lambda cc, g=g: w_mem_sb[:, cc, g * 128:(g + 1) * 128],
                            lambda cc: hT[:, cc, 0:NMEM], NMEM, [b_w["w_mem"], b_hT[0]])
            OP("dve", lambda e, pb=pb, g=g: e.tensor_copy(out=kmT[:, c, g, :], in_=bank(pb)[:, 0:NMEM]),
               r=[b_bank[pb]], w=[b_w["kmv"]])
        for mt in range(2):
            pb = proj_group(lambda cc, mt=mt: hT[:, cc, mt * 128:(mt + 1) * 128],
                            lambda cc: w_mem_sb[:, cc, 256:512], 256, [b_w["w_mem"], b_hT[0]])
            dst = vmaug[:, c * 8 + mt * 4: c * 8 + (mt + 1) * 4, 64:128]
            OP("dve", lambda e, pb=pb, dst=dst: e.tensor_copy(
                out=dst, in_=bank(pb)[:, 0:256].rearrange("p (h d) -> p h d", h=4)),
               r=[b_bank[pb]], w=[b_w["kmv"]])

    def x_src(l, c, tile):
        base = c * T + tile * 128
        if l == 0:
            return xs[base:base + 128, :], []
        return x1d.ap()[base:base + 128, :], [b_x1[c]]

    def pass1(l, c, do_kv=True):
        fence([b_yst[0]], b_hb)
        fence(b_sg + b_mix + b_qx + b_XSW[2:], b_ub + [b_ubh])
        fence([b_cs, bf("win"), b_w["w_mem"]] + p2_tmp(), b_XS1[2:])
        xslot = {}

        def xload(tile):
            s_ = nxt("xs1", 5)
            xslot[tile] = s_
            src, rd = x_src(l, c, tile)
            DMA("sp", b_XS1[s_], XS1[s_][:, :], src, r=rd, w=[b_XS1[s_]])
        for t_ in range(4):
            xload(t_)
        pending = []
        rot["n"] = 2
        for blk in range(4):
            tok0 = blk * 512
            for tt in range(4):
                tile = blk * 4 + tt
                if tile + 4 < 16:
                    xload(tile + 4)
                s = xslot[tile]
                si = (32 if c == 2 else 0) + tile
                if do_kv or c != 2:
                    rms_stats(XS1[s][:, :], si, [b_XS1[s]], D)
                h = nxt("hb", 2)
                OP("dve", lambda e, s=s, h=h, si=si: e.tensor_scalar(
                    out=hb[h][:, :], in0=XS1[s][:, :], scalar1=rstd[:, si:si + 1], scalar2=None, op0=ALU.mult),
                   r=[b_XS1[s], b_stat[si]], w=[b_hb[h]])
                transposes_to(lambda tile=tile: hT[:, :, tile * 128:(tile + 1) * 128], hb[h], b_hb[h], [b_hT[blk]])
                for _ in range(3):
                    if pending:
                        pending.pop(0)()
            for g in range(2):
                pb = proj_group(lambda cc, g=g: w_in_sb[:, cc, C_UPOOL + g * 128:C_UPOOL + (g + 1) * 128],
                                lambda cc: hT[:, cc, tok0:tok0 + 512], 512, [b_w["w_in"], b_hT[blk]])
                OP("act", lambda e, pb=pb, g=g: e.activation(out=ub[:, g, HALO + tok0:HALO + tok0 + 512],
                                                             in_=bank(pb), func=AF.Copy),
                   r=[b_bank[pb]], w=[b_ub[blk]])
            if not do_kv:
                continue
            while pending:
                pending.pop(0)()
            chains = []
            for kvh in range(2):
                pb = proj_group(lambda cc, kvh=kvh: w_krep[:, cc, kvh, :],
                                lambda cc: hT[:, cc, tok0:tok0 + 512], 512, [b_w["w_krep"], b_hT[blk]], pb=2 + kvh)
                chains.append(qk_norm_rope(pb, Pk, bf("Pk"), gk, tok0, kT2[:, kvh, tok0:tok0 + 512], [b_kT[blk]],
                                           nbs=(4 + 2 * kvh, 5 + 2 * kvh), staged=True, qs=kvh))
            for st_a, st_b in zip(chains[0], chains[1]):
                pending += [st_a, st_b]
            pb = nxt("proj", 4)

            def fv(e, tok0=tok0, pb=pb):
                i = None
                for tt in range(4):
                    for cc in range(8):
                        i = e.matmul(bank(pb)[:, tt * 128:(tt + 1) * 128],
                                     lhsT=hT[:, cc, tok0 + tt * 128: tok0 + (tt + 1) * 128],
                                     rhs=w_in_sb[:, cc, C_V:C_V + 128], start=(cc == 0), stop=(cc == 7))
                return i
            OP("pe", fv, r=[b_w["w_in"], b_hT[blk]], w=[b_bank[pb]])
            dst = vaug[:, blk * 8:(blk + 1) * 8, 64:128]
            OP("dve", lambda e, pb=pb, dst=dst: e.tensor_copy(
                out=dst, in_=bank(pb).rearrange("p (b d) -> p b d", b=8)),
               r=[b_bank[pb]], w=[b_v[blk]])
        while pending:
            pending.pop(0)()
        rot["n"] = 4

    def pool_stage(c, setidx):
        fence([b_w["w_mem"]] + p2_tmp() + b_XS1[2:], [b_cs, bf("win")])
        for ti in range(2):
            OP("dve", lambda e, ti=ti: e.tensor_tensor_scan(
                out=cs[:, :], data0=ub[:, ti, :], data1=ub[:, ti, :], initial=0.0, op0=ALU.add, op1=ALU.bypass),
               r=b_ub + [b_ubh], w=[b_cs])
            for hf in range(2):
                w = (2, 4, 8, 16)[ti * 2 + hf]
                lo = HALO - w // 2 - 1
                hi = HALO + w // 2 - 1
                ln = slice(hf * 64, (hf + 1) * 64)
                OP("dve", lambda e, ln=ln, lo=lo, hi=hi, ti=ti: e.tensor_tensor(
                    out=win[ln, :], in0=cs[ln, hi:hi + T], in1=cs[ln, lo:lo + T], op=ALU.subtract),
                   r=[b_cs], w=[bf("win")])
            OP("dve", lambda e, ti=ti: e.scalar_tensor_tensor(
                out=dT[:, ti, :], in0=win[:, :], scalar=invw[:, ti:ti + 1], in1=ub[:, ti, HALO:HALO + T],
                op0=ALU.mult, op1=ALU.subtract), r=[bf("win"), bf("invw")] + b_ub, w=[b_dT])
            for (e0, t0) in ((0, 0), (HALO, T - HALO)):
                OP("dve", lambda e, ti=ti, e0=e0, t0=t0: e.tensor_tensor(
                    out=etmp[:, ti, e0:e0 + HALO], in0=win[:, t0:t0 + HALO],
                    in1=ptab_sb[:, setidx, ti * 2 * HALO + e0: ti * 2 * HALO + e0 + HALO], op=ALU.mult),
                   r=[bf("win"), b_w["ptab"]], w=[b_w["etmp"]])
                OP("dve", lambda e, ti=ti, e0=e0, t0=t0: e.tensor_tensor(
                    out=dT[:, ti, t0:t0 + HALO], in0=etmp[:, ti, e0:e0 + HALO],
                    in1=ub[:, ti, HALO + t0:HALO + t0 + HALO], op=ALU.subtract),
                   r=[b_w["etmp"]] + b_ub, w=[b_dT])


    def attention_block(heads, hooks=None):
        seq = []
        for hi, hd in enumerate(heads):
            hd["ob"] = 4 + (hi % 2)
            for g in range(hd["nkt"] // 2):
                seq.append((hd, g))
        pt_of = {}

        def emit_S(n):
            hd, g = seq[n]
            sp = nxt("S", 2)
            k_fn, qap = hd["k_fn"], hd["qap"]

            def f(e):
                e.matmul(bank(2 * sp), lhsT=k_fn(2 * g), rhs=qap, start=True, stop=True)
                return e.matmul(bank(2 * sp + 1), lhsT=k_fn(2 * g + 1), rhs=qap, start=True, stop=True)
            OP("pe", f, r=hd["rd_q"] + hd["rd_k"](2 * g) + hd["rd_k"](2 * g + 1),
               w=[b_bank[2 * sp], b_bank[2 * sp + 1]])
            p = nxt("pt", 3)
            pt_of[n] = p
            OP("act", lambda e: e.activation(out=PTP[p], in_=bank(2 * sp, 2), func=AF.Exp, scale=0.125),
               r=[b_bank[2 * sp], b_bank[2 * sp + 1]], w=[b_PT[p]])

        def emit_PV(n):
            hd, g = seq[n]
            p = pt_of[n]
            ob, nkt, odd = hd["ob"], hd["nkt"], hd["odd"]

            def f(e):
                i = None
                for u in range(2):
                    j = 2 * g + u
                    if odd:
                        i = e.matmul(bank(ob), lhsT=hd["v_odd"](j), rhs=PTP[p][:, u * 512:(u + 1) * 512],
                                     start=(j == 0), stop=(j == nkt - 1))
                    else:
                        i = e.matmul(bank(ob), lhsT=hd["v_even"](j), rhs=PTP[p][:, u * 512:(u + 1) * 512],
                                     start=(j == 0), stop=(j == nkt - 1))
                return i
            OP("pe", f, r=[b_PT[p]] + hd["rd_v"](2 * g) + hd["rd_v"](2 * g + 1), w=[b_bank[ob]])

        def tail(hd):
            odd, ob, hidx = hd["odd"], hd["ob"], hd["hidx"]
            sl = 0 if odd else 64
            dl = slice(64, 128) if odd else slice(0, 64)
            nl = 128 if odd else 65
            rbi = nxt("sr", 2)
            OP("dve", lambda e: e.tensor_copy(out=Rb[rbi][0:nl, :], in_=bank(ob)[0:nl, :]), r=[b_bank[ob]],
               w=[b_R[rbi]])
            OP("dve", lambda e: e.tensor_tensor(out=hd["mix_ap"], in0=Rb[rbi][dl, :], in1=hd["sg_ap"], op=ALU.mult),
               r=[b_R[rbi], hd["sg_buf"]], w=[hd["mix_buf"]])
            DMA("sp", bf("srow_dma"), srow_all[hidx:hidx + 1, :], Rb[rbi][sl:sl + 1, :], r=[b_R[rbi]],
                pw=[bf("srow_all")])

        emit_S(0)
        if len(seq) > 1:
            emit_S(1)
        for n in range(len(seq)):
            if n + 2 < len(seq):
                emit_S(n + 2)
            emit_PV(n)
            hd, g = seq[n]
            if g == hd["nkt"] // 2 - 1:
                tail(hd)
            if hooks and n in hooks:
                for f in hooks[n]:
                    f()

    def normalize_block():
        OP("act", lambda e: e.activation(out=srow_all[0:12, :], in_=srow_all[0:12, :], func=AF.Ln),
           r=[bf("srow_all")], w=[bf("srow_all")])
        OP("act", lambda e: e.activation(out=srow_all[0:12, :], in_=srow_all[0:12, :], func=AF.Exp, scale=-1.0),
           r=[bf("srow_all")], w=[bf("srow_all")])
        OP("dve", lambda e: e.tensor_copy(out=rhl[0:12, 0, :], in_=srow_all[0:12, :]),
           r=[bf("srow_all")], w=[bf("rhl")])
        OP("dve", lambda e: e.tensor_tensor(out=rhl[0:12, 1, :], in0=srow_all[0:12, :], in1=rhl[0:12, 0, :],
                                            op=ALU.subtract), r=[bf("srow_all"), bf("rhl")], w=[bf("rhl")])
        for i in range(6):
            bb = 5 + (i % 2)

            def fb(e, i=i, bb=bb):
                e.matmul(bank(bb), lhsT=sel_bf[0:12, i, :], rhs=rhl[0:12, 0, :], start=True, stop=False)
                return e.matmul(bank(bb), lhsT=sel_bf[0:12, i, :], rhs=rhl[0:12, 1, :], start=False, stop=True)
            OP("pe", fb, r=[bf("rhl"), bf("sel")], w=[b_bank[bb]])
            OP("dve", lambda e, i=i, bb=bb: e.tensor_tensor(out=mixT_blk[:, 2 + i, :], in0=mixT_blk[:, 2 + i, :],
                                                           in1=bank(bb), op=ALU.mult),
               r=[b_mix[2 + i], b_bank[bb]], w=[b_mix[2 + i]])

    def pass2(l, c, nkt):
        fence([b_cs, bf("win")], p2_tmp())
        fence(b_ub + [b_ubh], b_sg + b_mix + b_qx)
        fence(b_hb, [b_yst[0]])
        OP("dve", lambda e: e.memset(qTm[:, :, :], 0.0), w=b_qT)
        gcols = [C_GPOOL, C_GPOOL + 128, C_GATTN, C_GATTN + 128, C_GATTN + 256, C_GATTN + 384, C_GX, C_GX + 128]
        for blk in range(4):
            tok0 = blk * 512
            hrd = [b_w["w_in"], b_hT[blk]]

            def proj_half(col0, pb, half, qb=None):
                qb = blk if qb is None else qb
                qt0 = qb * 512

                def f(e):
                    i_ = None
                    for cc in range(4 * half, 4 * half + 4):
                        i_ = e.matmul(bank(pb), lhsT=w_in_sb[:, cc, col0:col0 + 128], rhs=hT[:, cc, qt0:qt0 + 512],
                                      start=(cc == 0), stop=(cc == 7))
                    return i_
                OP("pe", f, r=[b_w["w_in"], b_hT[qb]], w=[b_bank[pb]])

            def q_chain(i, qb=None):
                qb = blk if qb is None else qb
                col0 = C_Q + i * 128
                st = qk_norm_rope(7, Pq, bf("Pq"), gq, qb * 512, (qTm[0:64, 2 * i, :], qTm[64:128, 2 * i + 1, :]),
                                  [b_qT[i]], nbs=(6, 7), staged=True)
                return [lambda: proj_half(col0, 7, 0, qb), lambda: proj_half(col0, 7, 1, qb)] + st

            def qx_chain(i):
                col0 = C_QX + i * 128

                def cp():
                    OP("act", lambda e: e.activation(out=qxT_blk[:, i, :], in_=bank(7), func=AF.Copy),
                       r=[b_bank[7]], w=[b_qx[i]])
                return [lambda: proj_half(col0, 7, 0), lambda: proj_half(col0, 7, 1), cp]

            def proj_gate(i, pb):
                proj_group(lambda cc: w_in_sb[:, cc, gcols[i]:gcols[i] + 128],
                           lambda cc: hT[:, cc, tok0:tok0 + 512], 512, hrd, pb=pb)
                OP("act", lambda e: e.activation(out=sg_blk[:, i, :], in_=bank(pb), func=AF.Silu),
                   r=[b_bank[pb]], w=[b_sg[i]])

            def pool_mix(ti):
                pb = 7
                OP("pe", lambda e: e.matmul(bank(pb), lhsT=poolw_sb[:, ti, :],
                                            rhs=dT[:, ti, tok0:tok0 + 512], start=True, stop=True),
                   r=[b_w["poolw"], b_dT], w=[b_bank[pb]])
                OP("dve", lambda e: e.scalar_tensor_tensor(
                    out=mixT_blk[:, ti, :], in0=bank(pb), scalar=pscale[:, ti:ti + 1], in1=sg_blk[:, ti, :],
                    op0=ALU.mult, op1=ALU.mult), r=[b_bank[pb], b_sg[ti], b_w["gains"]], w=[b_mix[ti]])

            def head_self(h):
                lane0 = (h % 2) * 64
                kvh = h // 4
                ln = slice(lane0, lane0 + 64)
                return dict(hidx=h, qap=qTm[:, h, :], odd=(lane0 == 64),
                            k_fn=lambda j: kT2[:, kvh, j * 128:(j + 1) * 128], nkt=nkt,
                            v_even=lambda j: vaug_f[:, (j * 2 + kvh) * VB + 64:(j * 2 + kvh) * VB + 192],
                            v_odd=lambda j: vaug[:, j * 2 + kvh, 0:128],
                            sg_ap=sg_blk[ln, 2 + h // 2, :], sg_buf=b_sg[2 + h // 2],
                            mix_ap=mixT_blk[ln, 2 + h // 2, :], mix_buf=b_mix[2 + h // 2],
                            rd_q=[b_qT[h // 2]], rd_k=lambda j: [b_kT[j // 4]], rd_v=lambda j: [b_v[j // 4]])

            def head_mem(hx):
                lane0 = (hx % 2) * 64
                ln = slice(lane0, lane0 + 64)
                return dict(hidx=8 + hx, qap=qxT_blk[ln, hx // 2, :], odd=(lane0 == 64),
                            k_fn=lambda j: kmT[ln, c, hx // 2, j * 128:(j + 1) * 128], nkt=2,
                            v_even=lambda j: vmaug_f[:, (c * 8 + j * 4 + hx) * VB + 64:(c * 8 + j * 4 + hx) * VB + 192],
                            v_odd=lambda j: vmaug[:, c * 8 + j * 4 + hx, 0:128],
                            sg_ap=sg_blk[ln, 6 + hx // 2, :], sg_buf=b_sg[6 + hx // 2],
                            mix_ap=mixT_blk[ln, 6 + hx // 2, :], mix_buf=b_mix[6 + hx // 2],
                            rd_q=[b_qx[hx // 2]], rd_k=lambda j: [b_w["kmv"]], rd_v=lambda j: [b_w["kmv"]])

            xpre = {}
            for tt in range(2):
                s_ = nxt("xst", 2)
                src, rd = x_src(l, c, blk * 4 + tt)
                DMA("sp", b_xst[s_], xst[s_][:, :], src, r=rd, w=[b_xst[s_]])
                xpre[tt] = s_
            for i in range(8):
                proj_gate(i, 5 + (i % 3))
            pool_mix(0)
            pool_mix(1)
            if blk == 0:
                for f in q_chain(0):
                    f()

            npair = nkt // 2
            hooks = {}

            def spread(stages, first):
                for k, f in enumerate(stages):
                    hooks.setdefault(first + k, []).append(f)
            spread(q_chain(1), 0)
            spread(qx_chain(0), npair)
            spread(q_chain(2), 2 * npair)
            spread(qx_chain(1), 3 * npair)
            spread(q_chain(3), 4 * npair)
            if blk < 3:
                spread(q_chain(0, blk + 1), 5 * npair + 1)
            hs = [head_self(h) for h in range(8)]
            hm = [head_mem(hx) for hx in range(4)]
            attention_block(hs[0:5] + [hm[0], hs[5], hm[1], hs[6], hm[2], hs[7], hm[3]], hooks)
            normalize_block()
            for tt in range(4):
                tile = blk * 4 + tt

                ob0 = (6, 2)[tt % 2]

                def fo(e, tt=tt, ob0=ob0):
                    i = None
                    for half in range(2):
                        for cc in range(8):
                            i = e.matmul(bank(ob0 + half), lhsT=mixT_blk[:, cc, tt * 128:(tt + 1) * 128],
                                         rhs=w_out_sb[:, cc, half * 512:(half + 1) * 512], start=(cc == 0),
                                         stop=(cc == 7))
                    return i
                OP("pe", fo, r=b_mix + [b_w["w_out"]], w=[b_bank[ob0], b_bank[ob0 + 1]])
                yps = bank(ob0, 2)
                sidx = 16 + tile
                rms_stats(yps, sidx, [b_bank[ob0], b_bank[ob0 + 1]], D)
                if tt in xpre:
                    s = xpre[tt]
                else:
                    s = nxt("xst", 2)
                    src, rd = x_src(l, c, tile)
                    DMA("sp", b_xst[s], xst[s][:, :], src, r=rd, w=[b_xst[s]])
                y = nxt("yst", 2)
                OP("dve", lambda e, y=y, sidx=sidx, yps=yps: e.scalar_tensor_tensor(
                    out=yst[y][:, :], in0=yps, scalar=rstd[:, sidx:sidx + 1], in1=gpost[:, :], op0=ALU.mult,
                    op1=ALU.mult), r=[b_bank[ob0], b_bank[ob0 + 1], b_stat[sidx], b_w["gains"]], w=[b_yst[y]])
                OP("dve", lambda e, y=y, s=s: e.tensor_tensor(out=yst[y][:, :], in0=yst[y][:, :], in1=xst[s][:, :],
                                                              op=ALU.add), r=[b_yst[y], b_xst[s]], w=[b_yst[y]])
                base = c * T + tile * 128
                if l == nlayers - 1:
                    DMA("sp", b_yst[y], yout[base:base + 128, :], yst[y][:, :], r=[b_yst[y]], pw=[bf("yout")])
                else:
                    DMA("sp", b_yst[y], x1d.ap()[base:base + 128, :], yst[y][:, :], r=[b_yst[y]], pw=[b_x1[c]])

    def load_rope(setidx):
        DMA("pool", b_w["rope"], ropeC[:, :], rope[setidx, 0], pw=[b_w["rope"]])
        DMA("pool", b_w["rope"], ropeS[:, :], rope[setidx, 1], pw=[b_w["rope"]])

    def zero_halos():
        fence(b_sg + b_mix + b_qx, b_ub + [b_ubh])
        OP("dve", lambda e: e.memset(ub[:, :, 0:HALO], 0.0), w=[b_ubh])
        OP("dve", lambda e: e.memset(ub[:, :, HALO + T:UBW], 0.0), w=[b_ubh])

    DMA("sp", b_w["ptab"], ptab_sb[:, :, :], ptab.rearrange("s p n -> p s n"), pw=[b_w["ptab"]])

    with nc.allow_low_precision(reason="bf16 matmul operands, fp32 accumulation"):
        for l in range(nlayers):
            load_layer(l)
            for c in range(NCH):
                mem_kv(c)
            load_rope(1)
            pass1(l, 2, do_kv=True)
            OP("dve", lambda e: e.tensor_copy(out=halo_st[:, :, 0:HALO], in_=ub[:, :, HALO:2 * HALO]),
               r=b_ub, w=[b_w["halo_st"]])
            OP("dve", lambda e: e.tensor_copy(out=halo_st[:, :, HALO:2 * HALO], in_=ub[:, :, T:T + HALO]),
               r=b_ub, w=[b_w["halo_st"]])
            xin = xin_d[l].ap()
            xout = xout_d[l].ap()
            with nc.allow_non_contiguous_dma(reason="exchange payload"):
                for kvh in range(2):
                    DMA("pool", b_xin[l], xin[kvh * 64:(kvh + 1) * 64, XK0:XK0 + T], kT2[0:64, kvh, 0:T],
                        r=b_kT[0:4], pw=[b_xin[l]])
                DMA("pool", b_xin[l], xin[:, XV0:XV0 + XVW], vaug[:, 0:32, :].rearrange("p b c -> p (b c)"), r=b_v[0:4],
                    pw=[b_xin[l]])
                DMA("pool", b_xin[l], xin[:, XH0:XW], halo_st[:, :, :].rearrange("p a b -> p (a b)"),
                    r=[b_w["halo_st"]], pw=[b_xin[l]])
            tr._collect("pool", [b_xin[l]], [b_xout[l]], True)
            E = tr.eng["pool"]
            inst = nc.gpsimd.collective_compute(
                "AllGather", ALU.bypass, replica_groups=[[2 * i, 2 * i + 1] for i in range(ncores // 2)],
                ins=[xin.opt()], outs=[xout.opt()])
            E["cnt"] += 1
            inst.then_inc(E["sem"], 1)
            tr._record(E["key"], (E["sem"], E["cnt"], None), [b_xin[l]], [b_xout[l]])
            for c in range(2):
                load_rope(0)
                zero_halos()
                pass1(l, c)
                pool_stage(c, 0)
                pass2(l, c, 16)
            load_rope(1)
            pass1(l, 2, do_kv=False)
            with nc.allow_non_contiguous_dma(reason="exchange payload"):
                for rnk in range(2):
                    for kvh in range(2):
                        for rep in range(2):
                            DMA("sp", bf("kvload"),
                                kT2[rep * 64:(rep + 1) * 64, kvh, rnk * T:(rnk + 1) * T],
                                xout[rnk * 128 + kvh * 64: rnk * 128 + (kvh + 1) * 64, XK0:XK0 + T],
                                r=[b_xout[l]], pw=b_kT)
                    DMA("sp", bf("kvload"), vaug[:, rnk * 32:(rnk + 1) * 32, :].rearrange("p b c -> p (b c)"),
                        xout[rnk * 128:(rnk + 1) * 128, XV0:XV0 + XVW], r=[b_xout[l]], pw=b_v)
                DMA("sp", b_w["halo_in"], halo_in[:, :, 0:2 * HALO],
                    xout[0:128, XH0:XW].rearrange("p (a b) -> p a b", a=2), r=[b_xout[l]], pw=[b_w["halo_in"]])
                DMA("sp", b_w["halo_in"], halo_in[:, :, 2 * HALO:4 * HALO],
                    xout[128:256, XH0:XW].rearrange("p (a b) -> p a b", a=2), r=[b_xout[l]], pw=[b_w["halo_in"]])
            OP("dve", lambda e: e.tensor_scalar(out=ub[:, :, 0:HALO], in0=halo_in[:, :, HALO:2 * HALO],
                                                scalar1=hmask_sb[:, 0:1], scalar2=None, op0=ALU.mult),
               r=[b_w["halo_in"], b_w["gains"]], w=[b_ubh])
            OP("dve", lambda e: e.tensor_scalar(out=ub[:, :, HALO + T:UBW], in0=halo_in[:, :, 2 * HALO:3 * HALO],
                                                scalar1=hmask_sb[:, 1:2], scalar2=None, op0=ALU.mult),
               r=[b_w["halo_in"], b_w["gains"]], w=[b_ubh])
            pool_stage(2, 1)
            pass2(l, 2, 32)
        tr.wait_all("sp", [bf("yout")] + b_yst)
    return nc, tr


def _rope_tables(pos):
    pos = np.asarray(pos, dtype=np.float64)
    row = np.floor(pos / 64.0)
    col = pos - 64.0 * row
    freqs = 10000.0 ** (-np.arange(16, dtype=np.float64) / 16.0)
    cosT = np.zeros((128, len(pos)), np.float32)
    sinT = np.zeros((128, len(pos)), np.float32)
    for lane in range(128):
        d = lane % 64
        axis, ab, p = d // 32, (d % 32) // 16, d % 16
        ang = (row if axis == 0 else col) * freqs[p]
        cosT[lane] = np.cos(ang)
        sinT[lane] = (-1.0 if ab == 0 else 1.0) * np.sin(ang)
    return cosT, sinT


def _pool_tab(t_glob, tseq):
    tab = np.zeros((128, 2, 2 * HALO), np.float32)
    for ti in range(2):
        for lane in range(128):
            w = (2, 4, 8, 16)[ti * 2 + lane // 64]
            for j, tg in enumerate(t_glob):
                lo = min(max(tg - w // 2, 0), tseq)
                hi = min(max(tg + w - w // 2, 0), tseq)
                tab[lane, ti, j] = 1.0 / float(hi - lo)
    return tab


def _consts():
    ident = np.eye(128, dtype=np.float32)
    swap = np.zeros((128, 128), np.float32)
    for m in range(128):
        swap[m ^ 16, m] = 1.0
    bones = np.zeros((128, 128), np.float32)
    bones[:64, :64] = 1.0 / 64.0
    bones[64:, 64:] = 1.0 / 64.0
    return np.stack([ident, swap, bones], 0)


def _selm():
    m = np.zeros((12, 6, 128), np.float32)
    for h in range(12):
        m[h, h // 2, (h % 2) * 64:(h % 2 + 1) * 64] = 1.0
    return m.reshape(12, 768)


_PROG = {}


def kernel(x_prompt, x_sample, mem_prompt, mem_sample, norm_pre, norm_post, w_in, pool_w, pool_scale,
           q_norm, k_norm, mem_norm, w_mem_kv, w_out):
    f32 = lambda a: np.ascontiguousarray(np.asarray(a, dtype=np.float32))
    x_prompt, x_sample, mem_prompt, mem_sample = map(f32, (x_prompt, x_sample, mem_prompt, mem_sample))
    if "nc" not in _PROG:
        _PROG["nc"], _PROG["tr"] = build_program()
    nc = _PROG["nc"]
    in_maps = _prep_inputs(x_prompt, x_sample, mem_prompt, mem_sample, norm_pre, norm_post, w_in, pool_w, pool_scale,
                           q_norm, k_norm, mem_norm, w_mem_kv, w_out)
    res = run_bass_kernel_spmd(nc, in_maps, core_ids=list(range(NCORES)))
    y_prompt = np.empty_like(x_prompt)
    y_sample = np.empty_like(x_sample)
    for c in range(NCORES):
        y = res.results[c]["y"]
        y_prompt[2 * c] = y[0:T]
        y_prompt[2 * c + 1] = y[T:2 * T]
        y_sample[c // 2, (c % 2) * T:(c % 2 + 1) * T] = y[2 * T:3 * T]
    return (y_prompt, y_sample)


def _prep_inputs(x_prompt, x_sample, mem_prompt, mem_sample, norm_pre, norm_post, w_in, pool_w, pool_scale,
                 q_norm, k_norm, mem_norm, w_mem_kv, w_out, ncores=NCORES):
    f32 = lambda a: np.ascontiguousarray(np.asarray(a, dtype=np.float32))
    cm = _consts()
    shared = {
        "w_in": f32(w_in), "w_out": f32(w_out), "w_mem": f32(w_mem_kv),
        "pool_w": f32(pool_w).reshape(2, 256, 64), "norm_pre": f32(norm_pre), "norm_post": f32(norm_post),
        "mem_norm": f32(mem_norm), "pool_scale": f32(pool_scale), "q_norm": f32(q_norm), "k_norm": f32(k_norm),
        "cmat": cm, "selm": _selm(),
    }
    cp, sp_ = _rope_tables(np.arange(T))
    tab_p = _pool_tab(list(range(HALO)) + list(range(T - HALO, T)), T)
    in_maps = []
    for c in range(ncores):
        sq, half = c // 2, c % 2
        t0 = half * T
        xs = np.concatenate([x_prompt[2 * c], x_prompt[2 * c + 1], x_sample[sq, t0:t0 + T]], 0)
        mm = np.concatenate([mem_prompt[2 * c], mem_prompt[2 * c + 1], mem_sample[sq]], 0)
        cs_, ss_ = _rope_tables(np.arange(t0, t0 + T))
        rope = np.stack([np.stack([cp, sp_], 0), np.stack([cs_, ss_], 0)], 0)
        tab_s = _pool_tab(list(range(t0, t0 + HALO)) + list(range(t0 + T - HALO, t0 + T)), 2 * T)
        ptab = np.stack([tab_p.reshape(128, -1), tab_s.reshape(128, -1)], 0)
        hm = np.zeros((128, 2), np.float32)
        hm[:, 0] = 1.0 if half == 1 else 0.0
        hm[:, 1] = 1.0 if half == 0 else 0.0
        m = dict(shared)
        m.update({"xs": np.ascontiguousarray(xs), "mems": np.ascontiguousarray(mm),
                  "rope": np.ascontiguousarray(rope.astype(np.float32)), "ptab": np.ascontiguousarray(ptab),
                  "hmask": hm})
        in_maps.append(m)
    return in_maps
```

```python
import numpy as np
import ml_dtypes
import concourse.bass as bass
import concourse.mybir as mybir
from concourse.bass_utils import run_bass_kernel_spmd

F32 = mybir.dt.float32
BF16 = mybir.dt.bfloat16
AF = mybir.ActivationFunctionType
ALU = mybir.AluOpType

NCORES = 8
D = 1024
T = 2048
NCH = 3
NMEM = 256
INW = 2304
EPS = 1e-6
HALO = 16
UBW = T + 2 * HALO
VB = 129
XW = 2 * T // 2 + 0

C_UPOOL, C_GPOOL, C_Q, C_K, C_V, C_GATTN, C_QX, C_GX = 0, 256, 512, 1024, 1152, 1280, 1792, 2048

XK0 = 0
XV0 = T
XVW = 16 * 2 * VB
XH0 = XV0 + XVW
XW = XH0 + 2 * 2 * HALO


class Buf:
    __slots__ = ("name", "w", "r", "sem", "semcnt", "excl")

    def __init__(self, name, excl=False):
        self.name = name
        self.excl = excl
        self.w = {}
        self.r = {}
        self.sem = None
        self.semcnt = 0


class Tracker:
    def __init__(self, nc):
        self.nc = nc
        self.eng = {}
        for n, e in (("pe", nc.tensor), ("act", nc.scalar), ("dve", nc.vector),
                     ("pool", nc.gpsimd), ("sp", nc.sync)):
            sem = nc.alloc_semaphore(name=f"sem_{n}")
            self.eng[n] = {"e": e, "sem": sem, "key": f"sem_{n}", "cnt": 0, "seen": {}}
        self.nwaits = 0
        self.nops = 0

    def _collect(self, en, reads, writes, is_dma, pwrites=(), pkey=None):
        E = self.eng[en]
        need = {}

        def add(key, ev, raw):
            h, val, owner = ev
            if owner is not None:
                val = owner.semcnt
            if not is_dma and key == E["key"] and en == "pe":
                return
            cur = need.get(key)
            if cur is None or cur[1] < val:
                need[key] = (h, val)

        for b in reads:
            for key, ev in b.w.items():
                add(key, ev, True)
        for b in writes:
            for key, ev in b.w.items():
                add(key, ev, False)
            for key, ev in b.r.items():
                add(key, ev, False)
        for b in pwrites:
            for key, ev in b.r.items():
                add(key, ev, False)
            for key, ev in b.w.items():
                if key != pkey:
                    add(key, ev, False)
        for key, (h, val) in need.items():
            if E["seen"].get(key, 0) < val:
                E["e"].wait_ge(h, val)
                E["seen"][key] = val
                self.nwaits += 1

    def _record(self, key, ev, reads, writes, pwrites=()):
        for b in reads:
            b.r[key] = ev
        for b in writes:
            b.w = {key: ev}
            b.r = {}
        for b in pwrites:
            b.w[key] = ev
            b.r = {}

    def op(self, en, fn, r=(), w=()):
        E = self.eng[en]
        if any(b.excl for b in r):
            w = list(w) + [b for b in r if b.excl]
            r = [b for b in r if not b.excl]
        self._collect(en, r, w, False)
        inst = fn(E["e"])
        E["cnt"] += 1
        inst.then_inc(E["sem"], 1)
        self._record(E["key"], (E["sem"], E["cnt"], None), r, w)
        self.nops += 1

    def dma(self, q, owner, out, in_, r=(), w=(), pw=(), **kw):
        E = self.eng[q]
        if owner.sem is None:
            owner.sem = self.nc.alloc_semaphore(name=f"dsem_{owner.name}")
        self._collect(q, r, w, True, pw, f"dsem_{owner.name}")
        inst = E["e"].dma_start(out=out, in_=in_, **kw)
        owner.semcnt += 16
        inst.then_inc(owner.sem, 16)
        self._record(f"dsem_{owner.name}", (owner.sem, owner.semcnt, owner), r, w, pw)
        self.nops += 1

    def wait_all(self, en, bufs):
        self._collect(en, [], bufs, True)


def build_program(nlayers=2, ncores=NCORES):
    nc = bass.Bass("TRN2", target_bir_lowering=False)
    tr = Tracker(nc)

    def dram_in(name, shape, dt=F32):
        return nc.dram_tensor(name, list(shape), dt, kind="ExternalInput").ap()

    xs = dram_in("xs", [NCH * T, D])
    mems = dram_in("mems", [NCH * NMEM, D])
    w_in = dram_in("w_in", [2, D, INW])
    w_out = dram_in("w_out", [2, D, D])
    w_mem = dram_in("w_mem", [2, D, 512])
    pool_w = dram_in("pool_w", [2, 256, 64])
    norm_pre = dram_in("norm_pre", [2, D])
    norm_post = dram_in("norm_post", [2, D])
    mem_norm = dram_in("mem_norm", [2, D])
    pool_scale = dram_in("pool_scale", [2, 256])
    q_norm = dram_in("q_norm", [2, 64])
    k_norm = dram_in("k_norm", [2, 64])
    rope = dram_in("rope", [2, 2, 128, T])
    ptab = dram_in("ptab", [2, 128, 2 * 2 * HALO])
    hmask = dram_in("hmask", [128, 2])
    cmat = dram_in("cmat", [3, 128, 128])
    selm = dram_in("selm", [12, 6 * 128])
    yout = nc.dram_tensor("y", [NCH * T, D], F32, kind="ExternalOutput").ap()
    x1d = nc.dram_tensor("x1_scratch", [NCH * T, D], F32)
    xin_d = [nc.dram_tensor(f"xin{l}", [128, XW], BF16) for l in range(2)]
    xout_d = [nc.dram_tensor(f"xout{l}", [256, XW], BF16) for l in range(2)]
    b_x1 = [Buf(f"x1_{c}") for c in range(NCH)]
    b_xin = [Buf(f"xin{l}") for l in range(2)]
    b_xout = [Buf(f"xout{l}") for l in range(2)]

    def sb(name, shape, dt):
        return nc.alloc_sbuf_tensor(name, list(shape), dt)

    w_in_sb = sb("w_in_sb", [128, 8, INW], BF16)
    w_out_sb = sb("w_out_sb", [128, 8, D], BF16)
    w_krep = sb("w_krep", [128, 8, 2, 128], BF16)
    poolw_sb = sb("poolw_sb", [128, 2, 128], BF16)
    hT = sb("hT", [128, 8, T], BF16)
    def view(region, byte_off, shape, dt):
        esz = 4 if dt == F32 else 2
        n = 1
        for d_ in shape[1:]:
            n *= d_
        a = region[:, byte_off // 2: byte_off // 2 + n * esz // 2]
        if dt == F32:
            a = a.bitcast(F32)
        if len(shape) == 3:
            a = a.rearrange("p (a b) -> p a b", a=shape[1])
        return a

    regX = sb("regX", [128, 18432 // 2], BF16)
    regY = sb("regY", [128, 16512 // 2], BF16)
    regZ = sb("regZ", [128, 4096 // 2], BF16)
    ub = view(regX, 0, [128, 2, UBW], F32)
    sg_blk = view(regX, 0, [128, 8, 512], BF16)
    mixT_blk = view(regX, 8192, [128, 8, 512], BF16)
    qxT_blk = view(regX, 16384, [128, 2, 512], BF16)
    w_mem_sb = view(regY, 0, [128, 8, 512], BF16)
    cs = view(regY, 0, [128, UBW], F32)
    win = view(regY, 8320, [128, T], F32)
    qTm = view(regY, 0, [128, 8, 512], BF16)
    kT2 = sb("kT2", [128, 2, 2 * T], BF16)
    vaug_f = sb("vaug", [128, 64 * VB + 64], BF16)
    vaug = vaug_f[:, 0:64 * VB].rearrange("p (b c) -> p b c", c=VB)
    xst = [sb(f"xst{i}", [128, D], F32) for i in range(2)]
    yst = [view(regZ, 0, [128, D], F32)] * 2
    NPT = 3
    PTP = [view(regY, 8192, [128, 1024], BF16), view(regY, 10240, [128, 1024], BF16),
           view(regY, 14336, [128, 1024], BF16)]
    ropeC = sb("ropeC", [128, T], BF16)
    ropeS = sb("ropeS", [128, T], BF16)
    dT = sb("dT", [128, 2, T], BF16)
    kmT = sb("kmT", [128, NCH, 2, NMEM], BF16)
    vmaug_f = sb("vmaug", [128, NCH * 8 * VB + 64], BF16)
    vmaug = vmaug_f[:, 0:NCH * 8 * VB].rearrange("p (b c) -> p b c", c=VB)
    hb = [view(regZ, i * 2048, [128, D], BF16) for i in range(2)]
    ident_bf = sb("ident_bf", [128, 128], BF16)
    bones_bf = sb("bones_bf", [128, 128], BF16)
    swapP = sb("swapP", [128, 128], F32)
    Pq = sb("Pq", [128, 128], BF16)
    Pk = sb("Pk", [128, 128], BF16)
    ones_bf = sb("ones_bf", [128, 64], BF16)
    gq = sb("gq", [128, 1], F32)
    gk = sb("gk", [128, 1], F32)
    gpre = sb("gpre", [128, 8], F32)
    gmem = sb("gmem", [128, 8], F32)
    gpost = sb("gpost", [128, D], F32)
    pscale = sb("pscale", [128, 2], F32)
    invw = sb("invw", [128, 2], F32)
    hmask_sb = sb("hmask_sb", [128, 2], F32)
    ptab_sb = sb("ptab_sb", [128, 2, 2 * 2 * HALO], F32)
    halo_st = sb("halo_st", [128, 2, 2 * HALO], BF16)
    halo_in = sb("halo_in", [128, 2, 2 * 2 * HALO], BF16)
    etmp = sb("etmp", [128, 2, 2 * HALO], F32)
    ss = sb("ss", [128, 64], F32)
    lnv = sb("lnv", [128, 64], F32)
    rstd = sb("rstd", [128, 64], F32)
    NQS = 1
    sqb = [sb(f"sqb{i}", [128, 512], BF16) for i in range(NQS)]
    zsb = [sb(f"zsb{i}", [128, 512], BF16) for i in range(NQS)]
    t1b = [sb(f"t1b{i}", [128, 512], F32) for i in range(NQS)]
    sqb.append(view(regY, 12288, [128, 512], BF16))
    zsb.append(view(regY, 13312, [128, 512], BF16))
    t1b.append(view(regY, 14336, [128, 512], F32))
    srow_all = view(regY, 12288, [128, 512], F32)
    rhl = view(regY, 14336, [128, 2, 512], BF16)
    sel_bf = sb("sel_bf", [12, 6, 128], BF16)
    Rb0 = sb("Rb0", [128, 512], F32)
    junk = Rb0[:, :].bitcast(BF16)
    Rb = [Rb0, t1b[0]]

    ps_all = nc.alloc_psum_tensor("ps_all", [128, 8 * 512], F32)

    def bank(b, n=1):
        return ps_all[:, b * 512:(b + n) * 512]

    B = {}

    def bf(name):
        if name not in B:
            B[name] = Buf(name)
        return B[name]

    b_bank = [bf(f"bank{i}") for i in range(8)]
    for b_ in b_bank:
        b_.excl = True
    b_xst = [bf(f"xst{i}") for i in range(2)]
    b_yst = [bf("yst0")] * 2
    b_hb = [bf(f"hb{i}") for i in range(2)]
    b_PT = [bf("PT0"), bf("PT1"), bf("rhl")]
    b_hT = [bf(f"hT{i}") for i in range(4)]
    b_kT = [bf(f"kT{i}") for i in range(8)]
    b_v = [bf(f"v{i}") for i in range(8)]
    b_ub = [bf(f"ub{i}") for i in range(4)]
    b_ubh = bf("ubh")
    b_dT = bf("dT")
    b_cs = bf("cs")
    b_qT = [bf(f"qT{i}") for i in range(4)]
    b_qz = bf("qzero")
    b_qx = [bf(f"qx{i}") for i in range(2)]
    b_sg = [bf(f"sg{i}") for i in range(8)]
    b_mix = [bf(f"mix{i}") for i in range(8)]
    b_w = {n: bf(n) for n in ("w_in", "w_out", "w_mem", "w_krep", "poolw", "consts", "gains",
                              "rope", "ptab", "kmv", "stat", "halo_st", "halo_in", "etmp", "junk")}
    b_stat = [bf(f"stat{i}") for i in range(64)]
    b_sq = [bf(f"sq{i}") for i in range(NQS)]
    b_zs = [bf(f"zs{i}") for i in range(NQS)]
    b_t1 = [bf(f"t1{i}") for i in range(NQS)]
    b_sq.append(bf("sq_1"))
    b_zs.append(bf("zs_1"))
    b_t1.append(bf("t1_1"))
    b_gt = []
    b_srow = [bf("srow_all"), bf("rhl")]
    b_R = [bf("R0"), b_t1[0]]
    b_tb = []

    OP = tr.op
    DMA = tr.dma
    XS1 = [xst[0], xst[1]] + [view(regY, i * 4096, [128, D], F32) for i in range(3)]
    b_XS1 = [b_xst[0], b_xst[1]] + [bf(f"xsY{i}") for i in range(3)] + [b_sq[1], b_zs[1], b_t1[1]]
    XSW = [xst[0], xst[1]] + [view(regX, i * 4096, [128, D], F32) for i in range(4)]
    b_XSW = [b_xst[0], b_xst[1]] + [bf(f"xsX{i}") for i in range(4)]

    def fence(A, Bs):
        for a in A:
            for src in (a.w, a.r):
                for key, ev in src.items():
                    for b in Bs:
                        cur = b.r.get(key)
                        if cur is None or cur[1] < ev[1]:
                            b.r[key] = ev

    def p2_tmp():
        return b_qT + b_PT + b_gt + b_srow + b_tb
    rr = {"proj": 0, "nrm": 0, "qs": 0, "gt": 0, "pt": 0, "S": 0, "O": 0, "xst": 0, "yst": 0, "hb": 0,
          "sr": 0, "xs1": 0, "xsw": 0}

    def nxt(k, n):
        if k == "proj":
            n = rot["n"]
        v = rr[k] % n
        rr[k] = (v + 1) % n
        return v
    rot = {"n": 4}

    cst = xst[0][:, 0:384].rearrange("p (k m) -> p k m", k=3)
    with nc.allow_non_contiguous_dma(reason="small constant / gain loads"):
        DMA("sp", b_xst[0], cst, cmat.rearrange("k p m -> p k m"), w=[b_xst[0]])
        DMA("sp", b_w["ptab"], hmask_sb[:, :], hmask, pw=[b_w["gains"]])
    OP("dve", lambda e: e.tensor_copy(out=ident_bf[:, :], in_=cst[:, 0, :]), r=[b_xst[0]], w=[bf("ident")])
    OP("dve", lambda e: e.tensor_copy(out=swapP[:, :], in_=cst[:, 1, :]), r=[b_xst[0]], w=[bf("swapP")])
    OP("dve", lambda e: e.tensor_copy(out=bones_bf[:, :], in_=cst[:, 2, :]), r=[b_xst[0]], w=[bf("bones")])
    OP("dve", lambda e: e.memset(ones_bf[:, :], 1.0), w=[bf("ones")])
    DMA("pool", bf("sel"), sel_bf[:, :, :].rearrange("p a b -> p (a b)"), selm, w=[bf("sel")])
    OP("dve", lambda e: e.memset(invw[0:64, 0:1], 0.5), w=[bf("invw")])
    OP("dve", lambda e: e.memset(invw[64:128, 0:1], 0.25), w=[bf("invw")])
    OP("dve", lambda e: e.memset(invw[0:64, 1:2], 0.125), w=[bf("invw")])
    OP("dve", lambda e: e.memset(invw[64:128, 1:2], 0.0625), w=[bf("invw")])
    OP("dve", lambda e: e.memset(vaug_f[:, :], 0.0), w=b_v)
    OP("dve", lambda e: e.memset(vaug[:, :, 0:1], 1.0), w=b_v)
    OP("dve", lambda e: e.memset(vaug[:, :, 128:129], 1.0), w=b_v)
    OP("dve", lambda e: e.memset(vmaug_f[:, :], 0.0), w=[b_w["kmv"]])
    OP("dve", lambda e: e.memset(vmaug[:, :, 0:1], 1.0), w=[b_w["kmv"]])
    OP("dve", lambda e: e.memset(vmaug[:, :, 128:129], 1.0), w=[b_w["kmv"]])

    def rms_stats(src_ap, idx, rd, nfree, extra_w=()):
        OP("act", lambda e: e.activation(out=junk[:, 0:nfree], in_=src_ap, func=AF.Square,
                                         accum_out=ss[:, idx:idx + 1]),
           r=rd, w=[b_R[0], b_stat[idx]])
        OP("act", lambda e: e.activation(out=lnv[:, idx:idx + 1], in_=ss[:, idx:idx + 1], func=AF.Ln,
                                         scale=1.0 / nfree, bias=eps_t[:, 0:1]),
           r=[b_stat[idx], bf("eps")], w=[b_stat[idx]])
        OP("act", lambda e: e.activation(out=rstd[:, idx:idx + 1], in_=lnv[:, idx:idx + 1], func=AF.Exp,
                                         scale=-0.5),
           r=[b_stat[idx]], w=[b_stat[idx]])

    eps_t = sb("eps_t", [128, 1], F32)
    OP("dve", lambda e: e.memset(eps_t[:, :], EPS), w=[bf("eps")])

    def transposes_to(dst_ap_fn, src_tile, src_buf, dst_bufs, nchunk=8):
        pb = nxt("proj", 4)
        psb = bank(pb).bitcast(BF16)

        def f(e):
            i = None
            for c in range(nchunk):
                i = e.transpose(out=psb[:, c * 128:(c + 1) * 128], in_=src_tile[:, c * 128:(c + 1) * 128],
                                identity=ident_bf[:, :])
            return i
        OP("pe", f, r=[src_buf, bf("ident")], w=[b_bank[pb]])
        OP("dve", lambda e: e.tensor_copy(out=dst_ap_fn(),
                                          in_=psb[:, 0:nchunk * 128].rearrange("p (c t) -> p c t", c=nchunk)),
           r=[b_bank[pb]], w=dst_bufs)

    def proj_group(lhs_fn, rhs_fn, n, rd, nk=8, pb=None):
        if pb is None:
            pb = nxt("proj", 4)

        def f(e):
            i = None
            for c in range(nk):
                i = e.matmul(bank(pb)[:, 0:n], lhsT=lhs_fn(c), rhs=rhs_fn(c), start=(c == 0), stop=(c == nk - 1))
            return i
        OP("pe", f, r=rd, w=[b_bank[pb]])
        return pb

    def qk_norm_rope(pb, Pmat, Pbuf, gvec, tok0, dst_ap, dst_bufs, nbs=None, staged=False, qs=0):
        s = qs
        if nbs is None:
            nb = 4 + 2 * nxt("nrm", 2)
            nbz = nb + 1
        else:
            nb, nbz = nbs
        z = bank(pb)

        def st_act1():
            OP("act", lambda e: e.activation(out=sqb[s][:, :], in_=z, func=AF.Square), r=[b_bank[pb]], w=[b_sq[s]])
            OP("act", lambda e: e.activation(out=zsb[s][:, :], in_=z, func=AF.Copy), r=[b_bank[pb]], w=[b_zs[s]])

        def st_dve0():
            OP("dve", lambda e: e.scalar_tensor_tensor(out=t1b[s][:, :], in0=z, scalar=gvec[:, 0:1],
                                                       in1=ropeC[:, tok0:tok0 + 512], op0=ALU.mult, op1=ALU.mult),
               r=[b_bank[pb], b_w["rope"], b_w["gains"]], w=[b_t1[s]])

        def st_pe():
            OP("pe", lambda e: e.matmul(bank(nb), lhsT=bones_bf[:, :], rhs=sqb[s][:, :], start=True, stop=True),
               r=[b_sq[s], bf("bones")], w=[b_bank[nb]])
            OP("pe", lambda e: e.matmul(bank(nbz), lhsT=Pmat[:, :], rhs=zsb[s][:, :], start=True, stop=True),
               r=[b_zs[s], Pbuf], w=[b_bank[nbz]])

        def st_act2():
            OP("act", lambda e: e.activation(out=bank(nb), in_=bank(nb), func=AF.Ln, bias=eps_t[:, 0:1]),
               r=[b_bank[nb], bf("eps")], w=[b_bank[nb]])
            OP("act", lambda e: e.activation(out=bank(nb), in_=bank(nb), func=AF.Exp, scale=-0.5),
               r=[b_bank[nb]], w=[b_bank[nb]])

        def st_dve1():
            OP("dve", lambda e: e.tensor_tensor(out=bank(nbz), in0=bank(nbz), in1=ropeS[:, tok0:tok0 + 512],
                                                op=ALU.mult),
               r=[b_bank[nbz], b_w["rope"]], w=[b_bank[nbz]])
            OP("dve", lambda e: e.tensor_tensor(out=t1b[s][:, :], in0=t1b[s][:, :], in1=bank(nbz), op=ALU.add),
               r=[b_t1[s], b_bank[nbz]], w=[b_t1[s]])

        def st_dve2():
            if isinstance(dst_ap, tuple):
                for hf, dap in enumerate(dst_ap):
                    ln_ = slice(hf * 64, (hf + 1) * 64)
                    OP("dve", lambda e, ln_=ln_, dap=dap: e.tensor_tensor(out=dap, in0=t1b[s][ln_, :],
                                                                         in1=bank(nb)[ln_, :], op=ALU.mult),
                       r=[b_t1[s], b_bank[nb]], w=dst_bufs)
            else:
                OP("dve", lambda e: e.tensor_tensor(out=dst_ap, in0=t1b[s][:, :], in1=bank(nb), op=ALU.mult),
                   r=[b_t1[s], b_bank[nb]], w=dst_bufs)
        stages = [st_act1, st_dve0, st_pe, st_act2, st_dve1, st_dve2]
        if staged:
            return stages
        for f in stages:
            f()

    def load_layer(l):
        fence([b_cs, bf("win")] + p2_tmp() + b_XS1[2:], [b_w["w_mem"]])
        fence([b_yst[0]], b_hb)
        with nc.allow_non_contiguous_dma(reason="gain vectors"):
            DMA("sp", b_w["gains"], gpre[:, :], norm_pre[l].rearrange("(c p) -> p c", p=128), pw=[b_w["gains"]])
            DMA("sp", b_w["gains"], gmem[:, :], mem_norm[l].rearrange("(c p) -> p c", p=128), pw=[b_w["gains"]])
            DMA("sp", b_w["gains"], pscale[:, :], pool_scale[l].rearrange("(c p) -> p c", p=128),
                pw=[b_w["gains"]])
            for hh in range(2):
                DMA("sp", b_w["gains"], gq[hh * 64:(hh + 1) * 64, :], q_norm[l].rearrange("(p o) -> p o", o=1),
                    pw=[b_w["gains"]])
                DMA("sp", b_w["gains"], gk[hh * 64:(hh + 1) * 64, :], k_norm[l].rearrange("(p o) -> p o", o=1),
                    pw=[b_w["gains"]])
            DMA("sp", b_w["gains"], gpost[:, :], norm_post[l:l + 1, :].to_broadcast([128, D]), pw=[b_w["gains"]])
        OP("dve", lambda e: e.tensor_scalar(out=Pq[:, :], in0=swapP[:, :], scalar1=gq[:, 0:1], scalar2=None,
                                            op0=ALU.mult), r=[bf("swapP"), b_w["gains"]], w=[bf("Pq")])
        OP("dve", lambda e: e.tensor_scalar(out=Pk[:, :], in0=swapP[:, :], scalar1=gk[:, 0:1], scalar2=None,
                                            op0=ALU.mult), r=[bf("swapP"), b_w["gains"]], w=[bf("Pk")])
        fence(b_ub + [b_ubh] + b_sg + b_mix + b_qx, b_XSW[2:])
        jobs = []
        for c in range(8):
            for (o, n) in [(0, 1024), (1024, 1024), (2048, 256)]:
                jobs.append(("in", c, o, n))
        for c in range(8):
            jobs.append(("mem", c, 0, 512))
        slot_of = {}

        def issue(k):
            kind, c, o, n = jobs[k]
            s_ = nxt("xsw", 6)
            slot_of[k] = s_
            src = w_in[l, c * 128:(c + 1) * 128, o:o + n] if kind == "in" else w_mem[l, c * 128:(c + 1) * 128, :]
            DMA("sp", b_XSW[s_], XSW[s_][:, 0:n], src, w=[b_XSW[s_]])
        for k in range(min(5, len(jobs))):
            issue(k)
        for k, (kind, c, o, n) in enumerate(jobs):
            if k + 5 < len(jobs):
                issue(k + 5)
            s_ = slot_of[k]
            if kind == "in":
                if k % 2 == 0:
                    OP("dve", lambda e, s_=s_, c=c, o=o, n=n: e.tensor_scalar(
                        out=w_in_sb[:, c, o:o + n], in0=XSW[s_][:, 0:n], scalar1=gpre[:, c:c + 1], scalar2=None,
                        op0=ALU.mult), r=[b_XSW[s_], b_w["gains"]], w=[b_w["w_in"]])
                else:
                    OP("act", lambda e, s_=s_, c=c, o=o, n=n: e.activation(
                        out=w_in_sb[:, c, o:o + n], in_=XSW[s_][:, 0:n], func=AF.Copy, scale=gpre[:, c:c + 1]),
                       r=[b_XSW[s_], b_w["gains"]], w=[b_w["w_in"]])
                if o == 1024:
                    for kvh in range(2):
                        for rep in range(2):
                            OP("dve", lambda e, c=c, kvh=kvh, rep=rep: e.tensor_copy(
                                out=w_krep[:, c, kvh, rep * 64:(rep + 1) * 64],
                                in_=w_in_sb[:, c, C_K + kvh * 64:C_K + (kvh + 1) * 64]),
                               r=[b_w["w_in"]], w=[b_w["w_krep"]])
            else:
                OP("dve", lambda e, s_=s_, c=c: e.tensor_scalar(
                    out=w_mem_sb[:, c, :], in0=XSW[s_][:, 0:512], scalar1=gmem[:, c:c + 1], scalar2=None,
                    op0=ALU.mult), r=[b_XSW[s_], b_w["gains"]], w=[b_w["w_mem"]])
        for c in range(8):
            DMA("pool", b_w["w_out"], w_out_sb[:, c, :], w_out[l, c * 128:(c + 1) * 128, :], pw=[b_w["w_out"]])
        OP("dve", lambda e: e.memset(poolw_sb[:, :, :], 0.0), w=[b_w["poolw"]])
        for g in range(4):
            ti, hf = g // 2, g % 2
            DMA("pool", b_w["poolw"], poolw_sb[hf * 64:(hf + 1) * 64, ti, hf * 64:(hf + 1) * 64],
                pool_w[l, g * 64:(g + 1) * 64, :], pw=[b_w["poolw"]])

    def mem_kv(c):
        for mt in range(2):
            s = nxt("xst", 2)
            DMA("sp", b_xst[s], xst[s][:, :], mems[c * NMEM + mt * 128: c * NMEM + (mt + 1) * 128, :], w=[b_xst[s]])
            rms_stats(xst[s][:, :], 60 + mt, [b_xst[s]], D)
            h = nxt("hb", 2)
            OP("dve", lambda e, s=s, h=h, mt=mt: e.tensor_scalar(
                out=hb[h][:, :], in0=xst[s][:, :], scalar1=rstd[:, 60 + mt:61 + mt], scalar2=None, op0=ALU.mult),
               r=[b_xst[s], b_stat[60 + mt]], w=[b_hb[h]])
            transposes_to(lambda mt=mt: hT[:, :, mt * 128:(mt + 1) * 128], hb[h], b_hb[h], [b_hT[0]])
        for g in range(2):
            pb = proj_group(lambda cc, g=g: w_mem_sb[:, cc, g * 128:(g + 1) * 128],
                            lambda cc: hT[:, cc, 0:NMEM], NMEM, [b_w["w_mem"], b_hT[0]])
            OP("dve", lambda e, pb=pb, g=g: e.tensor_copy(out=kmT[:, c, g, :], in_=bank(pb)[:, 0:NMEM]),
               r=[b_bank[pb]], w=[b_w["kmv"]])
        for mt in range(2):
            pb = proj_group(lambda cc, mt=mt: hT[:, cc, mt * 128:(mt + 1) * 128],
                            lambda cc: w_mem_sb[:, cc, 256:512], 256, [b_w["w_mem"], b_hT[0]])
            dst = vmaug[:, c * 8 + mt * 4: c * 8 + (mt + 1) * 4, 64:128]
            OP("dve", lambda e, pb=pb, dst=dst: e.tensor_copy(
                out=dst, in_=bank(pb)[:, 0:256].rearrange("p (h d) -> p h d", h=4)),
               r=[b_bank[pb]], w=[b_w["kmv"]])

    def x_src(l, c, tile):
        base = c * T + tile * 128
        if l == 0:
            return xs[base:base + 128, :], []
        return x1d.ap()[base:base + 128, :], [b_x1[c]]

    def pass1(l, c, do_kv=True):
        fence([b_yst[0]], b_hb)
        fence(b_sg + b_mix + b_qx + b_XSW[2:], b_ub + [b_ubh])
        fence([b_cs, bf("win"), b_w["w_mem"]] + p2_tmp(), b_XS1[2:])
        xslot = {}

        def xload(tile):
            s_ = nxt("xs1", 5)
            xslot[tile] = s_
            src, rd = x_src(l, c, tile)
            DMA("sp", b_XS1[s_], XS1[s_][:, :], src, r=rd, w=[b_XS1[s_]])
        for t_ in range(4):
            xload(t_)
        pending = []
        rot["n"] = 2
        for blk in range(4):
            tok0 = blk * 512
            for tt in range(4):
                tile = blk * 4 + tt
                if tile + 4 < 16:
                    xload(tile + 4)
                s = xslot[tile]
                si = (32 if c == 2 else 0) + tile
                if do_kv or c != 2:
                    rms_stats(XS1[s][:, :], si, [b_XS1[s]], D)
                h = nxt("hb", 2)
                OP("dve", lambda e, s=s, h=h, si=si: e.tensor_scalar(
                    out=hb[h][:, :], in0=XS1[s][:, :], scalar1=rstd[:, si:si + 1], scalar2=None, op0=ALU.mult),
                   r=[b_XS1[s], b_stat[si]], w=[b_hb[h]])
                transposes_to(lambda tile=tile: hT[:, :, tile * 128:(tile + 1) * 128], hb[h], b_hb[h], [b_hT[blk]])
                for _ in range(3):
                    if pending:
                        pending.pop(0)()
            for g in range(2):
                pb = proj_group(lambda cc, g=g: w_in_sb[:, cc, C_UPOOL + g * 128:C_UPOOL + (g + 1) * 128],
                                lambda cc: hT[:, cc, tok0:tok0 + 512], 512, [b_w["w_in"], b_hT[blk]])
                OP("act", lambda e, pb=pb, g=g: e.activation(out=ub[:, g, HALO + tok0:HALO + tok0 + 512],
                                                             in_=bank(pb), func=AF.Copy),
                   r=[b_bank[pb]], w=[b_ub[blk]])
            if not do_kv:
                continue
            while pending:
                pending.pop(0)()
            chains = []
            for kvh in range(2):
                pb = proj_group(lambda cc, kvh=kvh: w_krep[:, cc, kvh, :],
                                lambda cc: hT[:, cc, tok0:tok0 + 512], 512, [b_w["w_krep"], b_hT[blk]], pb=2 + kvh)
                chains.append(qk_norm_rope(pb, Pk, bf("Pk"), gk, tok0, kT2[:, kvh, tok0:tok0 + 512], [b_kT[blk]],
                                           nbs=(4 + 2 * kvh, 5 + 2 * kvh), staged=True, qs=kvh))
            for st_a, st_b in zip(chains[0], chains[1]):
                pending += [st_a, st_b]
            pb = nxt("proj", 4)

            def fv(e, tok0=tok0, pb=pb):
                i = None
                for tt in range(4):
                    for cc in range(8):
                        i = e.matmul(bank(pb)[:, tt * 128:(tt + 1) * 128],
                                     lhsT=hT[:, cc, tok0 + tt * 128: tok0 + (tt + 1) * 128],
                                     rhs=w_in_sb[:, cc, C_V:C_V + 128], start=(cc == 0), stop=(cc == 7))
                return i
            OP("pe", fv, r=[b_w["w_in"], b_hT[blk]], w=[b_bank[pb]])
            dst = vaug[:, blk * 8:(blk + 1) * 8, 64:128]
            OP("dve", lambda e, pb=pb, dst=dst: e.tensor_copy(
                out=dst, in_=bank(pb).rearrange("p (b d) -> p b d", b=8)),
               r=[b_bank[pb]], w=[b_v[blk]])
        while pending:
            pending.pop(0)()
        rot["n"] = 4

    def pool_stage(c, setidx):
        fence([b_w["w_mem"]] + p2_tmp() + b_XS1[2:], [b_cs, bf("win")])
        for ti in range(2):
            OP("dve", lambda e, ti=ti: e.tensor_tensor_scan(
                out=cs[:, :], data0=ub[:, ti, :], data1=ub[:, ti, :], initial=0.0, op0=ALU.add, op1=ALU.bypass),
               r=b_ub + [b_ubh], w=[b_cs])
            for hf in range(2):
                w = (2, 4, 8, 16)[ti * 2 + hf]
                lo = HALO - w // 2 - 1
                hi = HALO + w // 2 - 1
                ln = slice(hf * 64, (hf + 1) * 64)
                OP("dve", lambda e, ln=ln, lo=lo, hi=hi, ti=ti: e.tensor_tensor(
                    out=win[ln, :], in0=cs[ln, hi:hi + T], in1=cs[ln, lo:lo + T], op=ALU.subtract),
                   r=[b_cs], w=[bf("win")])
            OP("dve", lambda e, ti=ti: e.scalar_tensor_tensor(
                out=dT[:, ti, :], in0=win[:, :], scalar=invw[:, ti:ti + 1], in1=ub[:, ti, HALO:HALO + T],
                op0=ALU.mult, op1=ALU.subtract), r=[bf("win"), bf("invw")] + b_ub, w=[b_dT])
            for (e0, t0) in ((0, 0), (HALO, T - HALO)):
                OP("dve", lambda e, ti=ti, e0=e0, t0=t0: e.tensor_tensor(
                    out=etmp[:, ti, e0:e0 + HALO], in0=win[:, t0:t0 + HALO],
                    in1=ptab_sb[:, setidx, ti * 2 * HALO + e0: ti * 2 * HALO + e0 + HALO], op=ALU.mult),
                   r=[bf("win"), b_w["ptab"]], w=[b_w["etmp"]])
                OP("dve", lambda e, ti=ti, e0=e0, t0=t0: e.tensor_tensor(
                    out=dT[:, ti, t0:t0 + HALO], in0=etmp[:, ti, e0:e0 + HALO],
                    in1=ub[:, ti, HALO + t0:HALO + t0 + HALO], op=ALU.subtract),
                   r=[b_w["etmp"]] + b_ub, w=[b_dT])


    def attention_block(heads, hooks=None):
        seq = []
        for hi, hd in enumerate(heads):
            hd["ob"] = 4 + (hi % 2)
            for g in range(hd["nkt"] // 2):
                seq.append((hd, g))
        pt_of = {}

        def emit_S(n):
            hd, g = seq[n]
            sp = nxt("S", 2)
            k_fn, qap = hd["k_fn"], hd["qap"]

            def f(e):
                e.matmul(bank(2 * sp), lhsT=k_fn(2 * g), rhs=qap, start=True, stop=True)
                return e.matmul(bank(2 * sp + 1), lhsT=k_fn(2 * g + 1), rhs=qap, start=True, stop=True)
            OP("pe", f, r=hd["rd_q"] + hd["rd_k"](2 * g) + hd["rd_k"](2 * g + 1),
               w=[b_bank[2 * sp], b_bank[2 * sp + 1]])
            p = nxt("pt", 3)
            pt_of[n] = p
            OP("act", lambda e: e.activation(out=PTP[p], in_=bank(2 * sp, 2), func=AF.Exp, scale=0.125),
               r=[b_bank[2 * sp], b_bank[2 * sp + 1]], w=[b_PT[p]])

        def emit_PV(n):
            hd, g = seq[n]
            p = pt_of[n]
            ob, nkt, odd = hd["ob"], hd["nkt"], hd["odd"]

            def f(e):
                i = None
                for u in range(2):
                    j = 2 * g + u
                    if odd:
                        i = e.matmul(bank(ob), lhsT=hd["v_odd"](j), rhs=PTP[p][:, u * 512:(u + 1) * 512],
                                     start=(j == 0), stop=(j == nkt - 1))
                    else:
                        i = e.matmul(bank(ob), lhsT=hd["v_even"](j), rhs=PTP[p][:, u * 512:(u + 1) * 512],
                                     start=(j == 0), stop=(j == nkt - 1))
                return i
            OP("pe", f, r=[b_PT[p]] + hd["rd_v"](2 * g) + hd["rd_v"](2 * g + 1), w=[b_bank[ob]])

        def tail(hd):
            odd, ob, hidx = hd["odd"], hd["ob"], hd["hidx"]
            sl = 0 if odd else 64
            dl = slice(64, 128) if odd else slice(0, 64)
            nl = 128 if odd else 65
            rbi = nxt("sr", 2)
            OP("dve", lambda e: e.tensor_copy(out=Rb[rbi][0:nl, :], in_=bank(ob)[0:nl, :]), r=[b_bank[ob]],
               w=[b_R[rbi]])
            OP("dve", lambda e: e.tensor_tensor(out=hd["mix_ap"], in0=Rb[rbi][dl, :], in1=hd["sg_ap"], op=ALU.mult),
               r=[b_R[rbi], hd["sg_buf"]], w=[hd["mix_buf"]])
            DMA("sp", bf("srow_dma"), srow_all[hidx:hidx + 1, :], Rb[rbi][sl:sl + 1, :], r=[b_R[rbi]],
                pw=[bf("srow_all")])

        emit_S(0)
        if len(seq) > 1:
            emit_S(1)
        for n in range(len(seq)):
            if n + 2 < len(seq):
                emit_S(n + 2)
            emit_PV(n)
            hd, g = seq[n]
            if g == hd["nkt"] // 2 - 1:
                tail(hd)
            if hooks and n in hooks:
                for f in hooks[n]:
                    f()

    def normalize_block():
        OP("act", lambda e: e.activation(out=srow_all[0:12, :], in_=srow_all[0:12, :], func=AF.Ln),
           r=[bf("srow_all")], w=[bf("srow_all")])
        OP("act", lambda e: e.activation(out=srow_all[0:12, :], in_=srow_all[0:12, :], func=AF.Exp, scale=-1.0),
           r=[bf("srow_all")], w=[bf("srow_all")])
        OP("dve", lambda e: e.tensor_copy(out=rhl[0:12, 0, :], in_=srow_all[0:12, :]),
           r=[bf("srow_all")], w=[bf("rhl")])
        OP("dve", lambda e: e.tensor_tensor(out=rhl[0:12, 1, :], in0=srow_all[0:12, :], in1=rhl[0:12, 0, :],
                                            op=ALU.subtract), r=[bf("srow_all"), bf("rhl")], w=[bf("rhl")])
        for i in range(6):
            bb = 5 + (i % 2)

            def fb(e, i=i, bb=bb):
                e.matmul(bank(bb), lhsT=sel_bf[0:12, i, :], rhs=rhl[0:12, 0, :], start=True, stop=False)
                return e.matmul(bank(bb), lhsT=sel_bf[0:12, i, :], rhs=rhl[0:12, 1, :], start=False, stop=True)
            OP("pe", fb, r=[bf("rhl"), bf("sel")], w=[b_bank[bb]])
            OP("dve", lambda e, i=i, bb=bb: e.tensor_tensor(out=mixT_blk[:, 2 + i, :], in0=mixT_blk[:, 2 + i, :],
                                                           in1=bank(bb), op=ALU.mult),
               r=[b_mix[2 + i], b_bank[bb]], w=[b_mix[2 + i]])

    def pass2(l, c, nkt):
        fence([b_cs, bf("win")], p2_tmp())
        fence(b_ub + [b_ubh], b_sg + b_mix + b_qx)
        fence(b_hb, [b_yst[0]])
        OP("dve", lambda e: e.memset(qTm[:, :, :], 0.0), w=b_qT)
        gcols = [C_GPOOL, C_GPOOL + 128, C_GATTN, C_GATTN + 128, C_GATTN + 256, C_GATTN + 384, C_GX, C_GX + 128]
        for blk in range(4):
            tok0 = blk * 512
            hrd = [b_w["w_in"], b_hT[blk]]

            def proj_half(col0, pb, half, qb=None):
                qb = blk if qb is None else qb
                qt0 = qb * 512

                def f(e):
                    i_ = None
                    for cc in range(4 * half, 4 * half + 4):
                        i_ = e.matmul(bank(pb), lhsT=w_in_sb[:, cc, col0:col0 + 128], rhs=hT[:, cc, qt0:qt0 + 512],
                                      start=(cc == 0), stop=(cc == 7))
                    return i_
                OP("pe", f, r=[b_w["w_in"], b_hT[qb]], w=[b_bank[pb]])

            def q_chain(i, qb=None):
                qb = blk if qb is None else qb
                col0 = C_Q + i * 128
                st = qk_norm_rope(7, Pq, bf("Pq"), gq, qb * 512, (qTm[0:64, 2 * i, :], qTm[64:128, 2 * i + 1, :]),
                                  [b_qT[i]], nbs=(6, 7), staged=True)
                return [lambda: proj_half(col0, 7, 0, qb), lambda: proj_half(col0, 7, 1, qb)] + st

            def qx_chain(i):
                col0 = C_QX + i * 128

                def cp():
                    OP("act", lambda e: e.activation(out=qxT_blk[:, i, :], in_=bank(7), func=AF.Copy),
                       r=[b_bank[7]], w=[b_qx[i]])
                return [lambda: proj_half(col0, 7, 0), lambda: proj_half(col0, 7, 1), cp]

            def proj_gate(i, pb):
                proj_group(lambda cc: w_in_sb[:, cc, gcols[i]:gcols[i] + 128],
                           lambda cc: hT[:, cc, tok0:tok0 + 512], 512, hrd, pb=pb)
                OP("act", lambda e: e.activation(out=sg_blk[:, i, :], in_=bank(pb), func=AF.Silu),
                   r=[b_bank[pb]], w=[b_sg[i]])

            def pool_mix(ti):
                pb = 7
                OP("pe", lambda e: e.matmul(bank(pb), lhsT=poolw_sb[:, ti, :],
                                            rhs=dT[:, ti, tok0:tok0 + 512], start=True, stop=True),
                   r=[b_w["poolw"], b_dT], w=[b_bank[pb]])
                OP("dve", lambda e: e.scalar_tensor_tensor(
                    out=mixT_blk[:, ti, :], in0=bank(pb), scalar=pscale[:, ti:ti + 1], in1=sg_blk[:, ti, :],
                    op0=ALU.mult, op1=ALU.mult), r=[b_bank[pb], b_sg[ti], b_w["gains"]], w=[b_mix[ti]])

            def head_self(h):
                lane0 = (h % 2) * 64
                kvh = h // 4
                ln = slice(lane0, lane0 + 64)
                return dict(hidx=h, qap=qTm[:, h, :], odd=(lane0 == 64),
                            k_fn=lambda j: kT2[:, kvh, j * 128:(j + 1) * 128], nkt=nkt,
                            v_even=lambda j: vaug_f[:, (j * 2 + kvh) * VB + 64:(j * 2 + kvh) * VB + 192],
                            v_odd=lambda j: vaug[:, j * 2 + kvh, 0:128],
                            sg_ap=sg_blk[ln, 2 + h // 2, :], sg_buf=b_sg[2 + h // 2],
                            mix_ap=mixT_blk[ln, 2 + h // 2, :], mix_buf=b_mix[2 + h // 2],
                            rd_q=[b_qT[h // 2]], rd_k=lambda j: [b_kT[j // 4]], rd_v=lambda j: [b_v[j // 4]])

            def head_mem(hx):
                lane0 = (hx % 2) * 64
                ln = slice(lane0, lane0 + 64)
                return dict(hidx=8 + hx, qap=qxT_blk[ln, hx // 2, :], odd=(lane0 == 64),
                            k_fn=lambda j: kmT[ln, c, hx // 2, j * 128:(j + 1) * 128], nkt=2,
                            v_even=lambda j: vmaug_f[:, (c * 8 + j * 4 + hx) * VB + 64:(c * 8 + j * 4 + hx) * VB + 192],
                            v_odd=lambda j: vmaug[:, c * 8 + j * 4 + hx, 0:128],
                            sg_ap=sg_blk[ln, 6 + hx // 2, :], sg_buf=b_sg[6 + hx // 2],
                            mix_ap=mixT_blk[ln, 6 + hx // 2, :], mix_buf=b_mix[6 + hx // 2],
                            rd_q=[b_qx[hx // 2]], rd_k=lambda j: [b_w["kmv"]], rd_v=lambda j: [b_w["kmv"]])

            xpre = {}
            for tt in range(2):
                s_ = nxt("xst", 2)
                src, rd = x_src(l, c, blk * 4 + tt)
                DMA("sp", b_xst[s_], xst[s_][:, :], src, r=rd, w=[b_xst[s_]])
                xpre[tt] = s_
            for i in range(8):
                proj_gate(i, 5 + (i % 3))
            pool_mix(0)
            pool_mix(1)
            if blk == 0:
                for f in q_chain(0):
                    f()

            npair = nkt // 2
            hooks = {}

            def spread(stages, first):
                for k, f in enumerate(stages):
                    hooks.setdefault(first + k, []).append(f)
            spread(q_chain(1), 0)
            spread(qx_chain(0), npair)
            spread(q_chain(2), 2 * npair)
            spread(qx_chain(1), 3 * npair)
            spread(q_chain(3), 4 * npair)
            if blk < 3:
                spread(q_chain(0, blk + 1), 5 * npair + 1)
            hs = [head_self(h) for h in range(8)]
            hm = [head_mem(hx) for hx in range(4)]
            attention_block(hs[0:5] + [hm[0], hs[5], hm[1], hs[6], hm[2], hs[7], hm[3]], hooks)
            normalize_block()
            for tt in range(4):
                tile = blk * 4 + tt

                ob0 = (6, 2)[tt % 2]

                def fo(e, tt=tt, ob0=ob0):
                    i = None
                    for half in range(2):
                        for cc in range(8):
                            i = e.matmul(bank(ob0 + half), lhsT=mixT_blk[:, cc, tt * 128:(tt + 1) * 128],
                                         rhs=w_out_sb[:, cc, half * 512:(half + 1) * 512], start=(cc == 0),
                                         stop=(cc == 7))
                    return i
                OP("pe", fo, r=b_mix + [b_w["w_out"]], w=[b_bank[ob0], b_bank[ob0 + 1]])
                yps = bank(ob0, 2)
                sidx = 16 + tile
                rms_stats(yps, sidx, [b_bank[ob0], b_bank[ob0 + 1]], D)
                if tt in xpre:
                    s = xpre[tt]
                else:
                    s = nxt("xst", 2)
                    src, rd = x_src(l, c, tile)
                    DMA("sp", b_xst[s], xst[s][:, :], src, r=rd, w=[b_xst[s]])
                y = nxt("yst", 2)
                OP("dve", lambda e, y=y, sidx=sidx, yps=yps: e.scalar_tensor_tensor(
                    out=yst[y][:, :], in0=yps, scalar=rstd[:, sidx:sidx + 1], in1=gpost[:, :], op0=ALU.mult,
                    op1=ALU.mult), r=[b_bank[ob0], b_bank[ob0 + 1], b_stat[sidx], b_w["gains"]], w=[b_yst[y]])
                OP("dve", lambda e, y=y, s=s: e.tensor_tensor(out=yst[y][:, :], in0=yst[y][:, :], in1=xst[s][:, :],
                                                              op=ALU.add), r=[b_yst[y], b_xst[s]], w=[b_yst[y]])
                base = c * T + tile * 128
                if l == nlayers - 1:
                    DMA("sp", b_yst[y], yout[base:base + 128, :], yst[y][:, :], r=[b_yst[y]], pw=[bf("yout")])
                else:
                    DMA("sp", b_yst[y], x1d.ap()[base:base + 128, :], yst[y][:, :], r=[b_yst[y]], pw=[b_x1[c]])

    def load_rope(setidx):
        DMA("pool", b_w["rope"], ropeC[:, :], rope[setidx, 0], pw=[b_w["rope"]])
        DMA("pool", b_w["rope"], ropeS[:, :], rope[setidx, 1], pw=[b_w["rope"]])

    def zero_halos():
        fence(b_sg + b_mix + b_qx, b_ub + [b_ubh])
        OP("dve", lambda e: e.memset(ub[:, :, 0:HALO], 0.0), w=[b_ubh])
        OP("dve", lambda e: e.memset(ub[:, :, HALO + T:UBW], 0.0), w=[b_ubh])

    DMA("sp", b_w["ptab"], ptab_sb[:, :, :], ptab.rearrange("s p n -> p s n"), pw=[b_w["ptab"]])

    with nc.allow_low_precision(reason="bf16 matmul operands, fp32 accumulation"):
        for l in range(nlayers):
            load_layer(l)
            for c in range(NCH):
                mem_kv(c)
            load_rope(1)
            pass1(l, 2, do_kv=True)
            OP("dve", lambda e: e.tensor_copy(out=halo_st[:, :, 0:HALO], in_=ub[:, :, HALO:2 * HALO]),
               r=b_ub, w=[b_w["halo_st"]])
            OP("dve", lambda e: e.tensor_copy(out=halo_st[:, :, HALO:2 * HALO], in_=ub[:, :, T:T + HALO]),
               r=b_ub, w=[b_w["halo_st"]])
            xin = xin_d[l].ap()
            xout = xout_d[l].ap()
            with nc.allow_non_contiguous_dma(reason="exchange payload"):
                for kvh in range(2):
                    DMA("pool", b_xin[l], xin[kvh * 64:(kvh + 1) * 64, XK0:XK0 + T], kT2[0:64, kvh, 0:T],
                        r=b_kT[0:4], pw=[b_xin[l]])
                DMA("pool", b_xin[l], xin[:, XV0:XV0 + XVW], vaug[:, 0:32, :].rearrange("p b c -> p (b c)"), r=b_v[0:4],
                    pw=[b_xin[l]])
                DMA("pool", b_xin[l], xin[:, XH0:XW], halo_st[:, :, :].rearrange("p a b -> p (a b)"),
                    r=[b_w["halo_st"]], pw=[b_xin[l]])
            tr._collect("pool", [b_xin[l]], [b_xout[l]], True)
            E = tr.eng["pool"]
            inst = nc.gpsimd.collective_compute(
                "AllGather", ALU.bypass, replica_groups=[[2 * i, 2 * i + 1] for i in range(ncores // 2)],
                ins=[xin.opt()], outs=[xout.opt()])
            E["cnt"] += 1
            inst.then_inc(E["sem"], 1)
            tr._record(E["key"], (E["sem"], E["cnt"], None), [b_xin[l]], [b_xout[l]])
            for c in range(2):
                load_rope(0)
                zero_halos()
                pass1(l, c)
                pool_stage(c, 0)
                pass2(l, c, 16)
            load_rope(1)
            pass1(l, 2, do_kv=False)
            with nc.allow_non_contiguous_dma(reason="exchange payload"):
                for rnk in range(2):
                    for kvh in range(2):
                        for rep in range(2):
                            DMA("sp", bf("kvload"),
                                kT2[rep * 64:(rep + 1) * 64, kvh, rnk * T:(rnk + 1) * T],
                                xout[rnk * 128 + kvh * 64: rnk * 128 + (kvh + 1) * 64, XK0:XK0 + T],
                                r=[b_xout[l]], pw=b_kT)
                    DMA("sp", bf("kvload"), vaug[:, rnk * 32:(rnk + 1) * 32, :].rearrange("p b c -> p (b c)"),
                        xout[rnk * 128:(rnk + 1) * 128, XV0:XV0 + XVW], r=[b_xout[l]], pw=b_v)
                DMA("sp", b_w["halo_in"], halo_in[:, :, 0:2 * HALO],
                    xout[0:128, XH0:XW].rearrange("p (a b) -> p a b", a=2), r=[b_xout[l]], pw=[b_w["halo_in"]])
                DMA("sp", b_w["halo_in"], halo_in[:, :, 2 * HALO:4 * HALO],
                    xout[128:256, XH0:XW].rearrange("p (a b) -> p a b", a=2), r=[b_xout[l]], pw=[b_w["halo_in"]])
            OP("dve", lambda e: e.tensor_scalar(out=ub[:, :, 0:HALO], in0=halo_in[:, :, HALO:2 * HALO],
                                                scalar1=hmask_sb[:, 0:1], scalar2=None, op0=ALU.mult),
               r=[b_w["halo_in"], b_w["gains"]], w=[b_ubh])
            OP("dve", lambda e: e.tensor_scalar(out=ub[:, :, HALO + T:UBW], in0=halo_in[:, :, 2 * HALO:3 * HALO],
                                                scalar1=hmask_sb[:, 1:2], scalar2=None, op0=ALU.mult),
               r=[b_w["halo_in"], b_w["gains"]], w=[b_ubh])
            pool_stage(2, 1)
            pass2(l, 2, 32)
        tr.wait_all("sp", [bf("yout")] + b_yst)
    return nc, tr


def _rope_tables(pos):
    pos = np.asarray(pos, dtype=np.float64)
    row = np.floor(pos / 64.0)
    col = pos - 64.0 * row
    freqs = 10000.0 ** (-np.arange(16, dtype=np.float64) / 16.0)
    cosT = np.zeros((128, len(pos)), np.float32)
    sinT = np.zeros((128, len(pos)), np.float32)
    for lane in range(128):
        d = lane % 64
        axis, ab, p = d // 32, (d % 32) // 16, d % 16
        ang = (row if axis == 0 else col) * freqs[p]
        cosT[lane] = np.cos(ang)
        sinT[lane] = (-1.0 if ab == 0 else 1.0) * np.sin(ang)
    return cosT, sinT


def _pool_tab(t_glob, tseq):
    tab = np.zeros((128, 2, 2 * HALO), np.float32)
    for ti in range(2):
        for lane in range(128):
            w = (2, 4, 8, 16)[ti * 2 + lane // 64]
            for j, tg in enumerate(t_glob):
                lo = min(max(tg - w // 2, 0), tseq)
                hi = min(max(tg + w - w // 2, 0), tseq)
                tab[lane, ti, j] = 1.0 / float(hi - lo)
    return tab


def _consts():
    ident = np.eye(128, dtype=np.float32)
    swap = np.zeros((128, 128), np.float32)
    for m in range(128):
        swap[m ^ 16, m] = 1.0
    bones = np.zeros((128, 128), np.float32)
    bones[:64, :64] = 1.0 / 64.0
    bones[64:, 64:] = 1.0 / 64.0
    return np.stack([ident, swap, bones], 0)


def _selm():
    m = np.zeros((12, 6, 128), np.float32)
    for h in range(12):
        m[h, h // 2, (h % 2) * 64:(h % 2 + 1) * 64] = 1.0
    return m.reshape(12, 768)


_PROG = {}


def kernel(x_prompt, x_sample, mem_prompt, mem_sample, norm_pre, norm_post, w_in, pool_w, pool_scale,
           q_norm, k_norm, mem_norm, w_mem_kv, w_out):
    f32 = lambda a: np.ascontiguousarray(np.asarray(a, dtype=np.float32))
    x_prompt, x_sample, mem_prompt, mem_sample = map(f32, (x_prompt, x_sample, mem_prompt, mem_sample))
    if "nc" not in _PROG:
        _PROG["nc"], _PROG["tr"] = build_program()
    nc = _PROG["nc"]
    in_maps = _prep_inputs(x_prompt, x_sample, mem_prompt, mem_sample, norm_pre, norm_post, w_in, pool_w, pool_scale,
                           q_norm, k_norm, mem_norm, w_mem_kv, w_out)
    res = run_bass_kernel_spmd(nc, in_maps, core_ids=list(range(NCORES)))
    y_prompt = np.empty_like(x_prompt)
    y_sample = np.empty_like(x_sample)
    for c in range(NCORES):
        y = res.results[c]["y"]
        y_prompt[2 * c] = y[0:T]
        y_prompt[2 * c + 1] = y[T:2 * T]
        y_sample[c // 2, (c % 2) * T:(c % 2 + 1) * T] = y[2 * T:3 * T]
    return (y_prompt, y_sample)


def _prep_inputs(x_prompt, x_sample, mem_prompt, mem_sample, norm_pre, norm_post, w_in, pool_w, pool_scale,
                 q_norm, k_norm, mem_norm, w_mem_kv, w_out, ncores=NCORES):
    f32 = lambda a: np.ascontiguousarray(np.asarray(a, dtype=np.float32))
    cm = _consts()
    shared = {
        "w_in": f32(w_in), "w_out": f32(w_out), "w_mem": f32(w_mem_kv),
        "pool_w": f32(pool_w).reshape(2, 256, 64), "norm_pre": f32(norm_pre), "norm_post": f32(norm_post),
        "mem_norm": f32(mem_norm), "pool_scale": f32(pool_scale), "q_norm": f32(q_norm), "k_norm": f32(k_norm),
        "cmat": cm, "selm": _selm(),
    }
    cp, sp_ = _rope_tables(np.arange(T))
    tab_p = _pool_tab(list(range(HALO)) + list(range(T - HALO, T)), T)
    in_maps = []
    for c in range(ncores):
        sq, half = c // 2, c % 2
        t0 = half * T
        xs = np.concatenate([x_prompt[2 * c], x_prompt[2 * c + 1], x_sample[sq, t0:t0 + T]], 0)
        mm = np.concatenate([mem_prompt[2 * c], mem_prompt[2 * c + 1], mem_sample[sq]], 0)
        cs_, ss_ = _rope_tables(np.arange(t0, t0 + T))
        rope = np.stack([np.stack([cp, sp_], 0), np.stack([cs_, ss_], 0)], 0)
        tab_s = _pool_tab(list(range(t0, t0 + HALO)) + list(range(t0 + T - HALO, t0 + T)), 2 * T)
        ptab = np.stack([tab_p.reshape(128, -1), tab_s.reshape(128, -1)], 0)
        hm = np.zeros((128, 2), np.float32)
        hm[:, 0] = 1.0 if half == 1 else 0.0
        hm[:, 1] = 1.0 if half == 0 else 0.0
        m = dict(shared)
        m.update({"xs": np.ascontiguousarray(xs), "mems": np.ascontiguousarray(mm),
                  "rope": np.ascontiguousarray(rope.astype(np.float32)), "ptab": np.ascontiguousarray(ptab),
                  "hmask": hm})
        in_maps.append(m)
    return in_maps
```

```python
import numpy as np
import ml_dtypes
import concourse.bass as bass
import concourse.mybir as mybir
from concourse.bass_utils import run_bass_kernel_spmd

F32 = mybir.dt.float32
BF16 = mybir.dt.bfloat16
AF = mybir.ActivationFunctionType
ALU = mybir.AluOpType

NCORES = 8
D = 1024
T = 2048
NCH = 3
NMEM = 256
INW = 2304
EPS = 1e-6
HALO = 16
UBW = T + 2 * HALO
VB = 129
XW = 2 * T // 2 + 0

C_UPOOL, C_GPOOL, C_Q, C_K, C_V, C_GATTN, C_QX, C_GX = 0, 256, 512, 1024, 1152, 1280, 1792, 2048

XK0 = 0
XV0 = T
XVW = 16 * 2 * VB
XH0 = XV0 + XVW
XW = XH0 + 2 * 2 * HALO


class Buf:
    __slots__ = ("name", "w", "r", "sem", "semcnt", "excl")

    def __init__(self, name, excl=False):
        self.name = name
        self.excl = excl
        self.w = {}
        self.r = {}
        self.sem = None
        self.semcnt = 0


class Tracker:
    def __init__(self, nc):
        self.nc = nc
        self.eng = {}
        for n, e in (("pe", nc.tensor), ("act", nc.scalar), ("dve", nc.vector),
                     ("pool", nc.gpsimd), ("sp", nc.sync)):
            sem = nc.alloc_semaphore(name=f"sem_{n}")
            self.eng[n] = {"e": e, "sem": sem, "key": f"sem_{n}", "cnt": 0, "seen": {}}
        self.nwaits = 0
        self.nops = 0

    def _collect(self, en, reads, writes, is_dma, pwrites=(), pkey=None):
        E = self.eng[en]
        need = {}

        def add(key, ev, raw):
            h, val, owner = ev
            if owner is not None:
                val = owner.semcnt
            if not is_dma and key == E["key"] and en == "pe":
                return
            cur = need.get(key)
            if cur is None or cur[1] < val:
                need[key] = (h, val)

        for b in reads:
            for key, ev in b.w.items():
                add(key, ev, True)
        for b in writes:
            for key, ev in b.w.items():
                add(key, ev, False)
            for key, ev in b.r.items():
                add(key, ev, False)
        for b in pwrites:
            for key, ev in b.r.items():
                add(key, ev, False)
            for key, ev in b.w.items():
                if key != pkey:
                    add(key, ev, False)
        for key, (h, val) in need.items():
            if E["seen"].get(key, 0) < val:
                E["e"].wait_ge(h, val)
                E["seen"][key] = val
                self.nwaits += 1

    def _record(self, key, ev, reads, writes, pwrites=()):
        for b in reads:
            b.r[key] = ev
        for b in writes:
            b.w = {key: ev}
            b.r = {}
        for b in pwrites:
            b.w[key] = ev
            b.r = {}

    def op(self, en, fn, r=(), w=()):
        E = self.eng[en]
        if any(b.excl for b in r):
            w = list(w) + [b for b in r if b.excl]
            r = [b for b in r if not b.excl]
        self._collect(en, r, w, False)
        inst = fn(E["e"])
        E["cnt"] += 1
        inst.then_inc(E["sem"], 1)
        self._record(E["key"], (E["sem"], E["cnt"], None), r, w)
        self.nops += 1

    def dma(self, q, owner, out, in_, r=(), w=(), pw=(), **kw):
        E = self.eng[q]
        if owner.sem is None:
            owner.sem = self.nc.alloc_semaphore(name=f"dsem_{owner.name}")
        self._collect(q, r, w, True, pw, f"dsem_{owner.name}")
        inst = E["e"].dma_start(out=out, in_=in_, **kw)
        owner.semcnt += 16
        inst.then_inc(owner.sem, 16)
        self._record(f"dsem_{owner.name}", (owner.sem, owner.semcnt, owner), r, w, pw)
        self.nops += 1

    def wait_all(self, en, bufs):
        self._collect(en, [], bufs, True)


def build_program(nlayers=2, ncores=NCORES):
    nc = bass.Bass("TRN2", target_bir_lowering=False)
    tr = Tracker(nc)

    def dram_in(name, shape, dt=F32):
        return nc.dram_tensor(name, list(shape), dt, kind="ExternalInput").ap()

    xs = dram_in("xs", [NCH * T, D])
    mems = dram_in("mems", [NCH * NMEM, D])
    w_in = dram_in("w_in", [2, D, INW])
    w_out = dram_in("w_out", [2, D, D])
    w_mem = dram_in("w_mem", [2, D, 512])
    pool_w = dram_in("pool_w", [2, 256, 64])
    norm_pre = dram_in("norm_pre", [2, D])
    norm_post = dram_in("norm_post", [2, D])
    mem_norm = dram_in("mem_norm", [2, D])
    pool_scale = dram_in("pool_scale", [2, 256])
    q_norm = dram_in("q_norm", [2, 64])
    k_norm = dram_in("k_norm", [2, 64])
    rope = dram_in("rope", [2, 2, 128, T])
    ptab = dram_in("ptab", [2, 128, 2 * 2 * HALO])
    hmask = dram_in("hmask", [128, 2])
    cmat = dram_in("cmat", [3, 128, 128])
    selm = dram_in("selm", [12, 6 * 128])
    yout = nc.dram_tensor("y", [NCH * T, D], F32, kind="ExternalOutput").ap()
    x1d = nc.dram_tensor("x1_scratch", [NCH * T, D], F32)
    xin_d = [nc.dram_tensor(f"xin{l}", [128, XW], BF16) for l in range(2)]
    xout_d = [nc.dram_tensor(f"xout{l}", [256, XW], BF16) for l in range(2)]
    b_x1 = [Buf(f"x1_{c}") for c in range(NCH)]
    b_xin = [Buf(f"xin{l}") for l in range(2)]
    b_xout = [Buf(f"xout{l}") for l in range(2)]

    def sb(name, shape, dt):
        return nc.alloc_sbuf_tensor(name, list(shape), dt)

    w_in_sb = sb("w_in_sb", [128, 8, INW], BF16)
    w_out_sb = sb("w_out_sb", [128, 8, D], BF16)
    w_krep = sb("w_krep", [128, 8, 2, 128], BF16)
    poolw_sb = sb("poolw_sb", [128, 2, 128], BF16)
    hT = sb("hT", [128, 8, T], BF16)
    def view(region, byte_off, shape, dt):
        esz = 4 if dt == F32 else 2
        n = 1
        for d_ in shape[1:]:
            n *= d_
        a = region[:, byte_off // 2: byte_off // 2 + n * esz // 2]
        if dt == F32:
            a = a.bitcast(F32)
        if len(shape) == 3:
            a = a.rearrange("p (a b) -> p a b", a=shape[1])
        return a

    regX = sb("regX", [128, 18432 // 2], BF16)
    regY = sb("regY", [128, 16512 // 2], BF16)
    regZ = sb("regZ", [128, 4096 // 2], BF16)
    ub = view(regX, 0, [128, 2, UBW], F32)
    sg_blk = view(regX, 0, [128, 8, 512], BF16)
    mixT_blk = view(regX, 8192, [128, 8, 512], BF16)
    qxT_blk = view(regX, 16384, [128, 2, 512], BF16)
    w_mem_sb = view(regY, 0, [128, 8, 512], BF16)
    cs = view(regY, 0, [128, UBW], F32)
    win = view(regY, 8320, [128, T], F32)
    qTm = view(regY, 0, [128, 8, 512], BF16)
    kT2 = sb("kT2", [128, 2, 2 * T], BF16)
    vaug_f = sb("vaug", [128, 64 * VB + 64], BF16)
    vaug = vaug_f[:, 0:64 * VB].rearrange("p (b c) -> p b c", c=VB)
    xst = [sb(f"xst{i}", [128, D], F32) for i in range(2)]
    yst = [view(regZ, 0, [128, D], F32)] * 2
    NPT = 3
    PTP = [view(regY, 8192, [128, 1024], BF16), view(regY, 10240, [128, 1024], BF16),
           view(regY, 14336, [128, 1024], BF16)]
    ropeC = sb("ropeC", [128, T], BF16)
    ropeS = sb("ropeS", [128, T], BF16)
    dT = sb("dT", [128, 2, T], BF16)
    kmT = sb("kmT", [128, NCH, 2, NMEM], BF16)
    vmaug_f = sb("vmaug", [128, NCH * 8 * VB + 64], BF16)
    vmaug = vmaug_f[:, 0:NCH * 8 * VB].rearrange("p (b c) -> p b c", c=VB)
    hb = [view(regZ, i * 2048, [128, D], BF16) for i in range(2)]
    ident_bf = sb("ident_bf", [128, 128], BF16)
    bones_bf = sb("bones_bf", [128, 128], BF16)
    swapP = sb("swapP", [128, 128], F32)
    Pq = sb("Pq", [128, 128], BF16)
    Pk = sb("Pk", [128, 128], BF16)
    ones_bf = sb("ones_bf", [128, 64], BF16)
    gq = sb("gq", [128, 1], F32)
    gk = sb("gk", [128, 1], F32)
    gpre = sb("gpre", [128, 8], F32)
    gmem = sb("gmem", [128, 8], F32)
    gpost = sb("gpost", [128, D], F32)
    pscale = sb("pscale", [128, 2], F32)
    invw = sb("invw", [128, 2], F32)
    hmask_sb = sb("hmask_sb", [128, 2], F32)
    ptab_sb = sb("ptab_sb", [128, 2, 2 * 2 * HALO], F32)
    halo_st = sb("halo_st", [128, 2, 2 * HALO], BF16)
    halo_in = sb("halo_in", [128, 2, 2 * 2 * HALO], BF16)
    etmp = sb("etmp", [128, 2, 2 * HALO], F32)
    ss = sb("ss", [128, 64], F32)
    lnv = sb("lnv", [128, 64], F32)
    rstd = sb("rstd", [128, 64], F32)
    NQS = 1
    sqb = [sb(f"sqb{i}", [128, 512], BF16) for i in range(NQS)]
    zsb = [sb(f"zsb{i}", [128, 512], BF16) for i in range(NQS)]
    t1b = [sb(f"t1b{i}", [128, 512], F32) for i in range(NQS)]
    sqb.append(view(regY, 12288, [128, 512], BF16))
    zsb.append(view(regY, 13312, [128, 512], BF16))
    t1b.append(view(regY, 14336, [128, 512], F32))
    srow_all = view(regY, 12288, [128, 512], F32)
    rhl = view(regY, 14336, [128, 2, 512], BF16)
    sel_bf = sb("sel_bf", [12, 6, 128], BF16)
    Rb0 = sb("Rb0", [128, 512], F32)
    junk = Rb0[:, :].bitcast(BF16)
    Rb = [Rb0, t1b[0]]

    ps_all = nc.alloc_psum_tensor("ps_all", [128, 8 * 512], F32)

    def bank(b, n=1):
        return ps_all[:, b * 512:(b + n) * 512]

    B = {}

    def bf(name):
        if name not in B:
            B[name] = Buf(name)
        return B[name]

    b_bank = [bf(f"bank{i}") for i in range(8)]
    for b_ in b_bank:
        b_.excl = True
    b_xst = [bf(f"xst{i}") for i in range(2)]
    b_yst = [bf("yst0")] * 2
    b_hb = [bf(f"hb{i}") for i in range(2)]
    b_PT = [bf("PT0"), bf("PT1"), bf("rhl")]
    b_hT = [bf(f"hT{i}") for i in range(4)]
    b_kT = [bf(f"kT{i}") for i in range(8)]
    b_v = [bf(f"v{i}") for i in range(8)]
    b_ub = [bf(f"ub{i}") for i in range(4)]
    b_ubh = bf("ubh")
    b_dT = bf("dT")
    b_cs = bf("cs")
    b_qT = [bf(f"qT{i}") for i in range(4)]
    b_qz = bf("qzero")
    b_qx = [bf(f"qx{i}") for i in range(2)]
    b_sg = [bf(f"sg{i}") for i in range(8)]
    b_mix = [bf(f"mix{i}") for i in range(8)]
    b_w = {n: bf(n) for n in ("w_in", "w_out", "w_mem", "w_krep", "poolw", "consts", "gains",
                              "rope", "ptab", "kmv", "stat", "halo_st", "halo_in", "etmp", "junk")}
    b_stat = [bf(f"stat{i}") for i in range(64)]
    b_sq = [bf(f"sq{i}") for i in range(NQS)]
    b_zs = [bf(f"zs{i}") for i in range(NQS)]
    b_t1 = [bf(f"t1{i}") for i in range(NQS)]
    b_sq.append(bf("sq_1"))
    b_zs.append(bf("zs_1"))
    b_t1.append(bf("t1_1"))
    b_gt = []
    b_srow = [bf("srow_all"), bf("rhl")]
    b_R = [bf("R0"), b_t1[0]]
    b_tb = []

    OP = tr.op
    DMA = tr.dma
    XS1 = [xst[0], xst[1]] + [view(regY, i * 4096, [128, D], F32) for i in range(3)]
    b_XS1 = [b_xst[0], b_xst[1]] + [bf(f"xsY{i}") for i in range(3)] + [b_sq[1], b_zs[1], b_t1[1]]
    XSW = [xst[0], xst[1]] + [view(regX, i * 4096, [128, D], F32) for i in range(4)]
    b_XSW = [b_xst[0], b_xst[1]] + [bf(f"xsX{i}") for i in range(4)]

    def fence(A, Bs):
        for a in A:
            for src in (a.w, a.r):
                for key, ev in src.items():
                    for b in Bs:
                        cur = b.r.get(key)
                        if cur is None or cur[1] < ev[1]:
                            b.r[key] = ev

    def p2_tmp():
        return b_qT + b_PT + b_gt + b_srow + b_tb
    rr = {"proj": 0, "nrm": 0, "qs": 0, "gt": 0, "pt": 0, "S": 0, "O": 0, "xst": 0, "yst": 0, "hb": 0,
          "sr": 0, "xs1": 0, "xsw": 0}

    def nxt(k, n):
        if k == "proj":
            n = rot["n"]
        v = rr[k] % n
        rr[k] = (v + 1) % n
        return v
    rot = {"n": 4}

    cst = xst[0][:, 0:384].rearrange("p (k m) -> p k m", k=3)
    with nc.allow_non_contiguous_dma(reason="small constant / gain loads"):
        DMA("sp", b_xst[0], cst, cmat.rearrange("k p m -> p k m"), w=[b_xst[0]])
        DMA("sp", b_w["ptab"], hmask_sb[:, :], hmask, pw=[b_w["gains"]])
    OP("dve", lambda e: e.tensor_copy(out=ident_bf[:, :], in_=cst[:, 0, :]), r=[b_xst[0]], w=[bf("ident")])
    OP("dve", lambda e: e.tensor_copy(out=swapP[:, :], in_=cst[:, 1, :]), r=[b_xst[0]], w=[bf("swapP")])
    OP("dve", lambda e: e.tensor_copy(out=bones_bf[:, :], in_=cst[:, 2, :]), r=[b_xst[0]], w=[bf("bones")])
    OP("dve", lambda e: e.memset(ones_bf[:, :], 1.0), w=[bf("ones")])
    DMA("pool", bf("sel"), sel_bf[:, :, :].rearrange("p a b -> p (a b)"), selm, w=[bf("sel")])
    OP("dve", lambda e: e.memset(invw[0:64, 0:1], 0.5), w=[bf("invw")])
    OP("dve", lambda e: e.memset(invw[64:128, 0:1], 0.25), w=[bf("invw")])
    OP("dve", lambda e: e.memset(invw[0:64, 1:2], 0.125), w=[bf("invw")])
    OP("dve", lambda e: e.memset(invw[64:128, 1:2], 0.0625), w=[bf("invw")])
    OP("dve", lambda e: e.memset(vaug_f[:, :], 0.0), w=b_v)
    OP("dve", lambda e: e.memset(vaug[:, :, 0:1], 1.0), w=b_v)
    OP("dve", lambda e: e.memset(vaug[:, :, 128:129], 1.0), w=b_v)
    OP("dve", lambda e: e.memset(vmaug_f[:, :], 0.0), w=[b_w["kmv"]])
    OP("dve", lambda e: e.memset(vmaug[:, :, 0:1], 1.0), w=[b_w["kmv"]])
    OP("dve", lambda e: e.memset(vmaug[:, :, 128:129], 1.0), w=[b_w["kmv"]])

    def rms_stats(src_ap, idx, rd, nfree, extra_w=()):
        OP("act", lambda e: e.activation(out=junk[:, 0:nfree], in_=src_ap, func=AF.Square,
                                         accum_out=ss[:, idx:idx + 1]),
           r=rd, w=[b_R[0], b_stat[idx]])
        OP("act", lambda e: e.activation(out=lnv[:, idx:idx + 1], in_=ss[:, idx:idx + 1], func=AF.Ln,
                                         scale=1.0 / nfree, bias=eps_t[:, 0:1]),
           r=[b_stat[idx], bf("eps")], w=[b_stat[idx]])
        OP("act", lambda e: e.activation(out=rstd[:, idx:idx + 1], in_=lnv[:, idx:idx + 1], func=AF.Exp,
                                         scale=-0.5),
           r=[b_stat[idx]], w=[b_stat[idx]])

    eps_t = sb("eps_t", [128, 1], F32)
    OP("dve", lambda e: e.memset(eps_t[:, :], EPS), w=[bf("eps")])

    def transposes_to(dst_ap_fn, src_tile, src_buf, dst_bufs, nchunk=8):
        pb = nxt("proj", 4)
        psb = bank(pb).bitcast(BF16)

        def f(e):
            i = None
            for c in range(nchunk):
                i = e.transpose(out=psb[:, c * 128:(c + 1) * 128], in_=src_tile[:, c * 128:(c + 1) * 128],
                                identity=ident_bf[:, :])
            return i
        OP("pe", f, r=[src_buf, bf("ident")], w=[b_bank[pb]])
        OP("dve", lambda e: e.tensor_copy(out=dst_ap_fn(),
                                          in_=psb[:, 0:nchunk * 128].rearrange("p (c t) -> p c t", c=nchunk)),
           r=[b_bank[pb]], w=dst_bufs)

    def proj_group(lhs_fn, rhs_fn, n, rd, nk=8, pb=None):
        if pb is None:
            pb = nxt("proj", 4)

        def f(e):
            i = None
            for c in range(nk):
                i = e.matmul(bank(pb)[:, 0:n], lhsT=lhs_fn(c), rhs=rhs_fn(c), start=(c == 0), stop=(c == nk - 1))
            return i
        OP("pe", f, r=rd, w=[b_bank[pb]])
        return pb

    def qk_norm_rope(pb, Pmat, Pbuf, gvec, tok0, dst_ap, dst_bufs, nbs=None, staged=False, qs=0):
        s = qs
        if nbs is None:
            nb = 4 + 2 * nxt("nrm", 2)
            nbz = nb + 1
        else:
            nb, nbz = nbs
        z = bank(pb)

        def st_act1():
            OP("act", lambda e: e.activation(out=sqb[s][:, :], in_=z, func=AF.Square), r=[b_bank[pb]], w=[b_sq[s]])
            OP("act", lambda e: e.activation(out=zsb[s][:, :], in_=z, func=AF.Copy), r=[b_bank[pb]], w=[b_zs[s]])

        def st_dve0():
            OP("dve", lambda e: e.scalar_tensor_tensor(out=t1b[s][:, :], in0=z, scalar=gvec[:, 0:1],
                                                       in1=ropeC[:, tok0:tok0 + 512], op0=ALU.mult, op1=ALU.mult),
               r=[b_bank[pb], b_w["rope"], b_w["gains"]], w=[b_t1[s]])

        def st_pe():
            OP("pe", lambda e: e.matmul(bank(nb), lhsT=bones_bf[:, :], rhs=sqb[s][:, :], start=True, stop=True),
               r=[b_sq[s], bf("bones")], w=[b_bank[nb]])
            OP("pe", lambda e: e.matmul(bank(nbz), lhsT=Pmat[:, :], rhs=zsb[s][:, :], start=True, stop=True),
               r=[b_zs[s], Pbuf], w=[b_bank[nbz]])

        def st_act2():
            OP("act", lambda e: e.activation(out=bank(nb), in_=bank(nb), func=AF.Ln, bias=eps_t[:, 0:1]),
               r=[b_bank[nb], bf("eps")], w=[b_bank[nb]])
            OP("act", lambda e: e.activation(out=bank(nb), in_=bank(nb), func=AF.Exp, scale=-0.5),
               r=[b_bank[nb]], w=[b_bank[nb]])

        def st_dve1():
            OP("dve", lambda e: e.tensor_tensor(out=bank(nbz), in0=bank(nbz), in1=ropeS[:, tok0:tok0 + 512],
                                                op=ALU.mult),
               r=[b_bank[nbz], b_w["rope"]], w=[b_bank[nbz]])
            OP("dve", lambda e: e.tensor_tensor(out=t1b[s][:, :], in0=t1b[s][:, :], in1=bank(nbz), op=ALU.add),
               r=[b_t1[s], b_bank[nbz]], w=[b_t1[s]])

        def st_dve2():
            if isinstance(dst_ap, tuple):
                for hf, dap in enumerate(dst_ap):
                    ln_ = slice(hf * 64, (hf + 1) * 64)
                    OP("dve", lambda e, ln_=ln_, dap=dap: e.tensor_tensor(out=dap, in0=t1b[s][ln_, :],
                                                                         in1=bank(nb)[ln_, :], op=ALU.mult),
                       r=[b_t1[s], b_bank[nb]], w=dst_bufs)
            else:
                OP("dve", lambda e: e.tensor_tensor(out=dst_ap, in0=t1b[s][:, :], in1=bank(nb), op=ALU.mult),
                   r=[b_t1[s], b_bank[nb]], w=dst_bufs)
        stages = [st_act1, st_dve0, st_pe, st_act2, st_dve1, st_dve2]
        if staged:
            return stages
        for f in stages:
            f()

    def load_layer(l):
        fence([b_cs, bf("win")] + p2_tmp() + b_XS1[2:], [b_w["w_mem"]])
        fence([b_yst[0]], b_hb)
        with nc.allow_non_contiguous_dma(reason="gain vectors"):
            DMA("sp", b_w["gains"], gpre[:, :], norm_pre[l].rearrange("(c p) -> p c", p=128), pw=[b_w["gains"]])
            DMA("sp", b_w["gains"], gmem[:, :], mem_norm[l].rearrange("(c p) -> p c", p=128), pw=[b_w["gains"]])
            DMA("sp", b_w["gains"], pscale[:, :], pool_scale[l].rearrange("(c p) -> p c", p=128),
                pw=[b_w["gains"]])
            for hh in range(2):
                DMA("sp", b_w["gains"], gq[hh * 64:(hh + 1) * 64, :], q_norm[l].rearrange("(p o) -> p o", o=1),
                    pw=[b_w["gains"]])
                DMA("sp", b_w["gains"], gk[hh * 64:(hh + 1) * 64, :], k_norm[l].rearrange("(p o) -> p o", o=1),
                    pw=[b_w["gains"]])
            DMA("sp", b_w["gains"], gpost[:, :], norm_post[l:l + 1, :].to_broadcast([128, D]), pw=[b_w["gains"]])
        OP("dve", lambda e: e.tensor_scalar(out=Pq[:, :], in0=swapP[:, :], scalar1=gq[:, 0:1], scalar2=None,
                                            op0=ALU.mult), r=[bf("swapP"), b_w["gains"]], w=[bf("Pq")])
        OP("dve", lambda e: e.tensor_scalar(out=Pk[:, :], in0=swapP[:, :], scalar1=gk[:, 0:1], scalar2=None,
                                            op0=ALU.mult), r=[bf("swapP"), b_w["gains"]], w=[bf("Pk")])
        fence(b_ub + [b_ubh] + b_sg + b_mix + b_qx, b_XSW[2:])
        jobs = []
        for c in range(8):
            for (o, n) in [(0, 1024), (1024, 1024), (2048, 256)]:
                jobs.append(("in", c, o, n))
        for c in range(8):
            jobs.append(("mem", c, 0, 512))
        slot_of = {}

        def issue(k):
            kind, c, o, n = jobs[k]
            s_ = nxt("xsw", 6)
            slot_of[k] = s_
            src = w_in[l, c * 128:(c + 1) * 128, o:o + n] if kind == "in" else w_mem[l, c * 128:(c + 1) * 128, :]
            DMA("sp", b_XSW[s_], XSW[s_][:, 0:n], src, w=[b_XSW[s_]])
        for k in range(min(5, len(jobs))):
            issue(k)
        for k, (kind, c, o, n) in enumerate(jobs):
            if k + 5 < len(jobs):
                issue(k + 5)
            s_ = slot_of[k]
            if kind == "in":
                OP("dve", lambda e, s_=s_, c=c, o=o, n=n: e.tensor_scalar(
                    out=w_in_sb[:, c, o:o + n], in0=XSW[s_][:, 0:n], scalar1=gpre[:, c:c + 1], scalar2=None,
                    op0=ALU.mult), r=[b_XSW[s_], b_w["gains"]], w=[b_w["w_in"]])
                if o == 1024:
                    for kvh in range(2):
                        for rep in range(2):
                            OP("dve", lambda e, c=c, kvh=kvh, rep=rep: e.tensor_copy(
                                out=w_krep[:, c, kvh, rep * 64:(rep + 1) * 64],
                                in_=w_in_sb[:, c, C_K + kvh * 64:C_K + (kvh + 1) * 64]),
                               r=[b_w["w_in"]], w=[b_w["w_krep"]])
            else:
                OP("dve", lambda e, s_=s_, c=c: e.tensor_scalar(
                    out=w_mem_sb[:, c, :], in0=XSW[s_][:, 0:512], scalar1=gmem[:, c:c + 1], scalar2=None,
                    op0=ALU.mult), r=[b_XSW[s_], b_w["gains"]], w=[b_w["w_mem"]])
        for c in range(8):
            DMA("pool", b_w["w_out"], w_out_sb[:, c, :], w_out[l, c * 128:(c + 1) * 128, :], pw=[b_w["w_out"]])
        OP("dve", lambda e: e.memset(poolw_sb[:, :, :], 0.0), w=[b_w["poolw"]])
        for g in range(4):
            ti, hf = g // 2, g % 2
            DMA("pool", b_w["poolw"], poolw_sb[hf * 64:(hf + 1) * 64, ti, hf * 64:(hf + 1) * 64],
                pool_w[l, g * 64:(g + 1) * 64, :], pw=[b_w["poolw"]])

    def mem_kv(c):
        for mt in range(2):
            s = nxt("xst", 2)
            DMA("sp", b_xst[s], xst[s][:, :], mems[c * NMEM + mt * 128: c * NMEM + (mt + 1) * 128, :], w=[b_xst[s]])
            rms_stats(xst[s][:, :], 60 + mt, [b_xst[s]], D)
            h = nxt("hb", 2)
            OP("dve", lambda e, s=s, h=h, mt=mt: e.tensor_scalar(
                out=hb[h][:, :], in0=xst[s][:, :], scalar1=rstd[:, 60 + mt:61 + mt], scalar2=None, op0=ALU.mult),
               r=[b_xst[s], b_stat[60 + mt]], w=[b_hb[h]])
            transposes_to(lambda mt=mt: hT[:, :, mt * 128:(mt + 1) * 128], hb[h], b_hb[h], [b_hT[0]])
        for g in range(2):
            pb = proj_group(lambda cc, g=g: w_mem_sb[:, cc, g * 128:(g + 1) * 128],
                            lambda cc: hT[:, cc, 0:NMEM], NMEM, [b_w["w_mem"], b_hT[0]])
            OP("dve", lambda e, pb=pb, g=g: e.tensor_copy(out=kmT[:, c, g, :], in_=bank(pb)[:, 0:NMEM]),
               r=[b_bank[pb]], w=[b_w["kmv"]])
        for mt in range(2):
            pb = proj_group(lambda cc, mt=mt: hT[:, cc, mt * 128:(mt + 1) * 128],
                            lambda cc: w_mem_sb[:, cc, 256:512], 256, [b_w["w_mem"], b_hT[0]])
            dst = vmaug[:, c * 8 + mt * 4: c * 8 + (mt + 1) * 4, 64:128]
            OP("dve", lambda e, pb=pb, dst=dst: e.tensor_copy(
                out=dst, in_=bank(pb)[:, 0:256].rearrange("p (h d) -> p h d", h=4)),
               r=[b_bank[pb]], w=[b_w["kmv"]])

    def x_src(l, c, tile):
        base = c * T + tile * 128
        if l == 0:
            return xs[base:base + 128, :], []
        return x1d.ap()[base:base + 128, :], [b_x1[c]]

    def pass1(l, c, do_kv=True):
        fence([b_yst[0]], b_hb)
        fence(b_sg + b_mix + b_qx + b_XSW[2:], b_ub + [b_ubh])
        fence([b_cs, bf("win"), b_w["w_mem"]] + p2_tmp(), b_XS1[2:])
        xslot = {}

        def xload(tile):
            s_ = nxt("xs1", 5)
            xslot[tile] = s_
            src, rd = x_src(l, c, tile)
            DMA("sp", b_XS1[s_], XS1[s_][:, :], src, r=rd, w=[b_XS1[s_]])
        for t_ in range(4):
            xload(t_)
        pending = []
        rot["n"] = 2

        def proj_pieces(pblk):
            ptok0 = pblk * 512
            prd = [b_w["w_in"], b_hT[pblk]]

            def p_u():
                for g in range(2):
                    pb = proj_group(lambda cc, g=g: w_in_sb[:, cc, C_UPOOL + g * 128:C_UPOOL + (g + 1) * 128],
                                    lambda cc: hT[:, cc, ptok0:ptok0 + 512], 512, prd)
                    OP("act", lambda e, pb=pb, g=g: e.activation(out=ub[:, g, HALO + ptok0:HALO + ptok0 + 512],
                                                                 in_=bank(pb), func=AF.Copy),
                       r=[b_bank[pb]], w=[b_ub[pblk]])

            def p_k():
                while pending:
                    pending.pop(0)()
                chains = []
                for kvh in range(2):
                    pb = proj_group(lambda cc, kvh=kvh: w_krep[:, cc, kvh, :],
                                    lambda cc: hT[:, cc, ptok0:ptok0 + 512], 512, [b_w["w_krep"], b_hT[pblk]],
                                    pb=2 + kvh)
                    chains.append(qk_norm_rope(pb, Pk, bf("Pk"), gk, ptok0, kT2[:, kvh, ptok0:ptok0 + 512],
                                               [b_kT[pblk]], nbs=(4 + 2 * kvh, 5 + 2 * kvh), staged=True, qs=kvh))
                for st_a, st_b in zip(chains[0], chains[1]):
                    pending.extend([st_a, st_b])

            def p_v():
                pb = nxt("proj", 4)

                def fv(e):
                    i = None
                    for t4 in range(4):
                        for cc in range(8):
                            i = e.matmul(bank(pb)[:, t4 * 128:(t4 + 1) * 128],
                                         lhsT=hT[:, cc, ptok0 + t4 * 128: ptok0 + (t4 + 1) * 128],
                                         rhs=w_in_sb[:, cc, C_V:C_V + 128], start=(cc == 0), stop=(cc == 7))
                    return i
                OP("pe", fv, r=prd, w=[b_bank[pb]])
                dst = vaug[:, pblk * 8:(pblk + 1) * 8, 64:128]
                OP("dve", lambda e: e.tensor_copy(out=dst, in_=bank(pb).rearrange("p (b d) -> p b d", b=8)),
                   r=[b_bank[pb]], w=[b_v[pblk]])
            if do_kv:
                return [p_u, p_k, p_v, lambda: None]
            return [p_u, lambda: None, lambda: None, lambda: None]

        prev = None
        for blk in range(4):
            for tt in range(4):
                tile = blk * 4 + tt
                if tile + 4 < 16:
                    xload(tile + 4)
                s = xslot[tile]
                si = (32 if c == 2 else 0) + tile
                if do_kv or c != 2:
                    rms_stats(XS1[s][:, :], si, [b_XS1[s]], D)
                h = nxt("hb", 2)
                OP("dve", lambda e, s=s, h=h, si=si: e.tensor_scalar(
                    out=hb[h][:, :], in0=XS1[s][:, :], scalar1=rstd[:, si:si + 1], scalar2=None, op0=ALU.mult),
                   r=[b_XS1[s], b_stat[si]], w=[b_hb[h]])
                transposes_to(lambda tile=tile: hT[:, :, tile * 128:(tile + 1) * 128], hb[h], b_hb[h], [b_hT[blk]])
                for _ in range(3):
                    if pending:
                        pending.pop(0)()
                if prev is not None:
                    prev[tt]()
            prev = proj_pieces(blk)
        for f in prev:
            f()
        while pending:
            pending.pop(0)()
        rot["n"] = 4

    def pool_stage(c, setidx):
        fence([b_w["w_mem"]] + p2_tmp() + b_XS1[2:], [b_cs, bf("win")])
        for ti in range(2):
            OP("dve", lambda e, ti=ti: e.tensor_tensor_scan(
                out=cs[:, :], data0=ub[:, ti, :], data1=ub[:, ti, :], initial=0.0, op0=ALU.add, op1=ALU.bypass),
               r=b_ub + [b_ubh], w=[b_cs])
            for hf in range(2):
                w = (2, 4, 8, 16)[ti * 2 + hf]
                lo = HALO - w // 2 - 1
                hi = HALO + w // 2 - 1
                ln = slice(hf * 64, (hf + 1) * 64)
                OP("dve", lambda e, ln=ln, lo=lo, hi=hi, ti=ti: e.tensor_tensor(
                    out=win[ln, :], in0=cs[ln, hi:hi + T], in1=cs[ln, lo:lo + T], op=ALU.subtract),
                   r=[b_cs], w=[bf("win")])
            OP("dve", lambda e, ti=ti: e.scalar_tensor_tensor(
                out=dT[:, ti, :], in0=win[:, :], scalar=invw[:, ti:ti + 1], in1=ub[:, ti, HALO:HALO + T],
                op0=ALU.mult, op1=ALU.subtract), r=[bf("win"), bf("invw")] + b_ub, w=[b_dT])
            for (e0, t0) in ((0, 0), (HALO, T - HALO)):
                OP("dve", lambda e, ti=ti, e0=e0, t0=t0: e.tensor_tensor(
                    out=etmp[:, ti, e0:e0 + HALO], in0=win[:, t0:t0 + HALO],
                    in1=ptab_sb[:, setidx, ti * 2 * HALO + e0: ti * 2 * HALO + e0 + HALO], op=ALU.mult),
                   r=[bf("win"), b_w["ptab"]], w=[b_w["etmp"]])
                OP("dve", lambda e, ti=ti, e0=e0, t0=t0: e.tensor_tensor(
                    out=dT[:, ti, t0:t0 + HALO], in0=etmp[:, ti, e0:e0 + HALO],
                    in1=ub[:, ti, HALO + t0:HALO + t0 + HALO], op=ALU.subtract),
                   r=[b_w["etmp"]] + b_ub, w=[b_dT])


    def attention_block(heads, hooks=None):
        seq = []
        for hi, hd in enumerate(heads):
            hd["ob"] = 4 + (hi % 2)
            for g in range(hd["nkt"] // 2):
                seq.append((hd, g))
        pt_of = {}

        def emit_S(n):
            hd, g = seq[n]
            sp = nxt("S", 2)
            k_fn, qap = hd["k_fn"], hd["qap"]

            def f(e):
                e.matmul(bank(2 * sp), lhsT=k_fn(2 * g), rhs=qap, start=True, stop=True)
                return e.matmul(bank(2 * sp + 1), lhsT=k_fn(2 * g + 1), rhs=qap, start=True, stop=True)
            OP("pe", f, r=hd["rd_q"] + hd["rd_k"](2 * g) + hd["rd_k"](2 * g + 1),
               w=[b_bank[2 * sp], b_bank[2 * sp + 1]])
            p = nxt("pt", 3)
            pt_of[n] = p
            OP("act", lambda e: e.activation(out=PTP[p], in_=bank(2 * sp, 2), func=AF.Exp, scale=0.125),
               r=[b_bank[2 * sp], b_bank[2 * sp + 1]], w=[b_PT[p]])

        def emit_PV(n):
            hd, g = seq[n]
            p = pt_of[n]
            ob, nkt, odd = hd["ob"], hd["nkt"], hd["odd"]

            def f(e):
                i = None
                for u in range(2):
                    j = 2 * g + u
                    if odd:
                        i = e.matmul(bank(ob), lhsT=hd["v_odd"](j), rhs=PTP[p][:, u * 512:(u + 1) * 512],
                                     start=(j == 0), stop=(j == nkt - 1))
                    else:
                        i = e.matmul(bank(ob), lhsT=hd["v_even"](j), rhs=PTP[p][:, u * 512:(u + 1) * 512],
                                     start=(j == 0), stop=(j == nkt - 1))
                return i
            OP("pe", f, r=[b_PT[p]] + hd["rd_v"](2 * g) + hd["rd_v"](2 * g + 1), w=[b_bank[ob]])

        def tail(hd):
            odd, ob, hidx = hd["odd"], hd["ob"], hd["hidx"]
            sl = 0 if odd else 64
            dl = slice(64, 128) if odd else slice(0, 64)
            nl = 128 if odd else 65
            rbi = nxt("sr", 2)
            OP("dve", lambda e: e.tensor_copy(out=Rb[rbi][0:nl, :], in_=bank(ob)[0:nl, :]), r=[b_bank[ob]],
               w=[b_R[rbi]])
            OP("dve", lambda e: e.tensor_tensor(out=hd["mix_ap"], in0=Rb[rbi][dl, :], in1=hd["sg_ap"], op=ALU.mult),
               r=[b_R[rbi], hd["sg_buf"]], w=[hd["mix_buf"]])
            DMA("sp", bf("srow_dma"), srow_all[hidx:hidx + 1, :], Rb[rbi][sl:sl + 1, :], r=[b_R[rbi]],
                pw=[bf("srow_all")])

        emit_S(0)
        if len(seq) > 1:
            emit_S(1)
        for n in range(len(seq)):
            if n + 2 < len(seq):
                emit_S(n + 2)
            emit_PV(n)
            hd, g = seq[n]
            if g == hd["nkt"] // 2 - 1:
                tail(hd)
            if hooks and n in hooks:
                for f in hooks[n]:
                    f()

    def normalize_block():
        OP("act", lambda e: e.activation(out=srow_all[0:12, :], in_=srow_all[0:12, :], func=AF.Ln),
           r=[bf("srow_all")], w=[bf("srow_all")])
        OP("act", lambda e: e.activation(out=srow_all[0:12, :], in_=srow_all[0:12, :], func=AF.Exp, scale=-1.0),
           r=[bf("srow_all")], w=[bf("srow_all")])
        OP("dve", lambda e: e.tensor_copy(out=rhl[0:12, 0, :], in_=srow_all[0:12, :]),
           r=[bf("srow_all")], w=[bf("rhl")])
        OP("dve", lambda e: e.tensor_tensor(out=rhl[0:12, 1, :], in0=srow_all[0:12, :], in1=rhl[0:12, 0, :],
                                            op=ALU.subtract), r=[bf("srow_all"), bf("rhl")], w=[bf("rhl")])
        for i in range(6):
            bb = 5 + (i % 2)

            def fb(e, i=i, bb=bb):
                e.matmul(bank(bb), lhsT=sel_bf[0:12, i, :], rhs=rhl[0:12, 0, :], start=True, stop=False)
                return e.matmul(bank(bb), lhsT=sel_bf[0:12, i, :], rhs=rhl[0:12, 1, :], start=False, stop=True)
            OP("pe", fb, r=[bf("rhl"), bf("sel")], w=[b_bank[bb]])
            OP("dve", lambda e, i=i, bb=bb: e.tensor_tensor(out=mixT_blk[:, 2 + i, :], in0=mixT_blk[:, 2 + i, :],
                                                           in1=bank(bb), op=ALU.mult),
               r=[b_mix[2 + i], b_bank[bb]], w=[b_mix[2 + i]])

    def pass2(l, c, nkt):
        fence([b_cs, bf("win")], p2_tmp())
        fence(b_ub + [b_ubh], b_sg + b_mix + b_qx)
        fence(b_hb, [b_yst[0]])
        OP("dve", lambda e: e.memset(qTm[:, :, :], 0.0), w=b_qT)
        gcols = [C_GPOOL, C_GPOOL + 128, C_GATTN, C_GATTN + 128, C_GATTN + 256, C_GATTN + 384, C_GX, C_GX + 128]
        for blk in range(4):
            tok0 = blk * 512
            hrd = [b_w["w_in"], b_hT[blk]]

            def proj_half(col0, pb, half, qb=None):
                qb = blk if qb is None else qb
                qt0 = qb * 512

                def f(e):
                    i_ = None
                    for cc in range(4 * half, 4 * half + 4):
                        i_ = e.matmul(bank(pb), lhsT=w_in_sb[:, cc, col0:col0 + 128], rhs=hT[:, cc, qt0:qt0 + 512],
                                      start=(cc == 0), stop=(cc == 7))
                    return i_
                OP("pe", f, r=[b_w["w_in"], b_hT[qb]], w=[b_bank[pb]])

            def q_chain(i, qb=None):
                qb = blk if qb is None else qb
                col0 = C_Q + i * 128
                st = qk_norm_rope(7, Pq, bf("Pq"), gq, qb * 512, (qTm[0:64, 2 * i, :], qTm[64:128, 2 * i + 1, :]),
                                  [b_qT[i]], nbs=(6, 7), staged=True)
                return [lambda: proj_half(col0, 7, 0, qb), lambda: proj_half(col0, 7, 1, qb)] + st

            def qx_chain(i):
                col0 = C_QX + i * 128

                def cp():
                    OP("act", lambda e: e.activation(out=qxT_blk[:, i, :], in_=bank(7), func=AF.Copy),
                       r=[b_bank[7]], w=[b_qx[i]])
                return [lambda: proj_half(col0, 7, 0), lambda: proj_half(col0, 7, 1), cp]

            def proj_gate(i, pb):
                proj_group(lambda cc: w_in_sb[:, cc, gcols[i]:gcols[i] + 128],
                           lambda cc: hT[:, cc, tok0:tok0 + 512], 512, hrd, pb=pb)
                OP("act", lambda e: e.activation(out=sg_blk[:, i, :], in_=bank(pb), func=AF.Silu),
                   r=[b_bank[pb]], w=[b_sg[i]])

            def pool_mix(ti):
                pb = 7
                OP("pe", lambda e: e.matmul(bank(pb), lhsT=poolw_sb[:, ti, :],
                                            rhs=dT[:, ti, tok0:tok0 + 512], start=True, stop=True),
                   r=[b_w["poolw"], b_dT], w=[b_bank[pb]])
                OP("dve", lambda e: e.scalar_tensor_tensor(
                    out=mixT_blk[:, ti, :], in0=bank(pb), scalar=pscale[:, ti:ti + 1], in1=sg_blk[:, ti, :],
                    op0=ALU.mult, op1=ALU.mult), r=[b_bank[pb], b_sg[ti], b_w["gains"]], w=[b_mix[ti]])

            def head_self(h):
                lane0 = (h % 2) * 64
                kvh = h // 4
                ln = slice(lane0, lane0 + 64)
                return dict(hidx=h, qap=qTm[:, h, :], odd=(lane0 == 64),
                            k_fn=lambda j: kT2[:, kvh, j * 128:(j + 1) * 128], nkt=nkt,
                            v_even=lambda j: vaug_f[:, (j * 2 + kvh) * VB + 64:(j * 2 + kvh) * VB + 192],
                            v_odd=lambda j: vaug[:, j * 2 + kvh, 0:128],
                            sg_ap=sg_blk[ln, 2 + h // 2, :], sg_buf=b_sg[2 + h // 2],
                            mix_ap=mixT_blk[ln, 2 + h // 2, :], mix_buf=b_mix[2 + h // 2],
                            rd_q=[b_qT[h // 2]], rd_k=lambda j: [b_kT[j // 4]], rd_v=lambda j: [b_v[j // 4]])

            def head_mem(hx):
                lane0 = (hx % 2) * 64
                ln = slice(lane0, lane0 + 64)
                return dict(hidx=8 + hx, qap=qxT_blk[ln, hx // 2, :], odd=(lane0 == 64),
                            k_fn=lambda j: kmT[ln, c, hx // 2, j * 128:(j + 1) * 128], nkt=2,
                            v_even=lambda j: vmaug_f[:, (c * 8 + j * 4 + hx) * VB + 64:(c * 8 + j * 4 + hx) * VB + 192],
                            v_odd=lambda j: vmaug[:, c * 8 + j * 4 + hx, 0:128],
                            sg_ap=sg_blk[ln, 6 + hx // 2, :], sg_buf=b_sg[6 + hx // 2],
                            mix_ap=mixT_blk[ln, 6 + hx // 2, :], mix_buf=b_mix[6 + hx // 2],
                            rd_q=[b_qx[hx // 2]], rd_k=lambda j: [b_w["kmv"]], rd_v=lambda j: [b_w["kmv"]])

            xpre = {}
            for tt in range(2):
                s_ = nxt("xst", 2)
                src, rd = x_src(l, c, blk * 4 + tt)
                DMA("sp", b_xst[s_], xst[s_][:, :], src, r=rd, w=[b_xst[s_]])
                xpre[tt] = s_
            for i in range(8):
                proj_gate(i, 5 + (i % 3))
            pool_mix(0)
            pool_mix(1)
            if blk == 0:
                for f in q_chain(0):
                    f()

            npair = nkt // 2
            hooks = {}

            def spread(stages, first):
                for k, f in enumerate(stages):
                    hooks.setdefault(first + k, []).append(f)
            spread(q_chain(1), 0)
            spread(qx_chain(0), npair)
            spread(q_chain(2), 2 * npair)
            spread(qx_chain(1), 3 * npair)
            spread(q_chain(3), 4 * npair)
            if blk < 3:
                spread(q_chain(0, blk + 1), 5 * npair + 1)
            hs = [head_self(h) for h in range(8)]
            hm = [head_mem(hx) for hx in range(4)]
            attention_block(hs[0:5] + [hm[0], hs[5], hm[1], hs[6], hm[2], hs[7], hm[3]], hooks)
            normalize_block()
            for tt in range(4):
                tile = blk * 4 + tt

                ob0 = (6, 2)[tt % 2]

                def fo(e, tt=tt, ob0=ob0):
                    i = None
                    for half in range(2):
                        for cc in range(8):
                            i = e.matmul(bank(ob0 + half), lhsT=mixT_blk[:, cc, tt * 128:(tt + 1) * 128],
                                         rhs=w_out_sb[:, cc, half * 512:(half + 1) * 512], start=(cc == 0),
                                         stop=(cc == 7))
                    return i
                OP("pe", fo, r=b_mix + [b_w["w_out"]], w=[b_bank[ob0], b_bank[ob0 + 1]])
                yps = bank(ob0, 2)
                sidx = 16 + tile
                rms_stats(yps, sidx, [b_bank[ob0], b_bank[ob0 + 1]], D)
                if tt in xpre:
                    s = xpre[tt]
                else:
                    s = nxt("xst", 2)
                    src, rd = x_src(l, c, tile)
                    DMA("sp", b_xst[s], xst[s][:, :], src, r=rd, w=[b_xst[s]])
                y = nxt("yst", 2)
                OP("dve", lambda e, y=y, sidx=sidx, yps=yps: e.scalar_tensor_tensor(
                    out=yst[y][:, :], in0=yps, scalar=rstd[:, sidx:sidx + 1], in1=gpost[:, :], op0=ALU.mult,
                    op1=ALU.mult), r=[b_bank[ob0], b_bank[ob0 + 1], b_stat[sidx], b_w["gains"]], w=[b_yst[y]])
                OP("dve", lambda e, y=y, s=s: e.tensor_tensor(out=yst[y][:, :], in0=yst[y][:, :], in1=xst[s][:, :],
                                                              op=ALU.add), r=[b_yst[y], b_xst[s]], w=[b_yst[y]])
                base = c * T + tile * 128
                if l == nlayers - 1:
                    DMA("sp", b_yst[y], yout[base:base + 128, :], yst[y][:, :], r=[b_yst[y]], pw=[bf("yout")])
                else:
                    DMA("sp", b_yst[y], x1d.ap()[base:base + 128, :], yst[y][:, :], r=[b_yst[y]], pw=[b_x1[c]])

    def load_rope(setidx):
        DMA("pool", b_w["rope"], ropeC[:, :], rope[setidx, 0], pw=[b_w["rope"]])
        DMA("pool", b_w["rope"], ropeS[:, :], rope[setidx, 1], pw=[b_w["rope"]])

    def zero_halos():
        fence(b_sg + b_mix + b_qx, b_ub + [b_ubh])
        OP("dve", lambda e: e.memset(ub[:, :, 0:HALO], 0.0), w=[b_ubh])
        OP("dve", lambda e: e.memset(ub[:, :, HALO + T:UBW], 0.0), w=[b_ubh])

    DMA("sp", b_w["ptab"], ptab_sb[:, :, :], ptab.rearrange("s p n -> p s n"), pw=[b_w["ptab"]])

    with nc.allow_low_precision(reason="bf16 matmul operands, fp32 accumulation"):
        for l in range(nlayers):
            load_layer(l)
            for c in range(NCH):
                mem_kv(c)
            load_rope(1)
            pass1(l, 2, do_kv=True)
            OP("dve", lambda e: e.tensor_copy(out=halo_st[:, :, 0:HALO], in_=ub[:, :, HALO:2 * HALO]),
               r=b_ub, w=[b_w["halo_st"]])
            OP("dve", lambda e: e.tensor_copy(out=halo_st[:, :, HALO:2 * HALO], in_=ub[:, :, T:T + HALO]),
               r=b_ub, w=[b_w["halo_st"]])
            xin = xin_d[l].ap()
            xout = xout_d[l].ap()
            with nc.allow_non_contiguous_dma(reason="exchange payload"):
                for kvh in range(2):
                    DMA("pool", b_xin[l], xin[kvh * 64:(kvh + 1) * 64, XK0:XK0 + T], kT2[0:64, kvh, 0:T],
                        r=b_kT[0:4], pw=[b_xin[l]])
                DMA("pool", b_xin[l], xin[:, XV0:XV0 + XVW], vaug[:, 0:32, :].rearrange("p b c -> p (b c)"), r=b_v[0:4],
                    pw=[b_xin[l]])
                DMA("pool", b_xin[l], xin[:, XH0:XW], halo_st[:, :, :].rearrange("p a b -> p (a b)"),
                    r=[b_w["halo_st"]], pw=[b_xin[l]])
            tr._collect("pool", [b_xin[l]], [b_xout[l]], True)
            E = tr.eng["pool"]
            inst = nc.gpsimd.collective_compute(
                "AllGather", ALU.bypass, replica_groups=[[2 * i, 2 * i + 1] for i in range(ncores // 2)],
                ins=[xin.opt()], outs=[xout.opt()])
            E["cnt"] += 1
            inst.then_inc(E["sem"], 1)
            tr._record(E["key"], (E["sem"], E["cnt"], None), [b_xin[l]], [b_xout[l]])
            for c in range(2):
                load_rope(0)
                zero_halos()
                pass1(l, c)
                pool_stage(c, 0)
                pass2(l, c, 16)
            load_rope(1)
            pass1(l, 2, do_kv=False)
            with nc.allow_non_contiguous_dma(reason="exchange payload"):
                for rnk in range(2):
                    for kvh in range(2):
                        for rep in range(2):
                            DMA("sp", bf("kvload"),
                                kT2[rep * 64:(rep + 1) * 64, kvh, rnk * T:(rnk + 1) * T],
                                xout[rnk * 128 + kvh * 64: rnk * 128 + (kvh + 1) * 64, XK0:XK0 + T],
                                r=[b_xout[l]], pw=b_kT)
                    DMA("sp", bf("kvload"), vaug[:, rnk * 32:(rnk + 1) * 32, :].rearrange("p b c -> p (b c)"),
                        xout[rnk * 128:(rnk + 1) * 128, XV0:XV0 + XVW], r=[b_xout[l]], pw=b_v)
                DMA("sp", b_w["halo_in"], halo_in[:, :, 0:2 * HALO],
                    xout[0:128, XH0:XW].rearrange("p (a b) -> p a b", a=2), r=[b_xout[l]], pw=[b_w["halo_in"]])
                DMA("sp", b_w["halo_in"], halo_in[:, :, 2 * HALO:4 * HALO],
                    xout[128:256, XH0:XW].rearrange("p (a b) -> p a b", a=2), r=[b_xout[l]], pw=[b_w["halo_in"]])
            OP("dve", lambda e: e.tensor_scalar(out=ub[:, :, 0:HALO], in0=halo_in[:, :, HALO:2 * HALO],
                                                scalar1=hmask_sb[:, 0:1], scalar2=None, op0=ALU.mult),
               r=[b_w["halo_in"], b_w["gains"]], w=[b_ubh])
            OP("dve", lambda e: e.tensor_scalar(out=ub[:, :, HALO + T:UBW], in0=halo_in[:, :, 2 * HALO:3 * HALO],
                                                scalar1=hmask_sb[:, 1:2], scalar2=None, op0=ALU.mult),
               r=[b_w["halo_in"], b_w["gains"]], w=[b_ubh])
            pool_stage(2, 1)
            pass2(l, 2, 32)
        tr.wait_all("sp", [bf("yout")] + b_yst)
    return nc, tr


def _rope_tables(pos):
    pos = np.asarray(pos, dtype=np.float64)
    row = np.floor(pos / 64.0)
    col = pos - 64.0 * row
    freqs = 10000.0 ** (-np.arange(16, dtype=np.float64) / 16.0)
    cosT = np.zeros((128, len(pos)), np.float32)
    sinT = np.zeros((128, len(pos)), np.float32)
    for lane in range(128):
        d = lane % 64
        axis, ab, p = d // 32, (d % 32) // 16, d % 16
        ang = (row if axis == 0 else col) * freqs[p]
        cosT[lane] = np.cos(ang)
        sinT[lane] = (-1.0 if ab == 0 else 1.0) * np.sin(ang)
    return cosT, sinT


def _pool_tab(t_glob, tseq):
    tab = np.zeros((128, 2, 2 * HALO), np.float32)
    for ti in range(2):
        for lane in range(128):
            w = (2, 4, 8, 16)[ti * 2 + lane // 64]
            for j, tg in enumerate(t_glob):
                lo = min(max(tg - w // 2, 0), tseq)
                hi = min(max(tg + w - w // 2, 0), tseq)
                tab[lane, ti, j] = 1.0 / float(hi - lo)
    return tab


def _consts():
    ident = np.eye(128, dtype=np.float32)
    swap = np.zeros((128, 128), np.float32)
    for m in range(128):
        swap[m ^ 16, m] = 1.0
    bones = np.zeros((128, 128), np.float32)
    bones[:64, :64] = 1.0 / 64.0
    bones[64:, 64:] = 1.0 / 64.0
    return np.stack([ident, swap, bones], 0)


def _selm():
    m = np.zeros((12, 6, 128), np.float32)
    for h in range(12):
        m[h, h // 2, (h % 2) * 64:(h % 2 + 1) * 64] = 1.0
    return m.reshape(12, 768)


_PROG = {}


def kernel(x_prompt, x_sample, mem_prompt, mem_sample, norm_pre, norm_post, w_in, pool_w, pool_scale,
           q_norm, k_norm, mem_norm, w_mem_kv, w_out):
    f32 = lambda a: np.ascontiguousarray(np.asarray(a, dtype=np.float32))
    x_prompt, x_sample, mem_prompt, mem_sample = map(f32, (x_prompt, x_sample, mem_prompt, mem_sample))
    if "nc" not in _PROG:
        _PROG["nc"], _PROG["tr"] = build_program()
    nc = _PROG["nc"]
    in_maps = _prep_inputs(x_prompt, x_sample, mem_prompt, mem_sample, norm_pre, norm_post, w_in, pool_w, pool_scale,
                           q_norm, k_norm, mem_norm, w_mem_kv, w_out)
    res = run_bass_kernel_spmd(nc, in_maps, core_ids=list(range(NCORES)))
    y_prompt = np.empty_like(x_prompt)
    y_sample = np.empty_like(x_sample)
    for c in range(NCORES):
        y = res.results[c]["y"]
        y_prompt[2 * c] = y[0:T]
        y_prompt[2 * c + 1] = y[T:2 * T]
        y_sample[c // 2, (c % 2) * T:(c % 2 + 1) * T] = y[2 * T:3 * T]
    return (y_prompt, y_sample)


def _prep_inputs(x_prompt, x_sample, mem_prompt, mem_sample, norm_pre, norm_post, w_in, pool_w, pool_scale,
                 q_norm, k_norm, mem_norm, w_mem_kv, w_out, ncores=NCORES):
    f32 = lambda a: np.ascontiguousarray(np.asarray(a, dtype=np.float32))
    cm = _consts()
    shared = {
        "w_in": f32(w_in), "w_out": f32(w_out), "w_mem": f32(w_mem_kv),
        "pool_w": f32(pool_w).reshape(2, 256, 64), "norm_pre": f32(norm_pre), "norm_post": f32(norm_post),
        "mem_norm": f32(mem_norm), "pool_scale": f32(pool_scale), "q_norm": f32(q_norm), "k_norm": f32(k_norm),
        "cmat": cm, "selm": _selm(),
    }
    cp, sp_ = _rope_tables(np.arange(T))
    tab_p = _pool_tab(list(range(HALO)) + list(range(T - HALO, T)), T)
    in_maps = []
    for c in range(ncores):
        sq, half = c // 2, c % 2
        t0 = half * T
        xs = np.concatenate([x_prompt[2 * c], x_prompt[2 * c + 1], x_sample[sq, t0:t0 + T]], 0)
        mm = np.concatenate([mem_prompt[2 * c], mem_prompt[2 * c + 1], mem_sample[sq]], 0)
        cs_, ss_ = _rope_tables(np.arange(t0, t0 + T))
        rope = np.stack([np.stack([cp, sp_], 0), np.stack([cs_, ss_], 0)], 0)
        tab_s = _pool_tab(list(range(t0, t0 + HALO)) + list(range(t0 + T - HALO, t0 + T)), 2 * T)
        ptab = np.stack([tab_p.reshape(128, -1), tab_s.reshape(128, -1)], 0)
        hm = np.zeros((128, 2), np.float32)
        hm[:, 0] = 1.0 if half == 1 else 0.0
        hm[:, 1] = 1.0 if half == 0 else 0.0
        m = dict(shared)
        m.update({"xs": np.ascontiguousarray(xs), "mems": np.ascontiguousarray(mm),
                  "rope": np.ascontiguousarray(rope.astype(np.float32)), "ptab": np.ascontiguousarray(ptab),
                  "hmask": hm})
        in_maps.append(m)
    return in_maps
```

```python
import numpy as np
import ml_dtypes
import concourse.bass as bass
import concourse.mybir as mybir
from concourse.bass_utils import run_bass_kernel_spmd

F32 = mybir.dt.float32
BF16 = mybir.dt.bfloat16
AF = mybir.ActivationFunctionType
ALU = mybir.AluOpType

NCORES = 8
D = 1024
T = 2048
NCH = 3
NMEM = 256
INW = 2304
EPS = 1e-6
HALO = 16
UBW = T + 2 * HALO
VB = 129
XW = 2 * T // 2 + 0

C_UPOOL, C_GPOOL, C_Q, C_K, C_V, C_GATTN, C_QX, C_GX = 0, 256, 512, 1024, 1152, 1280, 1792, 2048

XK0 = 0
XV0 = T
XVW = 16 * 2 * VB
XH0 = XV0 + XVW
XW = XH0 + 2 * 2 * HALO


class Buf:
    __slots__ = ("name", "w", "r", "sem", "semcnt", "excl")

    def __init__(self, name, excl=False):
        self.name = name
        self.excl = excl
        self.w = {}
        self.r = {}
        self.sem = None
        self.semcnt = 0


class Tracker:
    def __init__(self, nc):
        self.nc = nc
        self.eng = {}
        for n, e in (("pe", nc.tensor), ("act", nc.scalar), ("dve", nc.vector),
                     ("pool", nc.gpsimd), ("sp", nc.sync)):
            sem = nc.alloc_semaphore(name=f"sem_{n}")
            self.eng[n] = {"e": e, "sem": sem, "key": f"sem_{n}", "cnt": 0, "seen": {}}
        self.nwaits = 0
        self.nops = 0

    def _collect(self, en, reads, writes, is_dma, pwrites=(), pkey=None):
        E = self.eng[en]
        need = {}

        def add(key, ev, raw):
            h, val, owner = ev
            if owner is not None:
                val = owner.semcnt
            if not is_dma and key == E["key"] and en == "pe":
                return
            cur = need.get(key)
            if cur is None or cur[1] < val:
                need[key] = (h, val)

        for b in reads:
            for key, ev in b.w.items():
                add(key, ev, True)
        for b in writes:
            for key, ev in b.w.items():
                add(key, ev, False)
            for key, ev in b.r.items():
                add(key, ev, False)
        for b in pwrites:
            for key, ev in b.r.items():
                add(key, ev, False)
            for key, ev in b.w.items():
                if key != pkey:
                    add(key, ev, False)
        for key, (h, val) in need.items():
            if E["seen"].get(key, 0) < val:
                E["e"].wait_ge(h, val)
                E["seen"][key] = val
                self.nwaits += 1

    def _record(self, key, ev, reads, writes, pwrites=()):
        for b in reads:
            b.r[key] = ev
        for b in writes:
            b.w = {key: ev}
            b.r = {}
        for b in pwrites:
            b.w[key] = ev
            b.r = {}

    def op(self, en, fn, r=(), w=()):
        E = self.eng[en]
        if any(b.excl for b in r):
            w = list(w) + [b for b in r if b.excl]
            r = [b for b in r if not b.excl]
        self._collect(en, r, w, False)
        inst = fn(E["e"])
        E["cnt"] += 1
        inst.then_inc(E["sem"], 1)
        self._record(E["key"], (E["sem"], E["cnt"], None), r, w)
        self.nops += 1

    def dma(self, q, owner, out, in_, r=(), w=(), pw=(), **kw):
        E = self.eng[q]
        if owner.sem is None:
            owner.sem = self.nc.alloc_semaphore(name=f"dsem_{owner.name}")
        self._collect(q, r, w, True, pw, f"dsem_{owner.name}")
        inst = E["e"].dma_start(out=out, in_=in_, **kw)
        owner.semcnt += 16
        inst.then_inc(owner.sem, 16)
        self._record(f"dsem_{owner.name}", (owner.sem, owner.semcnt, owner), r, w, pw)
        self.nops += 1

    def wait_all(self, en, bufs):
        self._collect(en, [], bufs, True)


def build_program(nlayers=2, ncores=NCORES):
    nc = bass.Bass("TRN2", target_bir_lowering=False)
    tr = Tracker(nc)

    def dram_in(name, shape, dt=F32):
        return nc.dram_tensor(name, list(shape), dt, kind="ExternalInput").ap()

    xs = dram_in("xs", [NCH * T, D])
    mems = dram_in("mems", [NCH * NMEM, D])
    w_in = dram_in("w_in", [2, D, INW])
    w_out = dram_in("w_out", [2, D, D])
    w_mem = dram_in("w_mem", [2, D, 512])
    pool_w = dram_in("pool_w", [2, 256, 64])
    norm_pre = dram_in("norm_pre", [2, D])
    norm_post = dram_in("norm_post", [2, D])
    mem_norm = dram_in("mem_norm", [2, D])
    pool_scale = dram_in("pool_scale", [2, 256])
    q_norm = dram_in("q_norm", [2, 64])
    k_norm = dram_in("k_norm", [2, 64])
    rope = dram_in("rope", [2, 2, 128, T])
    ptab = dram_in("ptab", [2, 128, 2 * 2 * HALO])
    hmask = dram_in("hmask", [128, 2])
    cmat = dram_in("cmat", [3, 128, 128])
    selm = dram_in("selm", [12, 6 * 128])
    yout = nc.dram_tensor("y", [NCH * T, D], F32, kind="ExternalOutput").ap()
    x1d = nc.dram_tensor("x1_scratch", [NCH * T, D], F32)
    xin_d = [nc.dram_tensor(f"xin{l}", [128, XW], BF16) for l in range(2)]
    xout_d = [nc.dram_tensor(f"xout{l}", [256, XW], BF16) for l in range(2)]
    b_x1 = [Buf(f"x1_{c}") for c in range(NCH)]
    b_xin = [Buf(f"xin{l}") for l in range(2)]
    b_xout = [Buf(f"xout{l}") for l in range(2)]

    def sb(name, shape, dt):
        return nc.alloc_sbuf_tensor(name, list(shape), dt)

    w_in_sb = sb("w_in_sb", [128, 8, INW], BF16)
    w_out_sb = sb("w_out_sb", [128, 8, D], BF16)
    w_krep = sb("w_krep", [128, 8, 2, 128], BF16)
    poolw_sb = sb("poolw_sb", [128, 2, 128], BF16)
    hT = sb("hT", [128, 8, T], BF16)
    def view(region, byte_off, shape, dt):
        esz = 4 if dt == F32 else 2
        n = 1
        for d_ in shape[1:]:
            n *= d_
        a = region[:, byte_off // 2: byte_off // 2 + n * esz // 2]
        if dt == F32:
            a = a.bitcast(F32)
        if len(shape) == 3:
            a = a.rearrange("p (a b) -> p a b", a=shape[1])
        return a

    regX = sb("regX", [128, 18432 // 2], BF16)
    regY = sb("regY", [128, 16512 // 2], BF16)
    regZ = sb("regZ", [128, 4096 // 2], BF16)
    ub = view(regX, 0, [128, 2, UBW], F32)
    sg_blk = view(regX, 0, [128, 8, 512], BF16)
    mixT_blk = view(regX, 8192, [128, 8, 512], BF16)
    qxT_blk = view(regX, 16384, [128, 2, 512], BF16)
    w_mem_sb = view(regY, 0, [128, 8, 512], BF16)
    cs = view(regY, 0, [128, UBW], F32)
    win = view(regY, 8320, [128, T], F32)
    qTm = view(regY, 0, [128, 8, 512], BF16)
    kT2 = sb("kT2", [128, 2, 2 * T], BF16)
    vaug_f = sb("vaug", [128, 64 * VB + 64], BF16)
    vaug = vaug_f[:, 0:64 * VB].rearrange("p (b c) -> p b c", c=VB)
    xst = [sb(f"xst{i}", [128, D], F32) for i in range(2)]
    yst = [view(regZ, 0, [128, D], F32)] * 2
    NPT = 3
    PTP = [view(regY, 8192, [128, 1024], BF16), view(regY, 10240, [128, 1024], BF16),
           view(regY, 14336, [128, 1024], BF16)]
    ropeC = sb("ropeC", [128, T], BF16)
    ropeS = sb("ropeS", [128, T], BF16)
    dT = sb("dT", [128, 2, T], BF16)
    kmT = sb("kmT", [128, NCH, 2, NMEM], BF16)
    vmaug_f = sb("vmaug", [128, NCH * 8 * VB + 64], BF16)
    vmaug = vmaug_f[:, 0:NCH * 8 * VB].rearrange("p (b c) -> p b c", c=VB)
    hb = [view(regZ, i * 2048, [128, D], BF16) for i in range(2)]
    ident_bf = sb("ident_bf", [128, 128], BF16)
    bones_bf = sb("bones_bf", [128, 128], BF16)
    swapP = sb("swapP", [128, 128], F32)
    Pq = sb("Pq", [128, 128], BF16)
    Pk = sb("Pk", [128, 128], BF16)
    ones_bf = sb("ones_bf", [128, 64], BF16)
    gq = sb("gq", [128, 1], F32)
    gk = sb("gk", [128, 1], F32)
    gpre = sb("gpre", [128, 8], F32)
    gmem = sb("gmem", [128, 8], F32)
    gpost = sb("gpost", [128, D], F32)
    pscale = sb("pscale", [128, 2], F32)
    invw = sb("invw", [128, 2], F32)
    hmask_sb = sb("hmask_sb", [128, 2], F32)
    ptab_sb = sb("ptab_sb", [128, 2, 2 * 2 * HALO], F32)
    halo_st = sb("halo_st", [128, 2, 2 * HALO], BF16)
    halo_in = sb("halo_in", [128, 2, 2 * 2 * HALO], BF16)
    etmp = sb("etmp", [128, 2, 2 * HALO], F32)
    ss = sb("ss", [128, 64], F32)
    lnv = sb("lnv", [128, 64], F32)
    rstd = sb("rstd", [128, 64], F32)
    NQS = 1
    sqb = [sb(f"sqb{i}", [128, 512], BF16) for i in range(NQS)]
    zsb = [sb(f"zsb{i}", [128, 512], BF16) for i in range(NQS)]
    t1b = [sb(f"t1b{i}", [128, 512], F32) for i in range(NQS)]
    sqb.append(view(regY, 12288, [128, 512], BF16))
    zsb.append(view(regY, 13312, [128, 512], BF16))
    t1b.append(view(regY, 14336, [128, 512], F32))
    srow_all = view(regY, 12288, [128, 512], F32)
    rhl = view(regY, 14336, [128, 2, 512], BF16)
    sel_bf = sb("sel_bf", [12, 6, 128], BF16)
    Rb0 = sb("Rb0", [128, 512], F32)
    junk = Rb0[:, :].bitcast(BF16)
    Rb = [Rb0, t1b[0]]

    ps_all = nc.alloc_psum_tensor("ps_all", [128, 8 * 512], F32)

    def bank(b, n=1):
        return ps_all[:, b * 512:(b + n) * 512]

    B = {}

    def bf(name):
        if name not in B:
            B[name] = Buf(name)
        return B[name]

    b_bank = [bf(f"bank{i}") for i in range(8)]
    for b_ in b_bank:
        b_.excl = True
    b_xst = [bf(f"xst{i}") for i in range(2)]
    b_yst = [bf("yst0")] * 2
    b_hb = [bf(f"hb{i}") for i in range(2)]
    b_PT = [bf("PT0"), bf("PT1"), bf("rhl")]
    b_hT = [bf(f"hT{i}") for i in range(4)]
    b_kT = [bf(f"kT{i}") for i in range(8)]
    b_v = [bf(f"v{i}") for i in range(8)]
    b_ub = [bf(f"ub{i}") for i in range(4)]
    b_ubh = bf("ubh")
    b_ubt = [bf("ubt0"), bf("ubt1")]
    b_ubhs = [bf("ubh0"), bf("ubh1")]
    b_dT = bf("dT")
    b_cs = bf("cs")
    b_qT = [bf(f"qT{i}") for i in range(4)]
    b_qz = bf("qzero")
    b_qx = [bf(f"qx{i}") for i in range(2)]
    b_sg = [bf(f"sg{i}") for i in range(8)]
    b_mix = [bf(f"mix{i}") for i in range(8)]
    b_w = {n: bf(n) for n in ("w_in", "w_out", "w_mem", "w_krep", "poolw", "consts", "gains",
                              "rope", "ptab", "kmv", "stat", "halo_st", "halo_in", "etmp", "junk")}
    b_stat = [bf(f"stat{i}") for i in range(64)]
    b_sq = [bf(f"sq{i}") for i in range(NQS)]
    b_zs = [bf(f"zs{i}") for i in range(NQS)]
    b_t1 = [bf(f"t1{i}") for i in range(NQS)]
    b_sq.append(bf("sq_1"))
    b_zs.append(bf("zs_1"))
    b_t1.append(bf("t1_1"))
    b_gt = []
    b_srow = [bf("srow_all"), bf("rhl")]
    b_R = [bf("R0"), b_t1[0]]
    b_tb = []

    OP = tr.op
    DMA = tr.dma
    XS1 = [xst[0], xst[1]] + [view(regY, i * 4096, [128, D], F32) for i in range(3)]
    b_XS1 = [b_xst[0], b_xst[1]] + [bf(f"xsY{i}") for i in range(3)] + [b_sq[1], b_zs[1], b_t1[1]]
    XSW = [xst[0], xst[1]] + [view(regX, i * 4096, [128, D], F32) for i in range(4)]
    b_XSW = [b_xst[0], b_xst[1]] + [bf(f"xsX{i}") for i in range(4)]

    def fence(A, Bs):
        for a in A:
            for src in (a.w, a.r):
                for key, ev in src.items():
                    for b in Bs:
                        cur = b.r.get(key)
                        if cur is None or cur[1] < ev[1]:
                            b.r[key] = ev

    def p2_tmp():
        return b_qT + b_PT + b_gt + b_srow + b_tb
    rr = {"proj": 0, "nrm": 0, "qs": 0, "gt": 0, "pt": 0, "S": 0, "O": 0, "xst": 0, "yst": 0, "hb": 0,
          "sr": 0, "xs1": 0, "xsw": 0}

    def nxt(k, n):
        if k == "proj":
            n = rot["n"]
        v = rr[k] % n
        rr[k] = (v + 1) % n
        return v
    rot = {"n": 4}

    cst = xst[0][:, 0:384].rearrange("p (k m) -> p k m", k=3)
    with nc.allow_non_contiguous_dma(reason="small constant / gain loads"):
        DMA("sp", b_xst[0], cst, cmat.rearrange("k p m -> p k m"), w=[b_xst[0]])
        DMA("sp", b_w["ptab"], hmask_sb[:, :], hmask, pw=[b_w["gains"]])
    OP("dve", lambda e: e.tensor_copy(out=ident_bf[:, :], in_=cst[:, 0, :]), r=[b_xst[0]], w=[bf("ident")])
    OP("dve", lambda e: e.tensor_copy(out=swapP[:, :], in_=cst[:, 1, :]), r=[b_xst[0]], w=[bf("swapP")])
    OP("dve", lambda e: e.tensor_copy(out=bones_bf[:, :], in_=cst[:, 2, :]), r=[b_xst[0]], w=[bf("bones")])
    OP("dve", lambda e: e.memset(ones_bf[:, :], 1.0), w=[bf("ones")])
    DMA("pool", bf("sel"), sel_bf[:, :, :].rearrange("p a b -> p (a b)"), selm, w=[bf("sel")])
    OP("dve", lambda e: e.memset(invw[0:64, 0:1], 0.5), w=[bf("invw")])
    OP("dve", lambda e: e.memset(invw[64:128, 0:1], 0.25), w=[bf("invw")])
    OP("dve", lambda e: e.memset(invw[0:64, 1:2], 0.125), w=[bf("invw")])
    OP("dve", lambda e: e.memset(invw[64:128, 1:2], 0.0625), w=[bf("invw")])
    OP("dve", lambda e: e.memset(vaug_f[:, :], 0.0), w=b_v)
    OP("dve", lambda e: e.memset(vaug[:, :, 0:1], 1.0), w=b_v)
    OP("dve", lambda e: e.memset(vaug[:, :, 128:129], 1.0), w=b_v)
    OP("dve", lambda e: e.memset(vmaug_f[:, :], 0.0), w=[b_w["kmv"]])
    OP("dve", lambda e: e.memset(vmaug[:, :, 0:1], 1.0), w=[b_w["kmv"]])
    OP("dve", lambda e: e.memset(vmaug[:, :, 128:129], 1.0), w=[b_w["kmv"]])

    def rms_stats(src_ap, idx, rd, nfree, extra_w=()):
        OP("act", lambda e: e.activation(out=junk[:, 0:nfree], in_=src_ap, func=AF.Square,
                                         accum_out=ss[:, idx:idx + 1]),
           r=rd, w=[b_R[0], b_stat[idx]])
        OP("act", lambda e: e.activation(out=lnv[:, idx:idx + 1], in_=ss[:, idx:idx + 1], func=AF.Ln,
                                         scale=1.0 / nfree, bias=eps_t[:, 0:1]),
           r=[b_stat[idx], bf("eps")], w=[b_stat[idx]])
        OP("act", lambda e: e.activation(out=rstd[:, idx:idx + 1], in_=lnv[:, idx:idx + 1], func=AF.Exp,
                                         scale=-0.5),
           r=[b_stat[idx]], w=[b_stat[idx]])

    eps_t = sb("eps_t", [128, 1], F32)
    OP("dve", lambda e: e.memset(eps_t[:, :], EPS), w=[bf("eps")])

    def transposes_to(dst_ap_fn, src_tile, src_buf, dst_bufs, nchunk=8):
        pb = nxt("proj", 4)
        psb = bank(pb).bitcast(BF16)

        def f(e):
            i = None
            for c in range(nchunk):
                i = e.transpose(out=psb[:, c * 128:(c + 1) * 128], in_=src_tile[:, c * 128:(c + 1) * 128],
                                identity=ident_bf[:, :])
            return i
        OP("pe", f, r=[src_buf, bf("ident")], w=[b_bank[pb]])
        OP("dve", lambda e: e.tensor_copy(out=dst_ap_fn(),
                                          in_=psb[:, 0:nchunk * 128].rearrange("p (c t) -> p c t", c=nchunk)),
           r=[b_bank[pb]], w=dst_bufs)

    def proj_group(lhs_fn, rhs_fn, n, rd, nk=8, pb=None):
        if pb is None:
            pb = nxt("proj", 4)

        def f(e):
            i = None
            for c in range(nk):
                i = e.matmul(bank(pb)[:, 0:n], lhsT=lhs_fn(c), rhs=rhs_fn(c), start=(c == 0), stop=(c == nk - 1))
            return i
        OP("pe", f, r=rd, w=[b_bank[pb]])
        return pb

    def qk_norm_rope(pb, Pmat, Pbuf, gvec, tok0, dst_ap, dst_bufs, nbs=None, staged=False, qs=0):
        s = qs
        if nbs is None:
            nb = 4 + 2 * nxt("nrm", 2)
            nbz = nb + 1
        else:
            nb, nbz = nbs
        z = bank(pb)

        def st_act1():
            OP("act", lambda e: e.activation(out=sqb[s][:, :], in_=z, func=AF.Square), r=[b_bank[pb]], w=[b_sq[s]])
            OP("act", lambda e: e.activation(out=zsb[s][:, :], in_=z, func=AF.Copy), r=[b_bank[pb]], w=[b_zs[s]])

        def st_dve0():
            OP("dve", lambda e: e.scalar_tensor_tensor(out=t1b[s][:, :], in0=z, scalar=gvec[:, 0:1],
                                                       in1=ropeC[:, tok0:tok0 + 512], op0=ALU.mult, op1=ALU.mult),
               r=[b_bank[pb], b_w["rope"], b_w["gains"]], w=[b_t1[s]])

        def st_pe():
            OP("pe", lambda e: e.matmul(bank(nb), lhsT=bones_bf[:, :], rhs=sqb[s][:, :], start=True, stop=True),
               r=[b_sq[s], bf("bones")], w=[b_bank[nb]])
            OP("pe", lambda e: e.matmul(bank(nbz), lhsT=Pmat[:, :], rhs=zsb[s][:, :], start=True, stop=True),
               r=[b_zs[s], Pbuf], w=[b_bank[nbz]])

        def st_act2():
            OP("act", lambda e: e.activation(out=bank(nb), in_=bank(nb), func=AF.Ln, bias=eps_t[:, 0:1]),
               r=[b_bank[nb], bf("eps")], w=[b_bank[nb]])
            OP("act", lambda e: e.activation(out=bank(nb), in_=bank(nb), func=AF.Exp, scale=-0.5),
               r=[b_bank[nb]], w=[b_bank[nb]])

        def st_dve1():
            OP("dve", lambda e: e.tensor_tensor(out=bank(nbz), in0=bank(nbz), in1=ropeS[:, tok0:tok0 + 512],
                                                op=ALU.mult),
               r=[b_bank[nbz], b_w["rope"]], w=[b_bank[nbz]])
            OP("dve", lambda e: e.tensor_tensor(out=t1b[s][:, :], in0=t1b[s][:, :], in1=bank(nbz), op=ALU.add),
               r=[b_t1[s], b_bank[nbz]], w=[b_t1[s]])

        def st_dve2():
            if isinstance(dst_ap, tuple):
                for hf, dap in enumerate(dst_ap):
                    ln_ = slice(hf * 64, (hf + 1) * 64)
                    OP("dve", lambda e, ln_=ln_, dap=dap: e.tensor_tensor(out=dap, in0=t1b[s][ln_, :],
                                                                         in1=bank(nb)[ln_, :], op=ALU.mult),
                       r=[b_t1[s], b_bank[nb]], w=dst_bufs)
            else:
                OP("dve", lambda e: e.tensor_tensor(out=dst_ap, in0=t1b[s][:, :], in1=bank(nb), op=ALU.mult),
                   r=[b_t1[s], b_bank[nb]], w=dst_bufs)
        stages = [st_act1, st_dve0, st_pe, st_act2, st_dve1, st_dve2]
        if staged:
            return stages
        for f in stages:
            f()

    def load_layer(l):
        fence([b_cs, bf("win")] + p2_tmp() + b_XS1[2:], [b_w["w_mem"]])
        fence([b_yst[0]], b_hb)
        with nc.allow_non_contiguous_dma(reason="gain vectors"):
            DMA("sp", b_w["gains"], gpre[:, :], norm_pre[l].rearrange("(c p) -> p c", p=128), pw=[b_w["gains"]])
            DMA("sp", b_w["gains"], gmem[:, :], mem_norm[l].rearrange("(c p) -> p c", p=128), pw=[b_w["gains"]])
            DMA("sp", b_w["gains"], pscale[:, :], pool_scale[l].rearrange("(c p) -> p c", p=128),
                pw=[b_w["gains"]])
            for hh in range(2):
                DMA("sp", b_w["gains"], gq[hh * 64:(hh + 1) * 64, :], q_norm[l].rearrange("(p o) -> p o", o=1),
                    pw=[b_w["gains"]])
                DMA("sp", b_w["gains"], gk[hh * 64:(hh + 1) * 64, :], k_norm[l].rearrange("(p o) -> p o", o=1),
                    pw=[b_w["gains"]])
            DMA("sp", b_w["gains"], gpost[:, :], norm_post[l:l + 1, :].to_broadcast([128, D]), pw=[b_w["gains"]])
        OP("dve", lambda e: e.tensor_scalar(out=Pq[:, :], in0=swapP[:, :], scalar1=gq[:, 0:1], scalar2=None,
                                            op0=ALU.mult), r=[bf("swapP"), b_w["gains"]], w=[bf("Pq")])
        OP("dve", lambda e: e.tensor_scalar(out=Pk[:, :], in0=swapP[:, :], scalar1=gk[:, 0:1], scalar2=None,
                                            op0=ALU.mult), r=[bf("swapP"), b_w["gains"]], w=[bf("Pk")])
        fence(b_ub + [b_ubh] + b_ubt + b_ubhs + b_sg + b_mix + b_qx, b_XSW[2:])
        jobs = []
        for c in range(8):
            for (o, n) in [(0, 1024), (1024, 1024), (2048, 256)]:
                jobs.append(("in", c, o, n))
        for c in range(8):
            jobs.append(("mem", c, 0, 512))
        slot_of = {}

        def issue(k):
            kind, c, o, n = jobs[k]
            s_ = nxt("xsw", 6)
            slot_of[k] = s_
            src = w_in[l, c * 128:(c + 1) * 128, o:o + n] if kind == "in" else w_mem[l, c * 128:(c + 1) * 128, :]
            DMA("sp", b_XSW[s_], XSW[s_][:, 0:n], src, w=[b_XSW[s_]])
        for k in range(min(5, len(jobs))):
            issue(k)
        for k, (kind, c, o, n) in enumerate(jobs):
            if k + 5 < len(jobs):
                issue(k + 5)
            s_ = slot_of[k]
            if kind == "in":
                OP("dve", lambda e, s_=s_, c=c, o=o, n=n: e.tensor_scalar(
                    out=w_in_sb[:, c, o:o + n], in0=XSW[s_][:, 0:n], scalar1=gpre[:, c:c + 1], scalar2=None,
                    op0=ALU.mult), r=[b_XSW[s_], b_w["gains"]], w=[b_w["w_in"]])
                if o == 1024:
                    for kvh in range(2):
                        for rep in range(2):
                            OP("dve", lambda e, c=c, kvh=kvh, rep=rep: e.tensor_copy(
                                out=w_krep[:, c, kvh, rep * 64:(rep + 1) * 64],
                                in_=w_in_sb[:, c, C_K + kvh * 64:C_K + (kvh + 1) * 64]),
                               r=[b_w["w_in"]], w=[b_w["w_krep"]])
            else:
                OP("dve", lambda e, s_=s_, c=c: e.tensor_scalar(
                    out=w_mem_sb[:, c, :], in0=XSW[s_][:, 0:512], scalar1=gmem[:, c:c + 1], scalar2=None,
                    op0=ALU.mult), r=[b_XSW[s_], b_w["gains"]], w=[b_w["w_mem"]])
        for c in range(8):
            DMA("pool", b_w["w_out"], w_out_sb[:, c, :], w_out[l, c * 128:(c + 1) * 128, :], pw=[b_w["w_out"]])
        OP("dve", lambda e: e.memset(poolw_sb[:, :, :], 0.0), w=[b_w["poolw"]])
        for g in range(4):
            ti, hf = g // 2, g % 2
            DMA("pool", b_w["poolw"], poolw_sb[hf * 64:(hf + 1) * 64, ti, hf * 64:(hf + 1) * 64],
                pool_w[l, g * 64:(g + 1) * 64, :], pw=[b_w["poolw"]])

    def mem_kv(c):
        for mt in range(2):
            s = nxt("xst", 2)
            DMA("sp", b_xst[s], xst[s][:, :], mems[c * NMEM + mt * 128: c * NMEM + (mt + 1) * 128, :], w=[b_xst[s]])
            rms_stats(xst[s][:, :], 60 + mt, [b_xst[s]], D)
            h = nxt("hb", 2)
            OP("dve", lambda e, s=s, h=h, mt=mt: e.tensor_scalar(
                out=hb[h][:, :], in0=xst[s][:, :], scalar1=rstd[:, 60 + mt:61 + mt], scalar2=None, op0=ALU.mult),
               r=[b_xst[s], b_stat[60 + mt]], w=[b_hb[h]])
            transposes_to(lambda mt=mt: hT[:, :, mt * 128:(mt + 1) * 128], hb[h], b_hb[h], [b_hT[0]])
        for g in range(2):
            pb = proj_group(lambda cc, g=g: w_mem_sb[:, cc, g * 128:(g + 1) * 128],
                            lambda cc: hT[:, cc, 0:NMEM], NMEM, [b_w["w_mem"], b_hT[0]])
            OP("dve", lambda e, pb=pb, g=g: e.tensor_copy(out=kmT[:, c, g, :], in_=bank(pb)[:, 0:NMEM]),
               r=[b_bank[pb]], w=[b_w["kmv"]])
        for mt in range(2):
            pb = proj_group(lambda cc, mt=mt: hT[:, cc, mt * 128:(mt + 1) * 128],
                            lambda cc: w_mem_sb[:, cc, 256:512], 256, [b_w["w_mem"], b_hT[0]])
            dst = vmaug[:, c * 8 + mt * 4: c * 8 + (mt + 1) * 4, 64:128]
            OP("dve", lambda e, pb=pb, dst=dst: e.tensor_copy(
                out=dst, in_=bank(pb)[:, 0:256].rearrange("p (h d) -> p h d", h=4)),
               r=[b_bank[pb]], w=[b_w["kmv"]])

    def x_src(l, c, tile):
        base = c * T + tile * 128
        if l == 0:
            return xs[base:base + 128, :], []
        return x1d.ap()[base:base + 128, :], [b_x1[c]]

    def pass1(l, c, do_kv=True):
        fence([b_yst[0]], b_hb)
        fence(b_sg + b_mix + b_qx + b_XSW[2:], b_ub + [b_ubh] + b_ubt + b_ubhs)
        fence([b_cs, bf("win"), b_w["w_mem"]] + p2_tmp(), b_XS1[2:])
        xslot = {}

        def xload(tile):
            s_ = nxt("xs1", 5)
            xslot[tile] = s_
            src, rd = x_src(l, c, tile)
            DMA("sp", b_XS1[s_], XS1[s_][:, :], src, r=rd, w=[b_XS1[s_]])
        for t_ in range(4):
            xload(t_)
        pending = []
        rot["n"] = 2

        def proj_pieces(pblk):
            ptok0 = pblk * 512
            prd = [b_w["w_in"], b_hT[pblk]]

            def p_u():
                for g in range(2):
                    pb = proj_group(lambda cc, g=g: w_in_sb[:, cc, C_UPOOL + g * 128:C_UPOOL + (g + 1) * 128],
                                    lambda cc: hT[:, cc, ptok0:ptok0 + 512], 512, prd)
                    OP("act", lambda e, pb=pb, g=g: e.activation(out=ub[:, g, HALO + ptok0:HALO + ptok0 + 512],
                                                                 in_=bank(pb), func=AF.Copy),
                       r=[b_bank[pb]], w=[b_ub[pblk], b_ubt[g]])

            def p_k():
                while pending:
                    pending.pop(0)()
                chains = []
                for kvh in range(2):
                    pb = proj_group(lambda cc, kvh=kvh: w_krep[:, cc, kvh, :],
                                    lambda cc: hT[:, cc, ptok0:ptok0 + 512], 512, [b_w["w_krep"], b_hT[pblk]],
                                    pb=2 + kvh)
                    chains.append(qk_norm_rope(pb, Pk, bf("Pk"), gk, ptok0, kT2[:, kvh, ptok0:ptok0 + 512],
                                               [b_kT[pblk]], nbs=(4 + 2 * kvh, 5 + 2 * kvh), staged=True, qs=kvh))
                for st_a, st_b in zip(chains[0], chains[1]):
                    pending.extend([st_a, st_b])

            def p_v():
                pb = nxt("proj", 4)

                def fv(e):
                    i = None
                    for t4 in range(4):
                        for cc in range(8):
                            i = e.matmul(bank(pb)[:, t4 * 128:(t4 + 1) * 128],
                                         lhsT=hT[:, cc, ptok0 + t4 * 128: ptok0 + (t4 + 1) * 128],
                                         rhs=w_in_sb[:, cc, C_V:C_V + 128], start=(cc == 0), stop=(cc == 7))
                    return i
                OP("pe", fv, r=prd, w=[b_bank[pb]])
                dst = vaug[:, pblk * 8:(pblk + 1) * 8, 64:128]
                OP("dve", lambda e: e.tensor_copy(out=dst, in_=bank(pb).rearrange("p (b d) -> p b d", b=8)),
                   r=[b_bank[pb]], w=[b_v[pblk]])
            if do_kv:
                return [p_u, p_k, p_v, lambda: None]
            return [p_u, lambda: None, lambda: None, lambda: None]

        prev = None
        for blk in range(4):
            for tt in range(4):
                tile = blk * 4 + tt
                if tile + 4 < 16:
                    xload(tile + 4)
                s = xslot[tile]
                si = (32 if c == 2 else 0) + tile
                if do_kv or c != 2:
                    rms_stats(XS1[s][:, :], si, [b_XS1[s]], D)
                h = nxt("hb", 2)
                OP("dve", lambda e, s=s, h=h, si=si: e.tensor_scalar(
                    out=hb[h][:, :], in0=XS1[s][:, :], scalar1=rstd[:, si:si + 1], scalar2=None, op0=ALU.mult),
                   r=[b_XS1[s], b_stat[si]], w=[b_hb[h]])
                transposes_to(lambda tile=tile: hT[:, :, tile * 128:(tile + 1) * 128], hb[h], b_hb[h], [b_hT[blk]])
                for _ in range(3):
                    if pending:
                        pending.pop(0)()
                if prev is not None:
                    prev[tt]()
            prev = proj_pieces(blk)
        for f in prev:
            f()
        while pending:
            pending.pop(0)()
        rot["n"] = 4

    def pool_stage(c, setidx):
        fence([b_w["w_mem"]] + p2_tmp() + b_XS1[2:], [b_cs, bf("win")])
        for ti in range(2):
            OP("dve", lambda e, ti=ti: e.tensor_tensor_scan(
                out=cs[:, :], data0=ub[:, ti, :], data1=ub[:, ti, :], initial=0.0, op0=ALU.add, op1=ALU.bypass),
               r=[b_ubt[ti], b_ubhs[ti]], w=[b_cs])
            for hf in range(2):
                w = (2, 4, 8, 16)[ti * 2 + hf]
                lo = HALO - w // 2 - 1
                hi = HALO + w // 2 - 1
                ln = slice(hf * 64, (hf + 1) * 64)
                OP("dve", lambda e, ln=ln, lo=lo, hi=hi, ti=ti: e.tensor_tensor(
                    out=win[ln, :], in0=cs[ln, hi:hi + T], in1=cs[ln, lo:lo + T], op=ALU.subtract),
                   r=[b_cs], w=[bf("win")])
            OP("dve", lambda e, ti=ti: e.scalar_tensor_tensor(
                out=dT[:, ti, :], in0=win[:, :], scalar=invw[:, ti:ti + 1], in1=ub[:, ti, HALO:HALO + T],
                op0=ALU.mult, op1=ALU.subtract), r=[bf("win"), bf("invw"), b_ubt[ti]], w=[b_dT])
            for (e0, t0) in ((0, 0), (HALO, T - HALO)):
                OP("dve", lambda e, ti=ti, e0=e0, t0=t0: e.tensor_tensor(
                    out=etmp[:, ti, e0:e0 + HALO], in0=win[:, t0:t0 + HALO],
                    in1=ptab_sb[:, setidx, ti * 2 * HALO + e0: ti * 2 * HALO + e0 + HALO], op=ALU.mult),
                   r=[bf("win"), b_w["ptab"]], w=[b_w["etmp"]])
                OP("dve", lambda e, ti=ti, e0=e0, t0=t0: e.tensor_tensor(
                    out=dT[:, ti, t0:t0 + HALO], in0=etmp[:, ti, e0:e0 + HALO],
                    in1=ub[:, ti, HALO + t0:HALO + t0 + HALO], op=ALU.subtract),
                   r=[b_w["etmp"], b_ubt[ti]], w=[b_dT])


    def attention_block(heads, hooks=None):
        seq = []
        for hi, hd in enumerate(heads):
            hd["ob"] = 4 + (hi % 2)
            for g in range(hd["nkt"] // 2):
                seq.append((hd, g))
        pt_of = {}

        def emit_S(n):
            hd, g = seq[n]
            sp = nxt("S", 2)
            k_fn, qap = hd["k_fn"], hd["qap"]

            def f(e):
                e.matmul(bank(2 * sp), lhsT=k_fn(2 * g), rhs=qap, start=True, stop=True)
                return e.matmul(bank(2 * sp + 1), lhsT=k_fn(2 * g + 1), rhs=qap, start=True, stop=True)
            OP("pe", f, r=hd["rd_q"] + hd["rd_k"](2 * g) + hd["rd_k"](2 * g + 1),
               w=[b_bank[2 * sp], b_bank[2 * sp + 1]])
            p = nxt("pt", 3)
            pt_of[n] = p
            OP("act", lambda e: e.activation(out=PTP[p], in_=bank(2 * sp, 2), func=AF.Exp, scale=0.125),
               r=[b_bank[2 * sp], b_bank[2 * sp + 1]], w=[b_PT[p]])

        def emit_PV(n):
            hd, g = seq[n]
            p = pt_of[n]
            ob, nkt, odd = hd["ob"], hd["nkt"], hd["odd"]

            def f(e):
                i = None
                for u in range(2):
                    j = 2 * g + u
                    if odd:
                        i = e.matmul(bank(ob), lhsT=hd["v_odd"](j), rhs=PTP[p][:, u * 512:(u + 1) * 512],
                                     start=(j == 0), stop=(j == nkt - 1))
                    else:
                        i = e.matmul(bank(ob), lhsT=hd["v_even"](j), rhs=PTP[p][:, u * 512:(u + 1) * 512],
                                     start=(j == 0), stop=(j == nkt - 1))
                return i
            OP("pe", f, r=[b_PT[p]] + hd["rd_v"](2 * g) + hd["rd_v"](2 * g + 1), w=[b_bank[ob]])

        def tail(hd):
            odd, ob, hidx = hd["odd"], hd["ob"], hd["hidx"]
            sl = 0 if odd else 64
            dl = slice(64, 128) if odd else slice(0, 64)
            nl = 128 if odd else 65
            rbi = nxt("sr", 2)
            OP("dve", lambda e: e.tensor_copy(out=Rb[rbi][0:nl, :], in_=bank(ob)[0:nl, :]), r=[b_bank[ob]],
               w=[b_R[rbi]])
            OP("dve", lambda e: e.tensor_tensor(out=hd["mix_ap"], in0=Rb[rbi][dl, :], in1=hd["sg_ap"], op=ALU.mult),
               r=[b_R[rbi], hd["sg_buf"]], w=[hd["mix_buf"]])
            DMA("sp", bf("srow_dma"), srow_all[hidx:hidx + 1, :], Rb[rbi][sl:sl + 1, :], r=[b_R[rbi]],
                pw=[bf("srow_all")])

        emit_S(0)
        if len(seq) > 1:
            emit_S(1)
        for n in range(len(seq)):
            if n + 2 < len(seq):
                emit_S(n + 2)
            emit_PV(n)
            hd, g = seq[n]
            if g == hd["nkt"] // 2 - 1:
                tail(hd)
            if hooks and n in hooks:
                for f in hooks[n]:
                    f()

    def normalize_block():
        OP("act", lambda e: e.activation(out=srow_all[0:12, :], in_=srow_all[0:12, :], func=AF.Ln),
           r=[bf("srow_all")], w=[bf("srow_all")])
        OP("act", lambda e: e.activation(out=srow_all[0:12, :], in_=srow_all[0:12, :], func=AF.Exp, scale=-1.0),
           r=[bf("srow_all")], w=[bf("srow_all")])
        OP("dve", lambda e: e.tensor_copy(out=rhl[0:12, 0, :], in_=srow_all[0:12, :]),
           r=[bf("srow_all")], w=[bf("rhl")])
        OP("dve", lambda e: e.tensor_tensor(out=rhl[0:12, 1, :], in0=srow_all[0:12, :], in1=rhl[0:12, 0, :],
                                            op=ALU.subtract), r=[bf("srow_all"), bf("rhl")], w=[bf("rhl")])
        for i in range(6):
            bb = 5 + (i % 2)

            def fb(e, i=i, bb=bb):
                e.matmul(bank(bb), lhsT=sel_bf[0:12, i, :], rhs=rhl[0:12, 0, :], start=True, stop=False)
                return e.matmul(bank(bb), lhsT=sel_bf[0:12, i, :], rhs=rhl[0:12, 1, :], start=False, stop=True)
            OP("pe", fb, r=[bf("rhl"), bf("sel")], w=[b_bank[bb]])
            OP("dve", lambda e, i=i, bb=bb: e.tensor_tensor(out=mixT_blk[:, 2 + i, :], in0=mixT_blk[:, 2 + i, :],
                                                           in1=bank(bb), op=ALU.mult),
               r=[b_mix[2 + i], b_bank[bb]], w=[b_mix[2 + i]])

    def pass2(l, c, nkt):
        fence([b_cs, bf("win")], p2_tmp())
        fence(b_ub + [b_ubh, b_ubt[0], b_ubhs[0]], b_sg)
        fence(b_ub + [b_ubh] + b_ubt + b_ubhs, b_mix + b_qx)
        fence(b_hb, [b_yst[0]])
        OP("dve", lambda e: e.memset(qTm[:, :, :], 0.0), w=b_qT)
        gcols = [C_GPOOL, C_GPOOL + 128, C_GATTN, C_GATTN + 128, C_GATTN + 256, C_GATTN + 384, C_GX, C_GX + 128]
        for blk in range(4):
            tok0 = blk * 512
            hrd = [b_w["w_in"], b_hT[blk]]

            def proj_half(col0, pb, half, qb=None):
                qb = blk if qb is None else qb
                qt0 = qb * 512

                def f(e):
                    i_ = None
                    for cc in range(4 * half, 4 * half + 4):
                        i_ = e.matmul(bank(pb), lhsT=w_in_sb[:, cc, col0:col0 + 128], rhs=hT[:, cc, qt0:qt0 + 512],
                                      start=(cc == 0), stop=(cc == 7))
                    return i_
                OP("pe", f, r=[b_w["w_in"], b_hT[qb]], w=[b_bank[pb]])

            def q_chain(i, qb=None):
                qb = blk if qb is None else qb
                col0 = C_Q + i * 128
                st = qk_norm_rope(7, Pq, bf("Pq"), gq, qb * 512, (qTm[0:64, 2 * i, :], qTm[64:128, 2 * i + 1, :]),
                                  [b_qT[i]], nbs=(6, 7), staged=True)
                return [lambda: proj_half(col0, 7, 0, qb), lambda: proj_half(col0, 7, 1, qb)] + st

            def qx_chain(i):
                col0 = C_QX + i * 128

                def cp():
                    OP("act", lambda e: e.activation(out=qxT_blk[:, i, :], in_=bank(7), func=AF.Copy),
                       r=[b_bank[7]], w=[b_qx[i]])
                return [lambda: proj_half(col0, 7, 0), lambda: proj_half(col0, 7, 1), cp]

            def proj_gate(i, pb):
                proj_group(lambda cc: w_in_sb[:, cc, gcols[i]:gcols[i] + 128],
                           lambda cc: hT[:, cc, tok0:tok0 + 512], 512, hrd, pb=pb)
                OP("act", lambda e: e.activation(out=sg_blk[:, i, :], in_=bank(pb), func=AF.Silu),
                   r=[b_bank[pb]], w=[b_sg[i]])

            def pool_mix(ti):
                pb = 7
                OP("pe", lambda e: e.matmul(bank(pb), lhsT=poolw_sb[:, ti, :],
                                            rhs=dT[:, ti, tok0:tok0 + 512], start=True, stop=True),
                   r=[b_w["poolw"], b_dT], w=[b_bank[pb]])
                OP("dve", lambda e: e.scalar_tensor_tensor(
                    out=mixT_blk[:, ti, :], in0=bank(pb), scalar=pscale[:, ti:ti + 1], in1=sg_blk[:, ti, :],
                    op0=ALU.mult, op1=ALU.mult), r=[b_bank[pb], b_sg[ti], b_w["gains"]], w=[b_mix[ti]])

            def head_self(h):
                lane0 = (h % 2) * 64
                kvh = h // 4
                ln = slice(lane0, lane0 + 64)
                return dict(hidx=h, qap=qTm[:, h, :], odd=(lane0 == 64),
                            k_fn=lambda j: kT2[:, kvh, j * 128:(j + 1) * 128], nkt=nkt,
                            v_even=lambda j: vaug_f[:, (j * 2 + kvh) * VB + 64:(j * 2 + kvh) * VB + 192],
                            v_odd=lambda j: vaug[:, j * 2 + kvh, 0:128],
                            sg_ap=sg_blk[ln, 2 + h // 2, :], sg_buf=b_sg[2 + h // 2],
                            mix_ap=mixT_blk[ln, 2 + h // 2, :], mix_buf=b_mix[2 + h // 2],
                            rd_q=[b_qT[h // 2]], rd_k=lambda j: [b_kT[j // 4]], rd_v=lambda j: [b_v[j // 4]])

            def head_mem(hx):
                lane0 = (hx % 2) * 64
                ln = slice(lane0, lane0 + 64)
                return dict(hidx=8 + hx, qap=qxT_blk[ln, hx // 2, :], odd=(lane0 == 64),
                            k_fn=lambda j: kmT[ln, c, hx // 2, j * 128:(j + 1) * 128], nkt=2,
                            v_even=lambda j: vmaug_f[:, (c * 8 + j * 4 + hx) * VB + 64:(c * 8 + j * 4 + hx) * VB + 192],
                            v_odd=lambda j: vmaug[:, c * 8 + j * 4 + hx, 0:128],
                            sg_ap=sg_blk[ln, 6 + hx // 2, :], sg_buf=b_sg[6 + hx // 2],
                            mix_ap=mixT_blk[ln, 6 + hx // 2, :], mix_buf=b_mix[6 + hx // 2],
                            rd_q=[b_qx[hx // 2]], rd_k=lambda j: [b_w["kmv"]], rd_v=lambda j: [b_w["kmv"]])

            xpre = {}
            for tt in range(2):
                s_ = nxt("xst", 2)
                src, rd = x_src(l, c, blk * 4 + tt)
                DMA("sp", b_xst[s_], xst[s_][:, :], src, r=rd, w=[b_xst[s_]])
                xpre[tt] = s_
            for i in range(8):
                proj_gate(i, 5 + (i % 3))
            pool_mix(0)
            pool_mix(1)
            if blk == 0:
                for f in q_chain(0):
                    f()

            npair = nkt // 2
            hooks = {}

            def spread(stages, first):
                for k, f in enumerate(stages):
                    hooks.setdefault(first + k, []).append(f)
            spread(q_chain(1), 0)
            spread(qx_chain(0), npair)
            spread(q_chain(2), 2 * npair)
            spread(qx_chain(1), 3 * npair)
            spread(q_chain(3), 4 * npair)
            if blk < 3:
                spread(q_chain(0, blk + 1), 5 * npair + 1)
            hs = [head_self(h) for h in range(8)]
            hm = [head_mem(hx) for hx in range(4)]
            attention_block(hs[0:5] + [hm[0], hs[5], hm[1], hs[6], hm[2], hs[7], hm[3]], hooks)
            normalize_block()
            for tt in range(4):
                tile = blk * 4 + tt

                ob0 = (6, 2)[tt % 2]

                def fo(e, tt=tt, ob0=ob0):
                    i = None
                    for half in range(2):
                        for cc in range(8):
                            i = e.matmul(bank(ob0 + half), lhsT=mixT_blk[:, cc, tt * 128:(tt + 1) * 128],
                                         rhs=w_out_sb[:, cc, half * 512:(half + 1) * 512], start=(cc == 0),
                                         stop=(cc == 7))
                    return i
                OP("pe", fo, r=b_mix + [b_w["w_out"]], w=[b_bank[ob0], b_bank[ob0 + 1]])
                yps = bank(ob0, 2)
                sidx = 16 + tile
                rms_stats(yps, sidx, [b_bank[ob0], b_bank[ob0 + 1]], D)
                if tt in xpre:
                    s = xpre[tt]
                else:
                    s = nxt("xst", 2)
                    src, rd = x_src(l, c, tile)
                    DMA("sp", b_xst[s], xst[s][:, :], src, r=rd, w=[b_xst[s]])
                y = nxt("yst", 2)
                OP("dve", lambda e, y=y, sidx=sidx, yps=yps: e.scalar_tensor_tensor(
                    out=yst[y][:, :], in0=yps, scalar=rstd[:, sidx:sidx + 1], in1=gpost[:, :], op0=ALU.mult,
                    op1=ALU.mult), r=[b_bank[ob0], b_bank[ob0 + 1], b_stat[sidx], b_w["gains"]], w=[b_yst[y]])
                OP("dve", lambda e, y=y, s=s: e.tensor_tensor(out=yst[y][:, :], in0=yst[y][:, :], in1=xst[s][:, :],
                                                              op=ALU.add), r=[b_yst[y], b_xst[s]], w=[b_yst[y]])
                base = c * T + tile * 128
                if l == nlayers - 1:
                    DMA("sp", b_yst[y], yout[base:base + 128, :], yst[y][:, :], r=[b_yst[y]], pw=[bf("yout")])
                else:
                    DMA("sp", b_yst[y], x1d.ap()[base:base + 128, :], yst[y][:, :], r=[b_yst[y]], pw=[b_x1[c]])

    def load_rope(setidx):
        DMA("pool", b_w["rope"], ropeC[:, :], rope[setidx, 0], pw=[b_w["rope"]])
        DMA("pool", b_w["rope"], ropeS[:, :], rope[setidx, 1], pw=[b_w["rope"]])

    def zero_halos():
        fence(b_sg + b_mix + b_qx, b_ub + [b_ubh] + b_ubt + b_ubhs)
        OP("dve", lambda e: e.memset(ub[:, :, 0:HALO], 0.0), w=[b_ubh] + b_ubhs)
        OP("dve", lambda e: e.memset(ub[:, :, HALO + T:UBW], 0.0), w=[b_ubh] + b_ubhs)

    DMA("sp", b_w["ptab"], ptab_sb[:, :, :], ptab.rearrange("s p n -> p s n"), pw=[b_w["ptab"]])

    with nc.allow_low_precision(reason="bf16 matmul operands, fp32 accumulation"):
        for l in range(nlayers):
            load_layer(l)
            for c in range(NCH):
                mem_kv(c)
            load_rope(1)
            pass1(l, 2, do_kv=True)
            OP("dve", lambda e: e.tensor_copy(out=halo_st[:, :, 0:HALO], in_=ub[:, :, HALO:2 * HALO]),
               r=b_ub, w=[b_w["halo_st"]])
            OP("dve", lambda e: e.tensor_copy(out=halo_st[:, :, HALO:2 * HALO], in_=ub[:, :, T:T + HALO]),
               r=b_ub, w=[b_w["halo_st"]])
            xin = xin_d[l].ap()
            xout = xout_d[l].ap()
            with nc.allow_non_contiguous_dma(reason="exchange payload"):
                for kvh in range(2):
                    DMA("pool", b_xin[l], xin[kvh * 64:(kvh + 1) * 64, XK0:XK0 + T], kT2[0:64, kvh, 0:T],
                        r=b_kT[0:4], pw=[b_xin[l]])
                DMA("pool", b_xin[l], xin[:, XV0:XV0 + XVW], vaug[:, 0:32, :].rearrange("p b c -> p (b c)"), r=b_v[0:4],
                    pw=[b_xin[l]])
                DMA("pool", b_xin[l], xin[:, XH0:XW], halo_st[:, :, :].rearrange("p a b -> p (a b)"),
                    r=[b_w["halo_st"]], pw=[b_xin[l]])
            tr._collect("pool", [b_xin[l]], [b_xout[l]], True)
            E = tr.eng["pool"]
            inst = nc.gpsimd.collective_compute(
                "AllGather", ALU.bypass, replica_groups=[[2 * i, 2 * i + 1] for i in range(ncores // 2)],
                ins=[xin.opt()], outs=[xout.opt()])
            E["cnt"] += 1
            inst.then_inc(E["sem"], 1)
            tr._record(E["key"], (E["sem"], E["cnt"], None), [b_xin[l]], [b_xout[l]])
            for c in range(2):
                load_rope(0)
                zero_halos()
                pass1(l, c)
                pool_stage(c, 0)
                pass2(l, c, 16)
            load_rope(1)
            pass1(l, 2, do_kv=False)
            with nc.allow_non_contiguous_dma(reason="exchange payload"):
                for rnk in range(2):
                    for kvh in range(2):
                        for rep in range(2):
                            DMA("sp", bf("kvload"),
                                kT2[rep * 64:(rep + 1) * 64, kvh, rnk * T:(rnk + 1) * T],
                                xout[rnk * 128 + kvh * 64: rnk * 128 + (kvh + 1) * 64, XK0:XK0 + T],
                                r=[b_xout[l]], pw=b_kT)
                    DMA("sp", bf("kvload"), vaug[:, rnk * 32:(rnk + 1) * 32, :].rearrange("p b c -> p (b c)"),
                        xout[rnk * 128:(rnk + 1) * 128, XV0:XV0 + XVW], r=[b_xout[l]], pw=b_v)
                DMA("sp", b_w["halo_in"], halo_in[:, :, 0:2 * HALO],
                    xout[0:128, XH0:XW].rearrange("p (a b) -> p a b", a=2), r=[b_xout[l]], pw=[b_w["halo_in"]])
                DMA("sp", b_w["halo_in"], halo_in[:, :, 2 * HALO:4 * HALO],
                    xout[128:256, XH0:XW].rearrange("p (a b) -> p a b", a=2), r=[b_xout[l]], pw=[b_w["halo_in"]])
            OP("dve", lambda e: e.tensor_scalar(out=ub[:, :, 0:HALO], in0=halo_in[:, :, HALO:2 * HALO],
                                                scalar1=hmask_sb[:, 0:1], scalar2=None, op0=ALU.mult),
               r=[b_w["halo_in"], b_w["gains"]], w=[b_ubh] + b_ubhs)
            OP("dve", lambda e: e.tensor_scalar(out=ub[:, :, HALO + T:UBW], in0=halo_in[:, :, 2 * HALO:3 * HALO],
                                                scalar1=hmask_sb[:, 1:2], scalar2=None, op0=ALU.mult),
               r=[b_w["halo_in"], b_w["gains"]], w=[b_ubh] + b_ubhs)
            pool_stage(2, 1)
            pass2(l, 2, 32)
        tr.wait_all("sp", [bf("yout")] + b_yst)
    return nc, tr


def _rope_tables(pos):
    pos = np.asarray(pos, dtype=np.float64)
    row = np.floor(pos / 64.0)
    col = pos - 64.0 * row
    freqs = 10000.0 ** (-np.arange(16, dtype=np.float64) / 16.0)
    cosT = np.zeros((128, len(pos)), np.float32)
    sinT = np.zeros((128, len(pos)), np.float32)
    for lane in range(128):
        d = lane % 64
        axis, ab, p = d // 32, (d % 32) // 16, d % 16
        ang = (row if axis == 0 else col) * freqs[p]
        cosT[lane] = np.cos(ang)
        sinT[lane] = (-1.0 if ab == 0 else 1.0) * np.sin(ang)
    return cosT, sinT


def _pool_tab(t_glob, tseq):
    tab = np.zeros((128, 2, 2 * HALO), np.float32)
    for ti in range(2):
        for lane in range(128):
            w = (2, 4, 8, 16)[ti * 2 + lane // 64]
            for j, tg in enumerate(t_glob):
                lo = min(max(tg - w // 2, 0), tseq)
                hi = min(max(tg + w - w // 2, 0), tseq)
                tab[lane, ti, j] = 1.0 / float(hi - lo)
    return tab


def _consts():
    ident = np.eye(128, dtype=np.float32)
    swap = np.zeros((128, 128), np.float32)
    for m in range(128):
        swap[m ^ 16, m] = 1.0
    bones = np.zeros((128, 128), np.float32)
    bones[:64, :64] = 1.0 / 64.0
    bones[64:, 64:] = 1.0 / 64.0
    return np.stack([ident, swap, bones], 0)


def _selm():
    m = np.zeros((12, 6, 128), np.float32)
    for h in range(12):
        m[h, h // 2, (h % 2) * 64:(h % 2 + 1) * 64] = 1.0
    return m.reshape(12, 768)


_PROG = {}


def kernel(x_prompt, x_sample, mem_prompt, mem_sample, norm_pre, norm_post, w_in, pool_w, pool_scale,
           q_norm, k_norm, mem_norm, w_mem_kv, w_out):
    f32 = lambda a: np.ascontiguousarray(np.asarray(a, dtype=np.float32))
    x_prompt, x_sample, mem_prompt, mem_sample = map(f32, (x_prompt, x_sample, mem_prompt, mem_sample))
    if "nc" not in _PROG:
        _PROG["nc"], _PROG["tr"] = build_program()
    nc = _PROG["nc"]
    in_maps = _prep_inputs(x_prompt, x_sample, mem_prompt, mem_sample, norm_pre, norm_post, w_in, pool_w, pool_scale,
                           q_norm, k_norm, mem_norm, w_mem_kv, w_out)
    res = run_bass_kernel_spmd(nc, in_maps, core_ids=list(range(NCORES)))
    y_prompt = np.empty_like(x_prompt)
    y_sample = np.empty_like(x_sample)
    for c in range(NCORES):
        y = res.results[c]["y"]
        y_prompt[2 * c] = y[0:T]
        y_prompt[2 * c + 1] = y[T:2 * T]
        y_sample[c // 2, (c % 2) * T:(c % 2 + 1) * T] = y[2 * T:3 * T]
    return (y_prompt, y_sample)


def _prep_inputs(x_prompt, x_sample, mem_prompt, mem_sample, norm_pre, norm_post, w_in, pool_w, pool_scale,
                 q_norm, k_norm, mem_norm, w_mem_kv, w_out, ncores=NCORES):
    f32 = lambda a: np.ascontiguousarray(np.asarray(a, dtype=np.float32))
    cm = _consts()
    shared = {
        "w_in": f32(w_in), "w_out": f32(w_out), "w_mem": f32(w_mem_kv),
        "pool_w": f32(pool_w).reshape(2, 256, 64), "norm_pre": f32(norm_pre), "norm_post": f32(norm_post),
        "mem_norm": f32(mem_norm), "pool_scale": f32(pool_scale), "q_norm": f32(q_norm), "k_norm": f32(k_norm),
        "cmat": cm, "selm": _selm(),
    }
    cp, sp_ = _rope_tables(np.arange(T))
    tab_p = _pool_tab(list(range(HALO)) + list(range(T - HALO, T)), T)
    in_maps = []
    for c in range(ncores):
        sq, half = c // 2, c % 2
        t0 = half * T
        xs = np.concatenate([x_prompt[2 * c], x_prompt[2 * c + 1], x_sample[sq, t0:t0 + T]], 0)
        mm = np.concatenate([mem_prompt[2 * c], mem_prompt[2 * c + 1], mem_sample[sq]], 0)
        cs_, ss_ = _rope_tables(np.arange(t0, t0 + T))
        rope = np.stack([np.stack([cp, sp_], 0), np.stack([cs_, ss_], 0)], 0)
        tab_s = _pool_tab(list(range(t0, t0 + HALO)) + list(range(t0 + T - HALO, t0 + T)), 2 * T)
        ptab = np.stack([tab_p.reshape(128, -1), tab_s.reshape(128, -1)], 0)
        hm = np.zeros((128, 2), np.float32)
        hm[:, 0] = 1.0 if half == 1 else 0.0
        hm[:, 1] = 1.0 if half == 0 else 0.0
        m = dict(shared)
        m.update({"xs": np.ascontiguousarray(xs), "mems": np.ascontiguousarray(mm),
                  "rope": np.ascontiguousarray(rope.astype(np.float32)), "ptab": np.ascontiguousarray(ptab),
                  "hmask": hm})
        in_maps.append(m)
    return in_maps
```

```python
import numpy as np
import ml_dtypes
import concourse.bass as bass
import concourse.mybir as mybir
from concourse.bass_utils import run_bass_kernel_spmd

F32 = mybir.dt.float32
BF16 = mybir.dt.bfloat16
AF = mybir.ActivationFunctionType
ALU = mybir.AluOpType

NCORES = 8
D = 1024
T = 2048
NCH = 3
NMEM = 256
INW = 2304
EPS = 1e-6
HALO = 16
UBW = T + 2 * HALO
VB = 129
XW = 2 * T // 2 + 0

C_UPOOL, C_GPOOL, C_Q, C_K, C_V, C_GATTN, C_QX, C_GX = 0, 256, 512, 1024, 1152, 1280, 1792, 2048

XK0 = 0
XV0 = T
XVW = 16 * 2 * VB
XH0 = XV0 + XVW
XW = XH0 + 2 * 2 * HALO


class Buf:
    __slots__ = ("name", "w", "r", "sem", "semcnt", "excl")

    def __init__(self, name, excl=False):
        self.name = name
        self.excl = excl
        self.w = {}
        self.r = {}
        self.sem = None
        self.semcnt = 0


class Tracker:
    def __init__(self, nc):
        self.nc = nc
        self.eng = {}
        for n, e in (("pe", nc.tensor), ("act", nc.scalar), ("dve", nc.vector),
                     ("pool", nc.gpsimd), ("sp", nc.sync)):
            sem = nc.alloc_semaphore(name=f"sem_{n}")
            self.eng[n] = {"e": e, "sem": sem, "key": f"sem_{n}", "cnt": 0, "seen": {}}
        self.nwaits = 0
        self.nops = 0

    def _collect(self, en, reads, writes, is_dma, pwrites=(), pkey=None):
        E = self.eng[en]
        need = {}

        def add(key, ev, raw):
            h, val, owner = ev
            if owner is not None:
                val = owner.semcnt
            if not is_dma and key == E["key"] and en == "pe":
                return
            cur = need.get(key)
            if cur is None or cur[1] < val:
                need[key] = (h, val)

        for b in reads:
            for key, ev in b.w.items():
                add(key, ev, True)
        for b in writes:
            for key, ev in b.w.items():
                add(key, ev, False)
            for key, ev in b.r.items():
                add(key, ev, False)
        for b in pwrites:
            for key, ev in b.r.items():
                add(key, ev, False)
            for key, ev in b.w.items():
                if key != pkey:
                    add(key, ev, False)
        for key, (h, val) in need.items():
            if E["seen"].get(key, 0) < val:
                E["e"].wait_ge(h, val)
                E["seen"][key] = val
                self.nwaits += 1

    def _record(self, key, ev, reads, writes, pwrites=()):
        for b in reads:
            b.r[key] = ev
        for b in writes:
            b.w = {key: ev}
            b.r = {}
        for b in pwrites:
            b.w[key] = ev
            b.r = {}

    def op(self, en, fn, r=(), w=()):
        E = self.eng[en]
        if any(b.excl for b in r):
            w = list(w) + [b for b in r if b.excl]
            r = [b for b in r if not b.excl]
        self._collect(en, r, w, False)
        inst = fn(E["e"])
        E["cnt"] += 1
        inst.then_inc(E["sem"], 1)
        self._record(E["key"], (E["sem"], E["cnt"], None), r, w)
        self.nops += 1

    def dma(self, q, owner, out, in_, r=(), w=(), pw=(), **kw):
        E = self.eng[q]
        if owner.sem is None:
            owner.sem = self.nc.alloc_semaphore(name=f"dsem_{owner.name}")
        self._collect(q, r, w, True, pw, f"dsem_{owner.name}")
        inst = E["e"].dma_start(out=out, in_=in_, **kw)
        owner.semcnt += 16
        inst.then_inc(owner.sem, 16)
        self._record(f"dsem_{owner.name}", (owner.sem, owner.semcnt, owner), r, w, pw)
        self.nops += 1

    def wait_all(self, en, bufs):
        self._collect(en, [], bufs, True)


def build_program(nlayers=2, ncores=NCORES):
    nc = bass.Bass("TRN2", target_bir_lowering=False)
    tr = Tracker(nc)

    def dram_in(name, shape, dt=F32):
        return nc.dram_tensor(name, list(shape), dt, kind="ExternalInput").ap()

    xs = dram_in("xs", [NCH * T, D])
    mems = dram_in("mems", [NCH * NMEM, D])
    w_in = dram_in("w_in", [2, D, INW])
    w_out = dram_in("w_out", [2, D, D])
    w_mem = dram_in("w_mem", [2, D, 512])
    pool_w = dram_in("pool_w", [2, 256, 64])
    norm_pre = dram_in("norm_pre", [2, D])
    norm_post = dram_in("norm_post", [2, D])
    mem_norm = dram_in("mem_norm", [2, D])
    pool_scale = dram_in("pool_scale", [2, 256])
    q_norm = dram_in("q_norm", [2, 64])
    k_norm = dram_in("k_norm", [2, 64])
    rope = dram_in("rope", [2, 2, 128, T])
    ptab = dram_in("ptab", [2, 128, 2 * 2 * HALO])
    hmask = dram_in("hmask", [128, 2])
    cmat = dram_in("cmat", [3, 128, 128])
    selm = dram_in("selm", [12, 6 * 128])
    yout = nc.dram_tensor("y", [NCH * T, D], F32, kind="ExternalOutput").ap()
    x1d = nc.dram_tensor("x1_scratch", [NCH * T, D], F32)
    xin_d = [nc.dram_tensor(f"xin{l}", [128, XW], BF16) for l in range(2)]
    xout_d = [nc.dram_tensor(f"xout{l}", [256, XW], BF16) for l in range(2)]
    b_x1 = [Buf(f"x1_{c}") for c in range(NCH)]
    b_xin = [Buf(f"xin{l}") for l in range(2)]
    b_xout = [Buf(f"xout{l}") for l in range(2)]

    def sb(name, shape, dt):
        return nc.alloc_sbuf_tensor(name, list(shape), dt)

    w_in_sb = sb("w_in_sb", [128, 8, INW], BF16)
    w_out_sb = sb("w_out_sb", [128, 8, D], BF16)
    w_krep = sb("w_krep", [128, 8, 2, 128], BF16)
    poolw_sb = sb("poolw_sb", [128, 2, 128], BF16)
    hT = sb("hT", [128, 8, T], BF16)
    def view(region, byte_off, shape, dt):
        esz = 4 if dt == F32 else 2
        n = 1
        for d_ in shape[1:]:
            n *= d_
        a = region[:, byte_off // 2: byte_off // 2 + n * esz // 2]
        if dt == F32:
            a = a.bitcast(F32)
        if len(shape) == 3:
            a = a.rearrange("p (a b) -> p a b", a=shape[1])
        return a

    regX = sb("regX", [128, 18432 // 2], BF16)
    regY = sb("regY", [128, 16512 // 2], BF16)
    regZ = sb("regZ", [128, 4096 // 2], BF16)
    ub = view(regX, 0, [128, 2, UBW], F32)
    sg_blk = view(regX, 0, [128, 8, 512], BF16)
    mixT_blk = view(regX, 8192, [128, 8, 512], BF16)
    qxT_blk = view(regX, 16384, [128, 2, 512], BF16)
    w_mem_sb = view(regY, 0, [128, 8, 512], BF16)
    cs = view(regY, 0, [128, UBW], F32)
    win = view(regY, 8320, [128, T], F32)
    qTm = view(regY, 0, [128, 8, 512], BF16)
    kT2 = sb("kT2", [128, 2, 2 * T], BF16)
    vaug_f = sb("vaug", [128, 64 * VB + 64], BF16)
    vaug = vaug_f[:, 0:64 * VB].rearrange("p (b c) -> p b c", c=VB)
    xst = [sb(f"xst{i}", [128, D], F32) for i in range(2)]
    yst = [view(regZ, 0, [128, D], F32)] * 2
    NPT = 3
    PTP = [view(regY, 8192, [128, 1024], BF16), view(regY, 10240, [128, 1024], BF16),
           view(regY, 14336, [128, 1024], BF16)]
    ropeC = sb("ropeC", [128, T], BF16)
    ropeS = sb("ropeS", [128, T], BF16)
    dT = sb("dT", [128, 2, T], BF16)
    kmT = sb("kmT", [128, NCH, 2, NMEM], BF16)
    vmaug_f = sb("vmaug", [128, NCH * 8 * VB + 64], BF16)
    vmaug = vmaug_f[:, 0:NCH * 8 * VB].rearrange("p (b c) -> p b c", c=VB)
    hb = [view(regZ, i * 2048, [128, D], BF16) for i in range(2)]
    ident_bf = sb("ident_bf", [128, 128], BF16)
    bones_bf = sb("bones_bf", [128, 128], BF16)
    swapP = sb("swapP", [128, 128], F32)
    Pq = sb("Pq", [128, 128], BF16)
    Pk = sb("Pk", [128, 128], BF16)
    ones_bf = sb("ones_bf", [128, 64], BF16)
    gq = sb("gq", [128, 1], F32)
    gk = sb("gk", [128, 1], F32)
    gpre = sb("gpre", [128, 8], F32)
    gmem = sb("gmem", [128, 8], F32)
    gpost = sb("gpost", [128, D], F32)
    pscale = sb("pscale", [128, 2], F32)
    invw = sb("invw", [128, 2], F32)
    hmask_sb = sb("hmask_sb", [128, 2], F32)
    ptab_sb = sb("ptab_sb", [128, 2, 2 * 2 * HALO], F32)
    halo_st = sb("halo_st", [128, 2, 2 * HALO], BF16)
    halo_in = sb("halo_in", [128, 2, 2 * 2 * HALO], BF16)
    etmp = sb("etmp", [128, 2, 2 * HALO], F32)
    ss = sb("ss", [128, 64], F32)
    lnv = sb("lnv", [128, 64], F32)
    rstd = sb("rstd", [128, 64], F32)
    NQS = 1
    sqb = [sb(f"sqb{i}", [128, 512], BF16) for i in range(NQS)]
    zsb = [sb(f"zsb{i}", [128, 512], BF16) for i in range(NQS)]
    t1b = [sb(f"t1b{i}", [128, 512], F32) for i in range(NQS)]
    sqb.append(view(regY, 12288, [128, 512], BF16))
    zsb.append(view(regY, 13312, [128, 512], BF16))
    t1b.append(view(regY, 14336, [128, 512], F32))
    srow_all = view(regY, 12288, [128, 512], F32)
    rhl = view(regY, 14336, [128, 2, 512], BF16)
    sel_bf = sb("sel_bf", [12, 6, 128], BF16)
    Rb0 = sb("Rb0", [128, 512], F32)
    junk = Rb0[:, :].bitcast(BF16)
    Rb = [Rb0, t1b[0]]

    ps_all = nc.alloc_psum_tensor("ps_all", [128, 8 * 512], F32)

    def bank(b, n=1):
        return ps_all[:, b * 512:(b + n) * 512]

    B = {}

    def bf(name):
        if name not in B:
            B[name] = Buf(name)
        return B[name]

    b_bank = [bf(f"bank{i}") for i in range(8)]
    for b_ in b_bank:
        b_.excl = True
    b_xst = [bf(f"xst{i}") for i in range(2)]
    b_yst = [bf("yst0")] * 2
    b_hb = [bf(f"hb{i}") for i in range(2)]
    b_PT = [bf("PT0"), bf("PT1"), bf("rhl")]
    b_hT = [bf(f"hT{i}") for i in range(4)]
    b_kT = [bf(f"kT{i}") for i in range(8)]
    b_v = [bf(f"v{i}") for i in range(8)]
    b_ub = [bf(f"ub{i}") for i in range(4)]
    b_ubh = bf("ubh")
    b_ubt = [bf("ubt0"), bf("ubt1")]
    b_ubhs = [bf("ubh0"), bf("ubh1")]
    b_dT = bf("dT")
    b_cs = bf("cs")
    b_qT = [bf(f"qT{i}") for i in range(4)]
    b_qz = bf("qzero")
    b_qx = [bf(f"qx{i}") for i in range(2)]
    b_sg = [bf(f"sg{i}") for i in range(8)]
    b_mix = [bf(f"mix{i}") for i in range(8)]
    b_w = {n: bf(n) for n in ("w_in", "w_out", "w_mem", "w_krep", "poolw", "consts", "gains",
                              "rope", "ptab", "kmv", "stat", "halo_st", "halo_in", "etmp", "junk")}
    b_stat = [bf(f"stat{i}") for i in range(64)]
    b_sq = [bf(f"sq{i}") for i in range(NQS)]
    b_zs = [bf(f"zs{i}") for i in range(NQS)]
    b_t1 = [bf(f"t1{i}") for i in range(NQS)]
    b_sq.append(bf("sq_1"))
    b_zs.append(bf("zs_1"))
    b_t1.append(bf("t1_1"))
    b_gt = []
    b_srow = [bf("srow_all"), bf("rhl")]
    b_R = [bf("R0"), b_t1[0]]
    b_tb = []

    OP = tr.op
    DMA = tr.dma
    XS1 = [xst[0], xst[1]] + [view(regY, i * 4096, [128, D], F32) for i in range(3)]
    b_XS1 = [b_xst[0], b_xst[1]] + [bf(f"xsY{i}") for i in range(3)] + [b_sq[1], b_zs[1], b_t1[1]]
    XSW = [xst[0], xst[1]] + [view(regX, i * 4096, [128, D], F32) for i in range(4)]
    b_XSW = [b_xst[0], b_xst[1]] + [bf(f"xsX{i}") for i in range(4)]

    def fence(A, Bs):
        for a in A:
            for src in (a.w, a.r):
                for key, ev in src.items():
                    for b in Bs:
                        cur = b.r.get(key)
                        if cur is None or cur[1] < ev[1]:
                            b.r[key] = ev

    def p2_tmp():
        return b_qT + b_PT + b_gt + b_srow + b_tb
    rr = {"proj": 0, "nrm": 0, "qs": 0, "gt": 0, "pt": 0, "S": 0, "O": 0, "xst": 0, "yst": 0, "hb": 0,
          "sr": 0, "xs1": 0, "xsw": 0}

    def nxt(k, n):
        if k == "proj":
            n = rot["n"]
        v = rr[k] % n
        rr[k] = (v + 1) % n
        return v
    rot = {"n": 4}

    cst = xst[0][:, 0:384].rearrange("p (k m) -> p k m", k=3)
    with nc.allow_non_contiguous_dma(reason="small constant / gain loads"):
        DMA("sp", b_xst[0], cst, cmat.rearrange("k p m -> p k m"), w=[b_xst[0]])
        DMA("sp", b_w["ptab"], hmask_sb[:, :], hmask, pw=[b_w["gains"]])
    OP("dve", lambda e: e.tensor_copy(out=ident_bf[:, :], in_=cst[:, 0, :]), r=[b_xst[0]], w=[bf("ident")])
    OP("dve", lambda e: e.tensor_copy(out=swapP[:, :], in_=cst[:, 1, :]), r=[b_xst[0]], w=[bf("swapP")])
    OP("dve", lambda e: e.tensor_copy(out=bones_bf[:, :], in_=cst[:, 2, :]), r=[b_xst[0]], w=[bf("bones")])
    OP("dve", lambda e: e.memset(ones_bf[:, :], 1.0), w=[bf("ones")])
    DMA("pool", bf("sel"), sel_bf[:, :, :].rearrange("p a b -> p (a b)"), selm, w=[bf("sel")])
    OP("dve", lambda e: e.memset(invw[0:64, 0:1], 0.5), w=[bf("invw")])
    OP("dve", lambda e: e.memset(invw[64:128, 0:1], 0.25), w=[bf("invw")])
    OP("dve", lambda e: e.memset(invw[0:64, 1:2], 0.125), w=[bf("invw")])
    OP("dve", lambda e: e.memset(invw[64:128, 1:2], 0.0625), w=[bf("invw")])
    OP("dve", lambda e: e.memset(vaug_f[:, :], 0.0), w=b_v)
    OP("dve", lambda e: e.memset(vaug[:, :, 0:1], 1.0), w=b_v)
    OP("dve", lambda e: e.memset(vaug[:, :, 128:129], 1.0), w=b_v)
    OP("dve", lambda e: e.memset(vmaug_f[:, :], 0.0), w=[b_w["kmv"]])
    OP("dve", lambda e: e.memset(vmaug[:, :, 0:1], 1.0), w=[b_w["kmv"]])
    OP("dve", lambda e: e.memset(vmaug[:, :, 128:129], 1.0), w=[b_w["kmv"]])

    def rms_stats(src_ap, idx, rd, nfree, extra_w=()):
        OP("act", lambda e: e.activation(out=junk[:, 0:nfree], in_=src_ap, func=AF.Square,
                                         accum_out=ss[:, idx:idx + 1]),
           r=rd, w=[b_R[0], b_stat[idx]])
        OP("act", lambda e: e.activation(out=lnv[:, idx:idx + 1], in_=ss[:, idx:idx + 1], func=AF.Ln,
                                         scale=1.0 / nfree, bias=eps_t[:, 0:1]),
           r=[b_stat[idx], bf("eps")], w=[b_stat[idx]])
        OP("act", lambda e: e.activation(out=rstd[:, idx:idx + 1], in_=lnv[:, idx:idx + 1], func=AF.Exp,
                                         scale=-0.5),
           r=[b_stat[idx]], w=[b_stat[idx]])

    eps_t = sb("eps_t", [128, 1], F32)
    OP("dve", lambda e: e.memset(eps_t[:, :], EPS), w=[bf("eps")])

    def transposes_to(dst_ap_fn, src_tile, src_buf, dst_bufs, nchunk=8):
        pb = nxt("proj", 4)
        psb = bank(pb).bitcast(BF16)

        def f(e):
            i = None
            for c in range(nchunk):
                i = e.transpose(out=psb[:, c * 128:(c + 1) * 128], in_=src_tile[:, c * 128:(c + 1) * 128],
                                identity=ident_bf[:, :])
            return i
        OP("pe", f, r=[src_buf, bf("ident")], w=[b_bank[pb]])
        OP("dve", lambda e: e.tensor_copy(out=dst_ap_fn(),
                                          in_=psb[:, 0:nchunk * 128].rearrange("p (c t) -> p c t", c=nchunk)),
           r=[b_bank[pb]], w=dst_bufs)

    def proj_group(lhs_fn, rhs_fn, n, rd, nk=8, pb=None):
        if pb is None:
            pb = nxt("proj", 4)

        def f(e):
            i = None
            for c in range(nk):
                i = e.matmul(bank(pb)[:, 0:n], lhsT=lhs_fn(c), rhs=rhs_fn(c), start=(c == 0), stop=(c == nk - 1))
            return i
        OP("pe", f, r=rd, w=[b_bank[pb]])
        return pb

    def qk_norm_rope(pb, Pmat, Pbuf, gvec, tok0, dst_ap, dst_bufs, nbs=None, staged=False, qs=0):
        s = qs
        if nbs is None:
            nb = 4 + 2 * nxt("nrm", 2)
            nbz = nb + 1
        else:
            nb, nbz = nbs
        z = bank(pb)

        def st_act1():
            OP("act", lambda e: e.activation(out=sqb[s][:, :], in_=z, func=AF.Square), r=[b_bank[pb]], w=[b_sq[s]])
            OP("act", lambda e: e.activation(out=zsb[s][:, :], in_=z, func=AF.Copy), r=[b_bank[pb]], w=[b_zs[s]])

        def st_dve0():
            OP("dve", lambda e: e.scalar_tensor_tensor(out=t1b[s][:, :], in0=z, scalar=gvec[:, 0:1],
                                                       in1=ropeC[:, tok0:tok0 + 512], op0=ALU.mult, op1=ALU.mult),
               r=[b_bank[pb], b_w["rope"], b_w["gains"]], w=[b_t1[s]])

        def st_pe():
            OP("pe", lambda e: e.matmul(bank(nb), lhsT=bones_bf[:, :], rhs=sqb[s][:, :], start=True, stop=True),
               r=[b_sq[s], bf("bones")], w=[b_bank[nb]])
            OP("pe", lambda e: e.matmul(bank(nbz), lhsT=Pmat[:, :], rhs=zsb[s][:, :], start=True, stop=True),
               r=[b_zs[s], Pbuf], w=[b_bank[nbz]])

        def st_act2():
            OP("act", lambda e: e.activation(out=bank(nb), in_=bank(nb), func=AF.Ln, bias=eps_t[:, 0:1]),
               r=[b_bank[nb], bf("eps")], w=[b_bank[nb]])
            OP("act", lambda e: e.activation(out=bank(nb), in_=bank(nb), func=AF.Exp, scale=-0.5),
               r=[b_bank[nb]], w=[b_bank[nb]])

        def st_dve1():
            OP("dve", lambda e: e.tensor_tensor(out=bank(nbz), in0=bank(nbz), in1=ropeS[:, tok0:tok0 + 512],
                                                op=ALU.mult),
               r=[b_bank[nbz], b_w["rope"]], w=[b_bank[nbz]])
            OP("dve", lambda e: e.tensor_tensor(out=t1b[s][:, :], in0=t1b[s][:, :], in1=bank(nbz), op=ALU.add),
               r=[b_t1[s], b_bank[nbz]], w=[b_t1[s]])

        def st_dve2():
            if isinstance(dst_ap, tuple):
                for hf, dap in enumerate(dst_ap):
                    ln_ = slice(hf * 64, (hf + 1) * 64)
                    OP("dve", lambda e, ln_=ln_, dap=dap: e.tensor_tensor(out=dap, in0=t1b[s][ln_, :],
                                                                         in1=bank(nb)[ln_, :], op=ALU.mult),
                       r=[b_t1[s], b_bank[nb]], w=dst_bufs)
            else:
                OP("dve", lambda e: e.tensor_tensor(out=dst_ap, in0=t1b[s][:, :], in1=bank(nb), op=ALU.mult),
                   r=[b_t1[s], b_bank[nb]], w=dst_bufs)
        stages = [st_act1, st_dve0, st_pe, st_act2, st_dve1, st_dve2]
        if staged:
            return stages
        for f in stages:
            f()

    def load_layer(l):
        fence([b_cs, bf("win")] + p2_tmp() + b_XS1[2:], [b_w["w_mem"]])
        fence([b_yst[0]], b_hb)
        with nc.allow_non_contiguous_dma(reason="gain vectors"):
            DMA("sp", b_w["gains"], gpre[:, :], norm_pre[l].rearrange("(c p) -> p c", p=128), pw=[b_w["gains"]])
            DMA("sp", b_w["gains"], gmem[:, :], mem_norm[l].rearrange("(c p) -> p c", p=128), pw=[b_w["gains"]])
            DMA("sp", b_w["gains"], pscale[:, :], pool_scale[l].rearrange("(c p) -> p c", p=128),
                pw=[b_w["gains"]])
            for hh in range(2):
                DMA("sp", b_w["gains"], gq[hh * 64:(hh + 1) * 64, :], q_norm[l].rearrange("(p o) -> p o", o=1),
                    pw=[b_w["gains"]])
                DMA("sp", b_w["gains"], gk[hh * 64:(hh + 1) * 64, :], k_norm[l].rearrange("(p o) -> p o", o=1),
                    pw=[b_w["gains"]])
            DMA("sp", b_w["gains"], gpost[:, :], norm_post[l:l + 1, :].to_broadcast([128, D]), pw=[b_w["gains"]])
        OP("dve", lambda e: e.tensor_scalar(out=Pq[:, :], in0=swapP[:, :], scalar1=gq[:, 0:1], scalar2=None,
                                            op0=ALU.mult), r=[bf("swapP"), b_w["gains"]], w=[bf("Pq")])
        OP("dve", lambda e: e.tensor_scalar(out=Pk[:, :], in0=swapP[:, :], scalar1=gk[:, 0:1], scalar2=None,
                                            op0=ALU.mult), r=[bf("swapP"), b_w["gains"]], w=[bf("Pk")])
        fence(b_ub + [b_ubh] + b_ubt + b_ubhs + b_sg + b_mix + b_qx, b_XSW[2:])
        jobs = []
        for c in range(8):
            for (o, n) in [(0, 1024), (1024, 1024), (2048, 256)]:
                jobs.append(("in", c, o, n))
        for c in range(8):
            jobs.append(("mem", c, 0, 512))
        slot_of = {}

        def issue(k):
            kind, c, o, n = jobs[k]
            s_ = nxt("xsw", 6)
            slot_of[k] = s_
            src = w_in[l, c * 128:(c + 1) * 128, o:o + n] if kind == "in" else w_mem[l, c * 128:(c + 1) * 128, :]
            DMA("sp", b_XSW[s_], XSW[s_][:, 0:n], src, w=[b_XSW[s_]])
        for k in range(min(5, len(jobs))):
            issue(k)
        for k, (kind, c, o, n) in enumerate(jobs):
            if k + 5 < len(jobs):
                issue(k + 5)
            s_ = slot_of[k]
            if kind == "in":
                OP("dve", lambda e, s_=s_, c=c, o=o, n=n: e.tensor_scalar(
                    out=w_in_sb[:, c, o:o + n], in0=XSW[s_][:, 0:n], scalar1=gpre[:, c:c + 1], scalar2=None,
                    op0=ALU.mult), r=[b_XSW[s_], b_w["gains"]], w=[b_w["w_in"]])
                if o == 1024:
                    for kvh in range(2):
                        for rep in range(2):
                            OP("dve", lambda e, c=c, kvh=kvh, rep=rep: e.tensor_copy(
                                out=w_krep[:, c, kvh, rep * 64:(rep + 1) * 64],
                                in_=w_in_sb[:, c, C_K + kvh * 64:C_K + (kvh + 1) * 64]),
                               r=[b_w["w_in"]], w=[b_w["w_krep"]])
            else:
                OP("dve", lambda e, s_=s_, c=c: e.tensor_scalar(
                    out=w_mem_sb[:, c, :], in0=XSW[s_][:, 0:512], scalar1=gmem[:, c:c + 1], scalar2=None,
                    op0=ALU.mult), r=[b_XSW[s_], b_w["gains"]], w=[b_w["w_mem"]])
        for c in range(8):
            DMA("pool", b_w["w_out"], w_out_sb[:, c, :], w_out[l, c * 128:(c + 1) * 128, :], pw=[b_w["w_out"]])
        OP("dve", lambda e: e.memset(poolw_sb[:, :, :], 0.0), w=[b_w["poolw"]])
        for g in range(4):
            ti, hf = g // 2, g % 2
            DMA("pool", b_w["poolw"], poolw_sb[hf * 64:(hf + 1) * 64, ti, hf * 64:(hf + 1) * 64],
                pool_w[l, g * 64:(g + 1) * 64, :], pw=[b_w["poolw"]])

    def mem_kv(c):
        for mt in range(2):
            s = nxt("xst", 2)
            DMA("sp", b_xst[s], xst[s][:, :], mems[c * NMEM + mt * 128: c * NMEM + (mt + 1) * 128, :], w=[b_xst[s]])
            rms_stats(xst[s][:, :], 60 + mt, [b_xst[s]], D)
            h = nxt("hb", 2)
            OP("dve", lambda e, s=s, h=h, mt=mt: e.tensor_scalar(
                out=hb[h][:, :], in0=xst[s][:, :], scalar1=rstd[:, 60 + mt:61 + mt], scalar2=None, op0=ALU.mult),
               r=[b_xst[s], b_stat[60 + mt]], w=[b_hb[h]])
            transposes_to(lambda mt=mt: hT[:, :, mt * 128:(mt + 1) * 128], hb[h], b_hb[h], [b_hT[0]])
        for g in range(2):
            pb = proj_group(lambda cc, g=g: w_mem_sb[:, cc, g * 128:(g + 1) * 128],
                            lambda cc: hT[:, cc, 0:NMEM], NMEM, [b_w["w_mem"], b_hT[0]])
            OP("dve", lambda e, pb=pb, g=g: e.tensor_copy(out=kmT[:, c, g, :], in_=bank(pb)[:, 0:NMEM]),
               r=[b_bank[pb]], w=[b_w["kmv"]])
        for mt in range(2):
            pb = proj_group(lambda cc, mt=mt: hT[:, cc, mt * 128:(mt + 1) * 128],
                            lambda cc: w_mem_sb[:, cc, 256:512], 256, [b_w["w_mem"], b_hT[0]])
            dst = vmaug[:, c * 8 + mt * 4: c * 8 + (mt + 1) * 4, 64:128]
            OP("dve", lambda e, pb=pb, dst=dst: e.tensor_copy(
                out=dst, in_=bank(pb)[:, 0:256].rearrange("p (h d) -> p h d", h=4)),
               r=[b_bank[pb]], w=[b_w["kmv"]])

    def x_src(l, c, tile):
        base = c * T + tile * 128
        if l == 0:
            return xs[base:base + 128, :], []
        return x1d.ap()[base:base + 128, :], [b_x1[c]]

    def pass1(l, c, do_kv=True):
        fence([b_yst[0]], b_hb)
        fence(b_sg + b_mix + b_qx + b_XSW[2:], b_ub + [b_ubh] + b_ubt + b_ubhs)
        fence([b_cs, bf("win"), b_w["w_mem"]] + p2_tmp(), b_XS1[2:])
        xslot = {}

        def xload(tile):
            s_ = nxt("xs1", 5)
            xslot[tile] = s_
            src, rd = x_src(l, c, tile)
            DMA("sp", b_XS1[s_], XS1[s_][:, :], src, r=rd, w=[b_XS1[s_]])
        for t_ in range(4):
            xload(t_)
        pending = []
        rot["n"] = 2

        def proj_pieces(pblk):
            ptok0 = pblk * 512
            prd = [b_w["w_in"], b_hT[pblk]]

            def p_u():
                for g in range(2):
                    pb = proj_group(lambda cc, g=g: w_in_sb[:, cc, C_UPOOL + g * 128:C_UPOOL + (g + 1) * 128],
                                    lambda cc: hT[:, cc, ptok0:ptok0 + 512], 512, prd)
                    OP("act", lambda e, pb=pb, g=g: e.activation(out=ub[:, g, HALO + ptok0:HALO + ptok0 + 512],
                                                                 in_=bank(pb), func=AF.Copy),
                       r=[b_bank[pb]], w=[b_ub[pblk], b_ubt[g]])

            def p_k():
                while pending:
                    pending.pop(0)()
                chains = []
                for kvh in range(2):
                    pb = proj_group(lambda cc, kvh=kvh: w_krep[:, cc, kvh, :],
                                    lambda cc: hT[:, cc, ptok0:ptok0 + 512], 512, [b_w["w_krep"], b_hT[pblk]],
                                    pb=2 + kvh)
                    chains.append(qk_norm_rope(pb, Pk, bf("Pk"), gk, ptok0, kT2[:, kvh, ptok0:ptok0 + 512],
                                               [b_kT[pblk]], nbs=(4 + 2 * kvh, 5 + 2 * kvh), staged=True, qs=kvh))
                for st_a, st_b in zip(chains[0], chains[1]):
                    pending.extend([st_a, st_b])

            def p_v():
                pb = nxt("proj", 4)

                def fv(e):
                    i = None
                    for t4 in range(4):
                        for cc in range(8):
                            i = e.matmul(bank(pb)[:, t4 * 128:(t4 + 1) * 128],
                                         lhsT=hT[:, cc, ptok0 + t4 * 128: ptok0 + (t4 + 1) * 128],
                                         rhs=w_in_sb[:, cc, C_V:C_V + 128], start=(cc == 0), stop=(cc == 7))
                    return i
                OP("pe", fv, r=prd, w=[b_bank[pb]])
                dst = vaug[:, pblk * 8:(pblk + 1) * 8, 64:128]
                OP("dve", lambda e: e.tensor_copy(out=dst, in_=bank(pb).rearrange("p (b d) -> p b d", b=8)),
                   r=[b_bank[pb]], w=[b_v[pblk]])
            if do_kv:
                return [p_u, p_k, p_v, lambda: None]
            return [p_u, lambda: None, lambda: None, lambda: None]

        prev = None
        for blk in range(4):
            for tt in range(4):
                tile = blk * 4 + tt
                if tile + 4 < 16:
                    xload(tile + 4)
                s = xslot[tile]
                si = (32 if c == 2 else 0) + tile
                if do_kv or c != 2:
                    rms_stats(XS1[s][:, :], si, [b_XS1[s]], D)
                h = nxt("hb", 2)
                OP("dve", lambda e, s=s, h=h, si=si: e.tensor_scalar(
                    out=hb[h][:, :], in0=XS1[s][:, :], scalar1=rstd[:, si:si + 1], scalar2=None, op0=ALU.mult),
                   r=[b_XS1[s], b_stat[si]], w=[b_hb[h]])
                transposes_to(lambda tile=tile: hT[:, :, tile * 128:(tile + 1) * 128], hb[h], b_hb[h], [b_hT[blk]])
                for _ in range(3):
                    if pending:
                        pending.pop(0)()
                if prev is not None:
                    prev[tt]()
            prev = proj_pieces(blk)
        for f in prev:
            f()
        while pending:
            pending.pop(0)()
        rot["n"] = 4

    def pool_stage(c, setidx):
        fence([b_w["w_mem"]] + p2_tmp() + b_XS1[2:], [b_cs, bf("win")])
        for ti in range(2):
            OP("dve", lambda e, ti=ti: e.tensor_tensor_scan(
                out=cs[:, :], data0=ub[:, ti, :], data1=ub[:, ti, :], initial=0.0, op0=ALU.add, op1=ALU.bypass),
               r=[b_ubt[ti], b_ubhs[ti]], w=[b_cs])
            for hf in range(2):
                w = (2, 4, 8, 16)[ti * 2 + hf]
                lo = HALO - w // 2 - 1
                hi = HALO + w // 2 - 1
                ln = slice(hf * 64, (hf + 1) * 64)
                OP("dve", lambda e, ln=ln, lo=lo, hi=hi, ti=ti: e.tensor_tensor(
                    out=win[ln, :], in0=cs[ln, hi:hi + T], in1=cs[ln, lo:lo + T], op=ALU.subtract),
                   r=[b_cs], w=[bf("win")])
            OP("dve", lambda e, ti=ti: e.scalar_tensor_tensor(
                out=dT[:, ti, :], in0=win[:, :], scalar=invw[:, ti:ti + 1], in1=ub[:, ti, HALO:HALO + T],
                op0=ALU.mult, op1=ALU.subtract), r=[bf("win"), bf("invw"), b_ubt[ti]], w=[b_dT])
            for (e0, t0) in ((0, 0), (HALO, T - HALO)):
                OP("dve", lambda e, ti=ti, e0=e0, t0=t0: e.tensor_tensor(
                    out=etmp[:, ti, e0:e0 + HALO], in0=win[:, t0:t0 + HALO],
                    in1=ptab_sb[:, setidx, ti * 2 * HALO + e0: ti * 2 * HALO + e0 + HALO], op=ALU.mult),
                   r=[bf("win"), b_w["ptab"]], w=[b_w["etmp"]])
                OP("dve", lambda e, ti=ti, e0=e0, t0=t0: e.tensor_tensor(
                    out=dT[:, ti, t0:t0 + HALO], in0=etmp[:, ti, e0:e0 + HALO],
                    in1=ub[:, ti, HALO + t0:HALO + t0 + HALO], op=ALU.subtract),
                   r=[b_w["etmp"], b_ubt[ti]], w=[b_dT])


    def attention_block(heads, hooks=None):
        seq = []
        for hi, hd in enumerate(heads):
            hd["ob"] = 4 + (hi % 2)
            for g in range(hd["nkt"] // 2):
                seq.append((hd, g))
        pt_of = {}

        def emit_S(n):
            hd, g = seq[n]
            sp = nxt("S", 2)
            k_fn, qap = hd["k_fn"], hd["qap"]

            def f(e):
                e.matmul(bank(2 * sp), lhsT=k_fn(2 * g), rhs=qap, start=True, stop=True)
                return e.matmul(bank(2 * sp + 1), lhsT=k_fn(2 * g + 1), rhs=qap, start=True, stop=True)
            OP("pe", f, r=hd["rd_q"] + hd["rd_k"](2 * g) + hd["rd_k"](2 * g + 1),
               w=[b_bank[2 * sp], b_bank[2 * sp + 1]])
            p = nxt("pt", 3)
            pt_of[n] = p
            OP("act", lambda e: e.activation(out=PTP[p], in_=bank(2 * sp, 2), func=AF.Exp, scale=0.125),
               r=[b_bank[2 * sp], b_bank[2 * sp + 1]], w=[b_PT[p]])

        def emit_PV(n):
            hd, g = seq[n]
            p = pt_of[n]
            ob, nkt, odd = hd["ob"], hd["nkt"], hd["odd"]

            def f(e):
                i = None
                for u in range(2):
                    j = 2 * g + u
                    if odd:
                        i = e.matmul(bank(ob), lhsT=hd["v_odd"](j), rhs=PTP[p][:, u * 512:(u + 1) * 512],
                                     start=(j == 0), stop=(j == nkt - 1))
                    else:
                        i = e.matmul(bank(ob), lhsT=hd["v_even"](j), rhs=PTP[p][:, u * 512:(u + 1) * 512],
                                     start=(j == 0), stop=(j == nkt - 1))
                return i
            OP("pe", f, r=[b_PT[p]] + hd["rd_v"](2 * g) + hd["rd_v"](2 * g + 1), w=[b_bank[ob]])

        def tail(hd):
            odd, ob, hidx = hd["odd"], hd["ob"], hd["hidx"]
            sl = 0 if odd else 64
            dl = slice(64, 128) if odd else slice(0, 64)
            nl = 128 if odd else 65
            rbi = nxt("sr", 2)
            OP("dve", lambda e: e.tensor_copy(out=Rb[rbi][0:nl, :], in_=bank(ob)[0:nl, :]), r=[b_bank[ob]],
               w=[b_R[rbi]])
            OP("dve", lambda e: e.tensor_tensor(out=hd["mix_ap"], in0=Rb[rbi][dl, :], in1=hd["sg_ap"], op=ALU.mult),
               r=[b_R[rbi], hd["sg_buf"]], w=[hd["mix_buf"]])
            DMA("sp", bf("srow_dma"), srow_all[hidx:hidx + 1, :], Rb[rbi][sl:sl + 1, :], r=[b_R[rbi]],
                pw=[bf("srow_all")])

        emit_S(0)
        if len(seq) > 1:
            emit_S(1)
        for n in range(len(seq)):
            if n + 2 < len(seq):
                emit_S(n + 2)
            emit_PV(n)
            hd, g = seq[n]
            if g == hd["nkt"] // 2 - 1:
                tail(hd)
            if hooks and n in hooks:
                for f in hooks[n]:
                    f()

    def normalize_block():
        OP("act", lambda e: e.activation(out=srow_all[0:12, :], in_=srow_all[0:12, :], func=AF.Ln),
           r=[bf("srow_all")], w=[bf("srow_all")])
        OP("act", lambda e: e.activation(out=srow_all[0:12, :], in_=srow_all[0:12, :], func=AF.Exp, scale=-1.0),
           r=[bf("srow_all")], w=[bf("srow_all")])
        OP("dve", lambda e: e.tensor_copy(out=rhl[0:12, 0, :], in_=srow_all[0:12, :]),
           r=[bf("srow_all")], w=[bf("rhl")])
        OP("dve", lambda e: e.tensor_tensor(out=rhl[0:12, 1, :], in0=srow_all[0:12, :], in1=rhl[0:12, 0, :],
                                            op=ALU.subtract), r=[bf("srow_all"), bf("rhl")], w=[bf("rhl")])
        for i in range(6):
            bb = 5 + (i % 2)

            def fb(e, i=i, bb=bb):
                e.matmul(bank(bb), lhsT=sel_bf[0:12, i, :], rhs=rhl[0:12, 0, :], start=True, stop=False)
                return e.matmul(bank(bb), lhsT=sel_bf[0:12, i, :], rhs=rhl[0:12, 1, :], start=False, stop=True)
            OP("pe", fb, r=[bf("rhl"), bf("sel")], w=[b_bank[bb]])
            OP("dve", lambda e, i=i, bb=bb: e.tensor_tensor(out=mixT_blk[:, 2 + i, :], in0=mixT_blk[:, 2 + i, :],
                                                           in1=bank(bb), op=ALU.mult),
               r=[b_mix[2 + i], b_bank[bb]], w=[b_mix[2 + i]])

    def pass2(l, c, nkt):
        fence([b_cs, bf("win")], p2_tmp())
        fence(b_ub + [b_ubh, b_ubt[0], b_ubhs[0]], b_sg)
        fence(b_ub + [b_ubh] + b_ubt + b_ubhs, b_mix + b_qx)
        fence(b_hb, [b_yst[0]])
        OP("dve", lambda e: e.memset(qTm[:, :, :], 0.0), w=b_qT)
        gcols = [C_GPOOL, C_GPOOL + 128, C_GATTN, C_GATTN + 128, C_GATTN + 256, C_GATTN + 384, C_GX, C_GX + 128]
        for blk in range(4):
            tok0 = blk * 512
            hrd = [b_w["w_in"], b_hT[blk]]

            def proj_half(col0, pb, half, qb=None):
                qb = blk if qb is None else qb
                qt0 = qb * 512

                def f(e):
                    i_ = None
                    for cc in range(4 * half, 4 * half + 4):
                        i_ = e.matmul(bank(pb), lhsT=w_in_sb[:, cc, col0:col0 + 128], rhs=hT[:, cc, qt0:qt0 + 512],
                                      start=(cc == 0), stop=(cc == 7))
                    return i_
                OP("pe", f, r=[b_w["w_in"], b_hT[qb]], w=[b_bank[pb]])

            def q_chain(i, qb=None):
                qb = blk if qb is None else qb
                col0 = C_Q + i * 128
                st = qk_norm_rope(7, Pq, bf("Pq"), gq, qb * 512, (qTm[0:64, 2 * i, :], qTm[64:128, 2 * i + 1, :]),
                                  [b_qT[i]], nbs=(6, 7), staged=True)
                return [lambda: proj_half(col0, 7, 0, qb), lambda: proj_half(col0, 7, 1, qb)] + st

            def qx_chain(i):
                col0 = C_QX + i * 128

                def cp():
                    OP("act", lambda e: e.activation(out=qxT_blk[:, i, :], in_=bank(7), func=AF.Copy),
                       r=[b_bank[7]], w=[b_qx[i]])
                return [lambda: proj_half(col0, 7, 0), lambda: proj_half(col0, 7, 1), cp]

            def gate_pe(gb, i, pb):
                gt0 = gb * 512
                proj_group(lambda cc: w_in_sb[:, cc, gcols[i]:gcols[i] + 128],
                           lambda cc: hT[:, cc, gt0:gt0 + 512], 512, [b_w["w_in"], b_hT[gb]], pb=pb)

            def gate_act(i, pb):
                OP("act", lambda e: e.activation(out=sg_blk[:, i, :], in_=bank(pb), func=AF.Silu),
                   r=[b_bank[pb]], w=[b_sg[i]])

            def proj_gate(i, pb):
                gate_pe(blk, i, pb)
                gate_act(i, pb)

            def pool_mix(ti):
                pb = 7
                OP("pe", lambda e: e.matmul(bank(pb), lhsT=poolw_sb[:, ti, :],
                                            rhs=dT[:, ti, tok0:tok0 + 512], start=True, stop=True),
                   r=[b_w["poolw"], b_dT], w=[b_bank[pb]])
                OP("dve", lambda e: e.scalar_tensor_tensor(
                    out=mixT_blk[:, ti, :], in0=bank(pb), scalar=pscale[:, ti:ti + 1], in1=sg_blk[:, ti, :],
                    op0=ALU.mult, op1=ALU.mult), r=[b_bank[pb], b_sg[ti], b_w["gains"]], w=[b_mix[ti]])

            def head_self(h):
                lane0 = (h % 2) * 64
                kvh = h // 4
                ln = slice(lane0, lane0 + 64)
                return dict(hidx=h, qap=qTm[:, h, :], odd=(lane0 == 64),
                            k_fn=lambda j: kT2[:, kvh, j * 128:(j + 1) * 128], nkt=nkt,
                            v_even=lambda j: vaug_f[:, (j * 2 + kvh) * VB + 64:(j * 2 + kvh) * VB + 192],
                            v_odd=lambda j: vaug[:, j * 2 + kvh, 0:128],
                            sg_ap=sg_blk[ln, 2 + h // 2, :], sg_buf=b_sg[2 + h // 2],
                            mix_ap=mixT_blk[ln, 2 + h // 2, :], mix_buf=b_mix[2 + h // 2],
                            rd_q=[b_qT[h // 2]], rd_k=lambda j: [b_kT[j // 4]], rd_v=lambda j: [b_v[j // 4]])

            def head_mem(hx):
                lane0 = (hx % 2) * 64
                ln = slice(lane0, lane0 + 64)
                return dict(hidx=8 + hx, qap=qxT_blk[ln, hx // 2, :], odd=(lane0 == 64),
                            k_fn=lambda j: kmT[ln, c, hx // 2, j * 128:(j + 1) * 128], nkt=2,
                            v_even=lambda j: vmaug_f[:, (c * 8 + j * 4 + hx) * VB + 64:(c * 8 + j * 4 + hx) * VB + 192],
                            v_odd=lambda j: vmaug[:, c * 8 + j * 4 + hx, 0:128],
                            sg_ap=sg_blk[ln, 6 + hx // 2, :], sg_buf=b_sg[6 + hx // 2],
                            mix_ap=mixT_blk[ln, 6 + hx // 2, :], mix_buf=b_mix[6 + hx // 2],
                            rd_q=[b_qx[hx // 2]], rd_k=lambda j: [b_w["kmv"]], rd_v=lambda j: [b_w["kmv"]])

            xpre = {}
            for tt in range(2):
                s_ = nxt("xst", 2)
                src, rd = x_src(l, c, blk * 4 + tt)
                DMA("sp", b_xst[s_], xst[s_][:, :], src, r=rd, w=[b_xst[s_]])
                xpre[tt] = s_
            for i in (range(8) if blk == 0 else range(4, 8)):
                proj_gate(i, 5 + (i % 3))
            pool_mix(0)
            pool_mix(1)
            if blk == 0:
                for f in q_chain(0):
                    f()

            npair = nkt // 2
            hooks = {}

            def spread(stages, first):
                for k, f in enumerate(stages):
                    hooks.setdefault(first + k, []).append(f)
            spread(q_chain(1), 0)
            spread(qx_chain(0), npair)
            spread(q_chain(2), 2 * npair)
            spread(qx_chain(1), 3 * npair)
            spread(q_chain(3), 4 * npair)
            if blk < 3:
                spread(q_chain(0, blk + 1), 5 * npair + 1)
            hs = [head_self(h) for h in range(8)]
            hm = [head_mem(hx) for hx in range(4)]
            attention_block(hs[0:5] + [hm[0], hs[5], hm[1], hs[6], hm[2], hs[7], hm[3]], hooks)
            normalize_block()
            for tt in range(4):
                tile = blk * 4 + tt

                ob0 = (6, 2)[tt % 2]

                def fo(e, tt=tt, ob0=ob0):
                    i = None
                    for half in range(2):
                        for cc in range(8):
                            i = e.matmul(bank(ob0 + half), lhsT=mixT_blk[:, cc, tt * 128:(tt + 1) * 128],
                                         rhs=w_out_sb[:, cc, half * 512:(half + 1) * 512], start=(cc == 0),
                                         stop=(cc == 7))
                    return i
                OP("pe", fo, r=b_mix + [b_w["w_out"]], w=[b_bank[ob0], b_bank[ob0 + 1]])
                if blk < 3:
                    gate_pe(blk + 1, tt, (0, 1, 4, 5)[tt])
                yps = bank(ob0, 2)
                sidx = 16 + tile
                rms_stats(yps, sidx, [b_bank[ob0], b_bank[ob0 + 1]], D)
                if tt in xpre:
                    s = xpre[tt]
                else:
                    s = nxt("xst", 2)
                    src, rd = x_src(l, c, tile)
                    DMA("sp", b_xst[s], xst[s][:, :], src, r=rd, w=[b_xst[s]])
                y = nxt("yst", 2)
                OP("dve", lambda e, y=y, sidx=sidx, yps=yps: e.scalar_tensor_tensor(
                    out=yst[y][:, :], in0=yps, scalar=rstd[:, sidx:sidx + 1], in1=gpost[:, :], op0=ALU.mult,
                    op1=ALU.mult), r=[b_bank[ob0], b_bank[ob0 + 1], b_stat[sidx], b_w["gains"]], w=[b_yst[y]])
                OP("dve", lambda e, y=y, s=s: e.tensor_tensor(out=yst[y][:, :], in0=yst[y][:, :], in1=xst[s][:, :],
                                                              op=ALU.add), r=[b_yst[y], b_xst[s]], w=[b_yst[y]])
                base = c * T + tile * 128
                if l == nlayers - 1:
                    DMA("sp", b_yst[y], yout[base:base + 128, :], yst[y][:, :], r=[b_yst[y]], pw=[bf("yout")])
                else:
                    DMA("sp", b_yst[y], x1d.ap()[base:base + 128, :], yst[y][:, :], r=[b_yst[y]], pw=[b_x1[c]])
            if blk < 3:
                for i in range(4):
                    gate_act(i, (0, 1, 4, 5)[i])

    def load_rope(setidx):
        DMA("pool", b_w["rope"], ropeC[:, :], rope[setidx, 0], pw=[b_w["rope"]])
        DMA("pool", b_w["rope"], ropeS[:, :], rope[setidx, 1], pw=[b_w["rope"]])

    def zero_halos():
        fence(b_sg + b_mix + b_qx, b_ub + [b_ubh] + b_ubt + b_ubhs)
        OP("dve", lambda e: e.memset(ub[:, :, 0:HALO], 0.0), w=[b_ubh] + b_ubhs)
        OP("dve", lambda e: e.memset(ub[:, :, HALO + T:UBW], 0.0), w=[b_ubh] + b_ubhs)

    DMA("sp", b_w["ptab"], ptab_sb[:, :, :], ptab.rearrange("s p n -> p s n"), pw=[b_w["ptab"]])

    with nc.allow_low_precision(reason="bf16 matmul operands, fp32 accumulation"):
        for l in range(nlayers):
            load_layer(l)
            for c in range(NCH):
                mem_kv(c)
            load_rope(1)
            pass1(l, 2, do_kv=True)
            OP("dve", lambda e: e.tensor_copy(out=halo_st[:, :, 0:HALO], in_=ub[:, :, HALO:2 * HALO]),
               r=b_ub, w=[b_w["halo_st"]])
            OP("dve", lambda e: e.tensor_copy(out=halo_st[:, :, HALO:2 * HALO], in_=ub[:, :, T:T + HALO]),
               r=b_ub, w=[b_w["halo_st"]])
            xin = xin_d[l].ap()
            xout = xout_d[l].ap()
            with nc.allow_non_contiguous_dma(reason="exchange payload"):
                for kvh in range(2):
                    DMA("pool", b_xin[l], xin[kvh * 64:(kvh + 1) * 64, XK0:XK0 + T], kT2[0:64, kvh, 0:T],
                        r=b_kT[0:4], pw=[b_xin[l]])
                DMA("pool", b_xin[l], xin[:, XV0:XV0 + XVW], vaug[:, 0:32, :].rearrange("p b c -> p (b c)"), r=b_v[0:4],
                    pw=[b_xin[l]])
                DMA("pool", b_xin[l], xin[:, XH0:XW], halo_st[:, :, :].rearrange("p a b -> p (a b)"),
                    r=[b_w["halo_st"]], pw=[b_xin[l]])
            tr._collect("pool", [b_xin[l]], [b_xout[l]], True)
            E = tr.eng["pool"]
            inst = nc.gpsimd.collective_compute(
                "AllGather", ALU.bypass, replica_groups=[[2 * i, 2 * i + 1] for i in range(ncores // 2)],
                ins=[xin.opt()], outs=[xout.opt()])
            E["cnt"] += 1
            inst.then_inc(E["sem"], 1)
            tr._record(E["key"], (E["sem"], E["cnt"], None), [b_xin[l]], [b_xout[l]])
            for c in range(2):
                load_rope(0)
                zero_halos()
                pass1(l, c)
                pool_stage(c, 0)
                pass2(l, c, 16)
            load_rope(1)
            pass1(l, 2, do_kv=False)
            with nc.allow_non_contiguous_dma(reason="exchange payload"):
                for rnk in range(2):
                    for kvh in range(2):
                        for rep in range(2):
                            DMA("sp", bf("kvload"),
                                kT2[rep * 64:(rep + 1) * 64, kvh, rnk * T:(rnk + 1) * T],
                                xout[rnk * 128 + kvh * 64: rnk * 128 + (kvh + 1) * 64, XK0:XK0 + T],
                                r=[b_xout[l]], pw=b_kT)
                    DMA("sp", bf("kvload"), vaug[:, rnk * 32:(rnk + 1) * 32, :].rearrange("p b c -> p (b c)"),
                        xout[rnk * 128:(rnk + 1) * 128, XV0:XV0 + XVW], r=[b_xout[l]], pw=b_v)
                DMA("sp", b_w["halo_in"], halo_in[:, :, 0:2 * HALO],
                    xout[0:128, XH0:XW].rearrange("p (a b) -> p a b", a=2), r=[b_xout[l]], pw=[b_w["halo_in"]])
                DMA("sp", b_w["halo_in"], halo_in[:, :, 2 * HALO:4 * HALO],
                    xout[128:256, XH0:XW].rearrange("p (a b) -> p a b", a=2), r=[b_xout[l]], pw=[b_w["halo_in"]])
            OP("dve", lambda e: e.tensor_scalar(out=ub[:, :, 0:HALO], in0=halo_in[:, :, HALO:2 * HALO],
                                                scalar1=hmask_sb[:, 0:1], scalar2=None, op0=ALU.mult),
               r=[b_w["halo_in"], b_w["gains"]], w=[b_ubh] + b_ubhs)
            OP("dve", lambda e: e.tensor_scalar(out=ub[:, :, HALO + T:UBW], in0=halo_in[:, :, 2 * HALO:3 * HALO],
                                                scalar1=hmask_sb[:, 1:2], scalar2=None, op0=ALU.mult),
               r=[b_w["halo_in"], b_w["gains"]], w=[b_ubh] + b_ubhs)
            pool_stage(2, 1)
            pass2(l, 2, 32)
        tr.wait_all("sp", [bf("yout")] + b_yst)
    return nc, tr


def _rope_tables(pos):
    pos = np.asarray(pos, dtype=np.float64)
    row = np.floor(pos / 64.0)
    col = pos - 64.0 * row
    freqs = 10000.0 ** (-np.arange(16, dtype=np.float64) / 16.0)
    cosT = np.zeros((128, len(pos)), np.float32)
    sinT = np.zeros((128, len(pos)), np.float32)
    for lane in range(128):
        d = lane % 64
        axis, ab, p = d // 32, (d % 32) // 16, d % 16
        ang = (row if axis == 0 else col) * freqs[p]
        cosT[lane] = np.cos(ang)
        sinT[lane] = (-1.0 if ab == 0 else 1.0) * np.sin(ang)
    return cosT, sinT


def _pool_tab(t_glob, tseq):
    tab = np.zeros((128, 2, 2 * HALO), np.float32)
    for ti in range(2):
        for lane in range(128):
            w = (2, 4, 8, 16)[ti * 2 + lane // 64]
            for j, tg in enumerate(t_glob):
                lo = min(max(tg - w // 2, 0), tseq)
                hi = min(max(tg + w - w // 2, 0), tseq)
                tab[lane, ti, j] = 1.0 / float(hi - lo)
    return tab


def _consts():
    ident = np.eye(128, dtype=np.float32)
    swap = np.zeros((128, 128), np.float32)
    for m in range(128):
        swap[m ^ 16, m] = 1.0
    bones = np.zeros((128, 128), np.float32)
    bones[:64, :64] = 1.0 / 64.0
    bones[64:, 64:] = 1.0 / 64.0
    return np.stack([ident, swap, bones], 0)


def _selm():
    m = np.zeros((12, 6, 128), np.float32)
    for h in range(12):
        m[h, h // 2, (h % 2) * 64:(h % 2 + 1) * 64] = 1.0
    return m.reshape(12, 768)


_PROG = {}


def kernel(x_prompt, x_sample, mem_prompt, mem_sample, norm_pre, norm_post, w_in, pool_w, pool_scale,
           q_norm, k_norm, mem_norm, w_mem_kv, w_out):
    f32 = lambda a: np.ascontiguousarray(np.asarray(a, dtype=np.float32))
    x_prompt, x_sample, mem_prompt, mem_sample = map(f32, (x_prompt, x_sample, mem_prompt, mem_sample))
    if "nc" not in _PROG:
        _PROG["nc"], _PROG["tr"] = build_program()
    nc = _PROG["nc"]
    in_maps = _prep_inputs(x_prompt, x_sample, mem_prompt, mem_sample, norm_pre, norm_post, w_in, pool_w, pool_scale,
                           q_norm, k_norm, mem_norm, w_mem_kv, w_out)
    res = run_bass_kernel_spmd(nc, in_maps, core_ids=list(range(NCORES)))
    y_prompt = np.empty_like(x_prompt)
    y_sample = np.empty_like(x_sample)
    for c in range(NCORES):
        y = res.results[c]["y"]
        y_prompt[2 * c] = y[0:T]
        y_prompt[2 * c + 1] = y[T:2 * T]
        y_sample[c // 2, (c % 2) * T:(c % 2 + 1) * T] = y[2 * T:3 * T]
    return (y_prompt, y_sample)


def _prep_inputs(x_prompt, x_sample, mem_prompt, mem_sample, norm_pre, norm_post, w_in, pool_w, pool_scale,
                 q_norm, k_norm, mem_norm, w_mem_kv, w_out, ncores=NCORES):
    f32 = lambda a: np.ascontiguousarray(np.asarray(a, dtype=np.float32))
    cm = _consts()
    shared = {
        "w_in": f32(w_in), "w_out": f32(w_out), "w_mem": f32(w_mem_kv),
        "pool_w": f32(pool_w).reshape(2, 256, 64), "norm_pre": f32(norm_pre), "norm_post": f32(norm_post),
        "mem_norm": f32(mem_norm), "pool_scale": f32(pool_scale), "q_norm": f32(q_norm), "k_norm": f32(k_norm),
        "cmat": cm, "selm": _selm(),
    }
    cp, sp_ = _rope_tables(np.arange(T))
    tab_p = _pool_tab(list(range(HALO)) + list(range(T - HALO, T)), T)
    in_maps = []
    for c in range(ncores):
        sq, half = c // 2, c % 2
        t0 = half * T
        xs = np.concatenate([x_prompt[2 * c], x_prompt[2 * c + 1], x_sample[sq, t0:t0 + T]], 0)
        mm = np.concatenate([mem_prompt[2 * c], mem_prompt[2 * c + 1], mem_sample[sq]], 0)
        cs_, ss_ = _rope_tables(np.arange(t0, t0 + T))
        rope = np.stack([np.stack([cp, sp_], 0), np.stack([cs_, ss_], 0)], 0)
        tab_s = _pool_tab(list(range(t0, t0 + HALO)) + list(range(t0 + T - HALO, t0 + T)), 2 * T)
        ptab = np.stack([tab_p.reshape(128, -1), tab_s.reshape(128, -1)], 0)
        hm = np.zeros((128, 2), np.float32)
        hm[:, 0] = 1.0 if half == 1 else 0.0
        hm[:, 1] = 1.0 if half == 0 else 0.0
        m = dict(shared)
        m.update({"xs": np.ascontiguousarray(xs), "mems": np.ascontiguousarray(mm),
                  "rope": np.ascontiguousarray(rope.astype(np.float32)), "ptab": np.ascontiguousarray(ptab),
                  "hmask": hm})
        in_maps.append(m)
    return in_maps
```
